# Optimizing a Trainium2 kernel written in Bass

```python
import jax, jax.numpy as jnp
from jax import lax
import numpy as np

D_MODEL = 2048
BATCH = 4
SEQ = 8192
DEPTH = 1
DEC_BATCH = 16
DEC_SEQ = 64
PAST_LEN = 4096

CHUNK = 64
Q_BLOCK = 128
RET_HEADS = 8
RET_DK = 64
RET_DV = 128
RET_WIDTH = RET_HEADS * RET_DV
MLA_HEADS = 8
MLA_NOPE = 128
MLA_ROPE = 64
MLA_DV = 128
MLA_WIDTH = MLA_HEADS * MLA_DV
Q_LORA = 512
KV_LORA = 512
MLA_SCALE = (MLA_NOPE + MLA_ROPE) ** -0.5
MIX_WIDTH = RET_WIDTH + MLA_WIDTH
IN_SIZES = (RET_HEADS * RET_DK, RET_HEADS * RET_DK, RET_WIDTH, RET_WIDTH, Q_LORA, KV_LORA, MLA_ROPE)
IN_WIDTH = sum(IN_SIZES)
IN_OFFSETS = tuple(int(o) for o in np.cumsum(IN_SIZES)[:-1])
D_FF = -(-8 * D_MODEL // (3 * 256)) * 256
ALPHA = (2 * DEPTH) ** 0.25
BETA = (8 * DEPTH) ** -0.25
ROPE_BASE = 10000.0
EPS = 1e-5

kernel_name = 'hymba_retention_mla_deepnorm_adaln_stream_step'


def _layernorm(x, g, b):
    xf = x.astype(jnp.float32)
    mu = jnp.mean(xf, axis=-1, keepdims=True)
    var = jnp.mean(jnp.square(xf - mu), axis=-1, keepdims=True)
    return ((xf - mu) * lax.rsqrt(var + EPS)).astype(x.dtype) * g + b


def _rmsnorm(x, g):
    xf = x.astype(jnp.float32)
    return (xf * lax.rsqrt(jnp.mean(xf * xf, axis=-1, keepdims=True) + EPS)).astype(x.dtype) * g


def _group_norm(o, g, b):
    of = o.astype(jnp.float32)
    mu = jnp.mean(of, axis=-1, keepdims=True)
    var = jnp.mean(jnp.square(of - mu), axis=-1, keepdims=True)
    y = ((of - mu) * lax.rsqrt(var + EPS)).astype(o.dtype)
    return y.reshape(o.shape[:2] + (-1,)) * g + b


def _rope(x, pos):
    half = x.shape[-1] // 2
    inv = ROPE_BASE ** (-jnp.arange(half, dtype=jnp.float32) / half)
    ang = pos.astype(jnp.float32)[:, None] * inv[None, :]
    cos = jnp.cos(ang)[None, :, None, :].astype(x.dtype)
    sin = jnp.sin(ang)[None, :, None, :].astype(x.dtype)
    x1, x2 = x[..., :half], x[..., half:]
    return jnp.concatenate([x1 * cos - x2 * sin, x2 * cos + x1 * sin], axis=-1)


def _ret_log_decay():
    h = jnp.arange(RET_HEADS, dtype=jnp.float32)
    return jnp.log1p(-jnp.exp2(-5.0 - h))


def _retention(q, k, v, s0):
    B, T, H, dk = q.shape
    dv = v.shape[-1]
    C = min(CHUNK, T)
    NC = T // C
    logg = _ret_log_decay()
    i = jnp.arange(C, dtype=jnp.float32)
    diff = i[:, None] - i[None, :]
    dmask = jnp.where(diff[None] >= 0.0,
                      jnp.exp(jnp.maximum(diff, 0.0)[None] * logg[:, None, None]), 0.0).astype(q.dtype)
    q_dec = jnp.exp((i + 1.0)[:, None] * logg[None, :]).astype(q.dtype)
    k_dec = jnp.exp((C - 1.0 - i)[:, None] * logg[None, :]).astype(q.dtype)
    c_dec = jnp.exp(C * logg).astype(q.dtype)
    qc = q.reshape(B, NC, C, H, dk)
    kc = k.reshape(B, NC, C, H, dk)
    vc = v.reshape(B, NC, C, H, dv)
    scores = jnp.einsum('bnihd,bnjhd->bnhij', qc, kc) * dmask
    intra = jnp.einsum('bnhij,bnjhe->bnihe', scores, vc)
    kv = jnp.einsum('bnjhd,bnjhe->bnhde', kc * k_dec[:, :, None], vc)

    def step(s, kv_n):
        return s * c_dec[:, None, None] + kv_n, s

    s_last, s_in = lax.scan(step, s0.astype(kv.dtype), jnp.moveaxis(kv, 1, 0))
    s_in = jnp.moveaxis(s_in, 0, 1)
    inter = jnp.einsum('bnihd,bnhde->bnihe', qc * q_dec[:, :, None], s_in)
    return (intra + inter).reshape(B, T, H, dv), s_last


def _attend(qn, qr, q_chunk, kn, kr, v, k_chunk):
    s = jnp.einsum('bqhd,bkhd->bhqk', qn, kn) + jnp.einsum('bqhd,bkd->bhqk', qr, kr)
    s = s.astype(jnp.float32) * MLA_SCALE
    allowed = (k_chunk[None, :] <= q_chunk[:, None])[None, None]
    s = jnp.where(allowed, s, jnp.finfo(jnp.float32).min)
    p = jax.nn.softmax(s, axis=-1).astype(v.dtype)
    return jnp.einsum('bhqk,bkhd->bqhd', p, v)


def _mixer(h, pos, ckv_past, kr_past, ret_s0, w_in, g_cq, g_ckv, w_uq, w_uk, w_uv, g_ret, b_ret, w_out):
    B, T, _ = h.shape
    rq, rk, rv, rg, cq, ckv, kr = jnp.split(h @ w_in, IN_OFFSETS, axis=-1)
    rq = _rope(rq.reshape(B, T, RET_HEADS, RET_DK), pos) * (RET_DK ** -0.5)
    rk = _rope(rk.reshape(B, T, RET_HEADS, RET_DK), pos)
    rv = rv.reshape(B, T, RET_HEADS, RET_DV)
    o_ret, ret_s1 = _retention(rq, rk, rv, ret_s0)
    o_ret = jax.nn.silu(rg) * _group_norm(o_ret, g_ret, b_ret)
    q = jnp.einsum('btr,rhd->bthd', _rmsnorm(cq, g_cq), w_uq)
    qn = q[..., :MLA_NOPE]
    qr = _rope(q[..., MLA_NOPE:], pos)
    ckv = _rmsnorm(ckv, g_ckv)
    kr = _rope(kr[:, :, None, :], pos)[:, :, 0, :]
    q_chunk = pos // CHUNK
    if ckv_past is None:
        ckv_all, kr_all, k_chunk = ckv, kr, q_chunk
    else:
        past = ckv_past.shape[1]
        ckv_all = jnp.concatenate([ckv_past, ckv], axis=1)
        kr_all = jnp.concatenate([kr_past, kr], axis=1)
        k_chunk = jnp.concatenate([jnp.arange(past, dtype=pos.dtype), pos]) // CHUNK
    kn = jnp.einsum('bsr,rhd->bshd', ckv_all, w_uk)
    v = jnp.einsum('bsr,rhd->bshd', ckv_all, w_uv)
    if T > Q_BLOCK and T % Q_BLOCK == 0:
        nb = T // Q_BLOCK
        blocks = lambda a: jnp.moveaxis(a.reshape((B, nb, Q_BLOCK) + a.shape[2:]), 1, 0)
        o = lax.map(lambda a: _attend(a[0], a[1], a[2], kn, kr_all, v, k_chunk),
                    (blocks(qn), blocks(qr), q_chunk.reshape(nb, Q_BLOCK)))
        o_mla = jnp.moveaxis(o, 0, 1).reshape(B, T, MLA_WIDTH)
    else:
        o_mla = _attend(qn, qr, q_chunk, kn, kr_all, v, k_chunk).reshape(B, T, MLA_WIDTH)
    y = jnp.concatenate([o_ret, o_mla], axis=-1) @ w_out
    return y, ckv, kr, ret_s1


def _layer(x, c, pos, ckv_past, kr_past, ret_s0, w_ada, b_ada, w_in, g_cq, g_ckv, w_uq, w_uk, w_uv,
           g_ret, b_ret, w_out, ln1_g, ln1_b, w_gate, w_up, w_down, ln2_g, ln2_b):
    mod = (jax.nn.silu(c) @ w_ada + b_ada)[:, None, :]
    sh1, sc1, g1, sh2, sc2, g2 = jnp.split(mod, 6, axis=-1)
    h = x * (1.0 + sc1) + sh1
    m, ckv, kr, ret_s1 = _mixer(h, pos, ckv_past, kr_past, ret_s0, w_in, g_cq, g_ckv, w_uq, w_uk, w_uv,
                                g_ret, b_ret, w_out)
    x = _layernorm(ALPHA * x + g1 * m, ln1_g, ln1_b)
    h = x * (1.0 + sc2) + sh2
    f = (jax.nn.silu(h @ w_gate) * (h @ w_up)) @ w_down
    x = _layernorm(ALPHA * x + g2 * f, ln2_g, ln2_b)
    return x, ckv, kr, ret_s1


def setup_inputs(seed: int = 0) -> dict:
    key = jax.random.key(seed)
    ks = jax.random.split(key, 32)
    f32 = jnp.float32
    L = DEPTH

    def nrm(k, shape, scale):
        return jax.random.normal(k, shape, f32) * scale

    def gain(k, shape):
        return 1.0 + 0.01 * jax.random.normal(k, shape, f32)

    n_qk = 2 * RET_HEADS * RET_DK
    col_scale = jnp.concatenate([jnp.ones((n_qk,), f32), jnp.full((RET_WIDTH,), BETA, f32),
                                 jnp.ones((IN_WIDTH - n_qk - RET_WIDTH,), f32)])
    return {
        'x_prompt': nrm(ks[0], (BATCH, SEQ, D_MODEL), 1.0),
        'x_sample': nrm(ks[1], (DEC_BATCH, DEC_SEQ, D_MODEL), 1.0),
        'c_prompt': nrm(ks[2], (BATCH, D_MODEL), 1.0),
        'c_sample': nrm(ks[3], (DEC_BATCH, D_MODEL), 1.0),
        'cache_mla_ckv': nrm(ks[4], (L, DEC_BATCH, PAST_LEN, KV_LORA), 1.0),
        'cache_mla_krope': nrm(ks[5], (L, DEC_BATCH, PAST_LEN, MLA_ROPE), 1.0),
        'state_ret': nrm(ks[6], (L, DEC_BATCH, RET_HEADS, RET_DK, RET_DV), 1.0),
        'w_ada': nrm(ks[7], (L, D_MODEL, 6 * D_MODEL), 0.5 * D_MODEL ** -0.5),
        'b_ada': nrm(ks[8], (L, 6 * D_MODEL), 0.01),
        'w_in': nrm(ks[9], (L, D_MODEL, IN_WIDTH), D_MODEL ** -0.5) * col_scale,
        'g_cq': gain(ks[10], (L, Q_LORA)),
        'g_ckv': gain(ks[11], (L, KV_LORA)),
        'w_uq': nrm(ks[12], (L, Q_LORA, MLA_HEADS, MLA_NOPE + MLA_ROPE), Q_LORA ** -0.5),
        'w_uk': nrm(ks[13], (L, KV_LORA, MLA_HEADS, MLA_NOPE), KV_LORA ** -0.5),
        'w_uv': nrm(ks[14], (L, KV_LORA, MLA_HEADS, MLA_DV), BETA * KV_LORA ** -0.5),
        'g_ret': gain(ks[15], (L, RET_WIDTH)),
        'b_ret': nrm(ks[16], (L, RET_WIDTH), 0.01),
        'w_out': nrm(ks[17], (L, MIX_WIDTH, D_MODEL), BETA * MIX_WIDTH ** -0.5),
        'ln1_g': gain(ks[18], (L, D_MODEL)),
        'ln1_b': nrm(ks[19], (L, D_MODEL), 0.01),
        'w_gate': nrm(ks[20], (L, D_MODEL, D_FF), BETA * D_MODEL ** -0.5),
        'w_up': nrm(ks[21], (L, D_MODEL, D_FF), BETA * D_MODEL ** -0.5),
        'w_down': nrm(ks[22], (L, D_FF, D_MODEL), BETA * D_FF ** -0.5),
        'ln2_g': gain(ks[23], (L, D_MODEL)),
        'ln2_b': nrm(ks[24], (L, D_MODEL), 0.01),
    }


def reference(x_prompt, x_sample, c_prompt, c_sample, cache_mla_ckv, cache_mla_krope, state_ret,
              w_ada, b_ada, w_in, g_cq, g_ckv, w_uq, w_uk, w_uv, g_ret, b_ret, w_out,
              ln1_g, ln1_b, w_gate, w_up, w_down, ln2_g, ln2_b):
    t_p = x_prompt.shape[1]
    t_s = x_sample.shape[1]
    past = cache_mla_ckv.shape[2]
    pos_p = jnp.arange(t_p, dtype=jnp.int32)
    pos_s = past + jnp.arange(t_s, dtype=jnp.int32)
    yp, ys = x_prompt, x_sample
    ckv_p_l, kr_p_l, rs_p_l, ckv_s_l, kr_s_l, rs_s_l = [], [], [], [], [], []
    for l in range(DEPTH):
        wts = (w_ada[l], b_ada[l], w_in[l], g_cq[l], g_ckv[l], w_uq[l], w_uk[l], w_uv[l],
               g_ret[l], b_ret[l], w_out[l], ln1_g[l], ln1_b[l], w_gate[l], w_up[l], w_down[l],
               ln2_g[l], ln2_b[l])
        s0 = jnp.zeros((yp.shape[0], RET_HEADS, RET_DK, RET_DV), yp.dtype)
        yp, ckv_p, kr_p, rs_p = _layer(yp, c_prompt, pos_p, None, None, s0, *wts)
        ys, ckv_s, kr_s, rs_s = _layer(ys, c_sample, pos_s, cache_mla_ckv[l], cache_mla_krope[l],
                                       state_ret[l], *wts)
        ckv_p_l.append(ckv_p)
        kr_p_l.append(kr_p)
        rs_p_l.append(rs_p)
        ckv_s_l.append(ckv_s)
        kr_s_l.append(kr_s)
        rs_s_l.append(rs_s)
    new_ckv_prompt = jnp.stack(ckv_p_l, axis=0)
    new_krope_prompt = jnp.stack(kr_p_l, axis=0)
    ret_state_prompt = jnp.stack(rs_p_l, axis=0)
    new_ckv_sample = jnp.stack(ckv_s_l, axis=0)
    new_krope_sample = jnp.stack(kr_s_l, axis=0)
    ret_state_sample = jnp.stack(rs_s_l, axis=0)
    return (yp, ys, new_ckv_prompt, new_krope_prompt, ret_state_prompt, new_ckv_sample, new_krope_sample, ret_state_sample)
```

```python
import numpy as np
import ml_dtypes
import concourse.bass as bass
import concourse.mybir as mybir
from concourse.bass_utils import run_bass_kernel_spmd
from contextlib import ExitStack

F32 = mybir.dt.float32
BF16 = mybir.dt.bfloat16
ALU = mybir.AluOpType
AF = mybir.ActivationFunctionType

PE, ACT, DVE, POOL, SP = "pe", "act", "dve", "pool", "sp"


def C(name, *a, **k):
    return (name, a, k)

D = 2048
NB = 4
SEQ = 8192
NSB = 16
SSEQ = 64
PAST = 4096
H = 8
RDK = 64
RDV = 128
NOPE = 128
ROPE = 64
QL = 512
KVL = 512
DFF = 5632
ALPHA = 2.0 ** 0.25
EPS = 1e-5
MLA_SCALE = 192.0 ** -0.5
HALF = 4096
TBP = 512
NBLK = HALF // TBP
NEG = -30000.0


class Buf:
    __slots__ = ("name", "w", "r", "aliases", "dsem", "excl")

    def __init__(self, name, dsem=None):
        self.name = name
        self.w = None
        self.r = {}
        self.aliases = []
        self.dsem = dsem
        self.excl = False


class DmaSem:
    __slots__ = ("sem", "cnt", "key")

    def __init__(self, sem, key):
        self.sem = sem
        self.cnt = 0
        self.key = key


class Sched:
    def __init__(self, nc, stack, same_engine_sync=True):
        self.nc = nc
        self.stack = stack
        self.same_engine_sync = same_engine_sync
        self.sems = {}
        self.prog = {}
        self.cnt = {}
        self.seen = {}
        self.nsem = 0
        for k in (PE, ACT, DVE, POOL, SP):
            self.prog[k] = []
            self.cnt[k] = 0
            self.seen[k] = {}
            if k != SP:
                self.sems[k] = stack.enter_context(nc.semaphore("prog_" + k))
                self.nsem += 1
        self.n_wait = 0
        self.n_ops = 0
        self.dsems = []
        self.tag = ""
        self.pfx = ""
        self.pe_tags = []

    def dma_sem(self, name):
        key = "d_" + name
        self.sems[key] = self.stack.enter_context(self.nc.semaphore(key))
        self.nsem += 1
        ds = DmaSem(self.sems[key], key)
        self.dsems.append(ds)
        return ds

    def buf(self, name, dma=False):
        return Buf(name, self.dma_sem(name) if dma else None)

    def _collect(self, reads, writes, ek=None):
        deps = {}

        def add(ev):
            if ev is None:
                return
            k, v = ev
            if deps.get(k, 0) < v:
                deps[k] = v

        for b in reads:
            add(b.w)
            if b.excl:
                for k, v in b.r.items():
                    if k != ek:
                        add((k, v))
            for a in b.aliases:
                add(a.w)
        for b in writes:
            add(b.w)
            for k, v in b.r.items():
                add((k, v))
            for a in b.aliases:
                add(a.w)
                for k, v in a.r.items():
                    add((k, v))
        return deps

    def _waits(self, ek, deps, self_sync=False):
        seen = self.seen[ek]
        waits = []
        for k, v in deps.items():
            if k == ek and (ek == PE or not self.same_engine_sync) and not self_sync:
                continue
            if seen.get(k, 0) < v:
                seen[k] = v
                waits.append((k, v))
        self.n_wait += len(waits)
        return waits

    def _update(self, ev, reads, writes):
        k, v = ev
        for b in reads:
            if b.r.get(k, 0) < v:
                b.r[k] = v
        for b in writes:
            b.w = ev
            b.r = {}

    def op(self, ek, fn, reads=(), writes=(), signal=True, self_sync=False):
        if ek == PE:
            self.pe_tags.append(self.pfx + self.tag)
        deps = self._collect(reads, writes, ek)
        waits = self._waits(ek, deps, self_sync)
        if signal:
            self.cnt[ek] += 1
            ev = (ek, self.cnt[ek])
            self.prog[ek].append((waits, fn, (ek, 1)))
        else:
            ev = (ek, self.cnt[ek] + 1)
            self.prog[ek].append((waits, fn, None))
        self._update(ev, reads, writes)
        self.n_ops += 1
        return ev

    def dma(self, qk, fn, sem, reads=(), writes=(), skip_own=False):
        deps = self._collect(reads, writes)
        if skip_own:
            deps.pop(sem.key, None)
        waits = self._waits(qk, deps)
        sem.cnt += 16
        ev = (sem.key, sem.cnt)
        self.prog[qk].append((waits, fn, (sem.key, 16)))
        self._update(ev, reads, writes)
        self.n_ops += 1
        return ev

    def final_wait(self, ek, bufs):
        deps = self._collect((), bufs)
        for ds in self.dsems:
            if ds.cnt > 0:
                deps[ds.key] = max(deps.get(ds.key, 0), ds.cnt)
        waits = self._waits(ek, deps)
        self.prog[ek].append((waits, None, None))

    def replay(self):
        nc = self.nc
        sems = self.sems
        prog = self.prog

        def run(eng, items):
            for waits, fn, inc in items:
                for k, v in waits:
                    eng.wait_ge(sems[k], v)
                if fn is None:
                    continue
                ins = getattr(eng, fn[0])(*fn[1], **fn[2])
                if inc is not None:
                    ins.then_inc(sems[inc[0]], inc[1])

        with nc.Block() as block:
            @block.tensor
            def _(e):
                run(e, prog[PE])

            @block.scalar
            def _(e):
                run(e, prog[ACT])

            @block.vector
            def _(e):
                run(e, prog[DVE])

            @block.gpsimd
            def _(e):
                run(e, prog[POOL])

            @block.sync
            def _(e):
                run(e, prog[SP])


class Stream:
    def __init__(self, S, tiles, name, slack=0):
        self.S = S
        self.slack = slack
        self.tiles = tiles
        self.bufs = [S.buf(f"{name}{i}", dma=True) for i in range(len(tiles))]
        self.sw_sems = [S.dma_sem(f"{name}{i}_sw") for i in range(len(tiles))]
        self.plan = []
        self.issued = 0
        self.taken = 0

    def extend(self, loads):
        self.plan.extend(loads)

    def next(self):
        R = len(self.tiles)
        while self.issued < len(self.plan) and self.issued < max(self.taken + 1, self.taken + R - self.slack):
            i = self.issued
            s = i % R
            self.plan[i](self.tiles[s], self.bufs[s], self.sw_sems[s])
            self.issued += 1
        s = self.taken % R
        assert self.taken < self.issued
        self.taken += 1
        return self.tiles[s], self.bufs[s]


def _tm_chunk(w, cols):
    sub = w[:, cols]
    kt = sub.shape[0] // 128
    return np.ascontiguousarray(sub.reshape(kt, 128, sub.shape[1]).transpose(1, 0, 2)).reshape(128, -1)


def _pad(a, L):
    if a.shape[1] == L:
        return a
    out = np.zeros((a.shape[0], L), a.dtype)
    out[:, :a.shape[1]] = a
    return out


def _fm(v):
    return np.ascontiguousarray(v.reshape(-1, 128).T)


def _decay_tables():
    h = np.arange(H, dtype=np.float64)
    logg = np.log1p(-np.exp2(-5.0 - h))
    out = {}
    for C in (128, 64):
        i = np.arange(C, dtype=np.float64)
        diff = i[None, :] - i[:, None]
        dt = np.where(diff[:, None, :] >= 0, np.exp(np.maximum(diff, 0)[:, None, :] * logg[None, :, None]), 0.0)
        DT = np.zeros((128, H, C), np.float32)
        DT[:C] = dt
        qd = np.zeros((128, 4, C), np.float32)
        for m in range(4):
            for hh in range(2):
                qd[hh * 64:(hh + 1) * 64, m, :] = (RDK ** -0.5) * np.exp((i + 1.0) * logg[2 * m + hh])[None, :]
        kd = np.zeros((128, H), np.float32)
        kd[:C] = np.exp((C - 1.0 - i)[:, None] * logg[None, :])
        cd = np.zeros((128, 4), np.float32)
        for m in range(4):
            for hh in range(2):
                cd[hh * 64:(hh + 1) * 64, m] = np.exp(C * logg[2 * m + hh])
        out[C] = (DT, qd, kd, cd)
    return out


def _rope_table(pos):
    half = 32
    inv = (np.float32(10000.0) ** (-np.arange(half, dtype=np.float32) / np.float32(half))).astype(np.float32)
    ang = (pos.astype(np.float32)[:, None] * inv[None, :]).astype(np.float32)
    return np.concatenate([np.cos(ang.astype(np.float64)), np.sin(ang.astype(np.float64))], axis=1).astype(np.float32)


T_LN1G, T_LN1B, T_LN2G, T_LN2B = 0, 16, 32, 48
T_GRET, T_BRET, T_GCQ = 64, 72, 80
T_BADA = 84
T_VF = 180
T_KD128, T_KD64 = 182, 190
T_CD128, T_CD64 = 198, 202
T_EPS, T_ZERO = 206, 207
NTAB = 208


def build_program(stage=99, sub=99):
    nc = bass.Bass("TRN2", target_bir_lowering=False)

    def din(name, shape, dt=F32):
        return nc.dram_tensor(name, list(shape), dt, kind="ExternalInput").ap()

    def dout(name, shape, dt=F32):
        return nc.dram_tensor(name, list(shape), dt, kind="ExternalOutput").ap()

    def dscr(name, shape, dt=BF16):
        return nc.dram_tensor(name, list(shape), dt, kind="Internal").ap()

    xpre = din("xpre", [HALF, D])
    xown = din("xown", [HALF, D])
    xsmp = din("xsmp", [2, SSEQ, D])
    cT_d = din("cT", [128, 16, 3])
    tab_d = din("tab", [128, NTAB])
    cs_pre = din("cs_pre", [HALF, 64])
    cs_own = din("cs_own", [HALF, 64])
    cs_smp = din("cs_smp", [SSEQ, 64])
    ckv_c = din("ckv_c", [2, PAST, KVL])
    kr_c = din("kr_c", [2, PAST, ROPE])
    st_c = din("st_c", [2, 128, 4, 128])
    ident_d = din("ident", [128, 128])
    dt128_d = din("dt128", [128, H, 128], BF16)
    qd128_d = din("qd128", [128, 4, 128])
    gckv_d = din("gckv", [128, KVL])
    wada_d = din("wada", [24, 128, 8192])
    wtm_d = din("wtm", [7, 128, 8192])
    wrg_d = din("wrg", [2, 128, 8192])
    wmla_d = din("wmla", [4, 128, 4096])
    wout_d = din("wout", [4, 128, 8192])
    wgu_d = din("wgu", [22, 128, 8192])
    wd_d = din("wd", [16, 128, 5632])

    y_own = dout("y_own", [HALF, D])
    y_smp = dout("y_smp", [2, SSEQ, D])
    ckv_own = dout("ckv_own", [HALF, KVL])
    kr_own = dout("kr_own", [HALF, ROPE])
    st_own = dout("st_own", [128, 4, 128])
    ckv_so = dout("ckv_so", [2, SSEQ, KVL])
    kr_so = dout("kr_so", [2, SSEQ, ROPE])
    st_so = dout("st_so", [2, 128, 4, 128])

    wtm_s = dscr("wtm_s", [7, 128, 8192])
    wrg_s = dscr("wrg_s", [2, 128, 8192])
    wmla_s = dscr("wmla_s", [4, 128, 4096])
    wout_s = dscr("wout_s", [4, 128, 8192])
    wgu_s = dscr("wgu_s", [22, 128, 8192])
    wd_s = dscr("wd_s", [16, 128, 5632])
    NKB = 17
    kscr = dscr("kscr", [NKB, 128, H, TBP])
    vscr = dscr("vscr", [NKB, 128, H, 4, 128])

    with ExitStack() as st:
        S = Sched(nc, st)
        nalloc = [0]

        def T(name, shape, dt):
            return st.enter_context(nc.sbuf_tensor("s_" + name, list(shape), dt))

        x_stage = [T(f"x_stage{i}", [128, D], F32) for i in range(2)]
        b_xs = [S.buf(f"x_stage{i}", dma=True) for i in range(2)]

        xaT = T("xaT", [128, 16, TBP], F32); b_xaT = [S.buf(f"xaT{i}") for i in range(16)]
        actb = T("actb", [128, 16, TBP], BF16); b_hT = [S.buf(f"hT{i}") for i in range(16)]
        U = T("U", [128, 22528], BF16)
        uo = [0]

        def carve(n_elems, shape):
            a = U[:, uo[0]:uo[0] + n_elems]
            uo[0] += n_elems
            return a

        kdec_tok = U[:, 0:2048].rearrange("p (t c) -> p t c", t=4)
        v_tok = U[:, 2048:6144].rearrange("p (t c) -> p t c", t=4)
        qT_ret = U[:, 6144:8192].rearrange("p (m c) -> p m c", m=4)
        kT_ret = U[:, 8192:10240].rearrange("p (m c) -> p m c", m=4)
        qdT_ret = U[:, 10240:12288].rearrange("p (m c) -> p m c", m=4)
        rgT = U[:, 12288:16384].rearrange("p (m c) -> p m c", m=8)
        cqnT = U[:, 16384:18432].rearrange("p (m c) -> p m c", m=4)
        ckvT = U[:, 18432:20480].rearrange("p (m c) -> p m c", m=4)
        knT_blk = U[:, 0:4096].rearrange("p (h c) -> p h c", h=8)
        v_blk = U[:, 4096:8192].rearrange("p (h t e) -> p h t e", h=8, t=4)
        qnT = U[:, 0:4096].rearrange("p (h c) -> p h c", h=8)
        qrT = U[:, 4096:8192].rearrange("p (h c) -> p h c", h=8)
        actT = U[:, 0:44 * 512].rearrange("p (m c) -> p m c", m=44)
        b_U = S.buf("U")
        b_kdec = S.buf("kdec_tok"); b_vtok = S.buf("v_tok"); b_qT = S.buf("qT_ret"); b_kT = S.buf("kT_ret")
        b_qdT = S.buf("qdT_ret"); b_rg = S.buf("rgT"); b_cqn = S.buf("cqnT"); b_ckvT = S.buf("ckvT")
        b_knb = S.buf("knT_blk", dma=True); b_vb = S.buf("v_blk", dma=True)
        b_qn = S.buf("qnT"); b_qr = S.buf("qrT"); b_act = [S.buf(f"actT{i}") for i in range(44)]; b_actw = S.buf("actT_all")
        retb = [b_kdec, b_vtok, b_qT, b_kT, b_qdT]
        ag = [[b_knb, b_vb], [b_kdec, b_vtok, b_qT], [b_qn, b_qr]]
        for gi_, g_ in enumerate(ag):
            for x_ in g_:
                for gj_, h_ in enumerate(ag):
                    if gi_ != gj_:
                        x_.aliases.extend(h_)
        allA = retb + [b_rg, b_cqn, b_ckvT, b_knb, b_vb, b_qn, b_qr]
        b_actw.aliases = list(allA)
        for a_ in allA:
            a_.aliases.append(b_actw)

        krT_all = T("krT_all", [128, 8192 + 128], BF16); b_krT = S.buf("krT_all")
        WR = 2
        w_ring = [T(f"w_ring{i}", [128, 8192], BF16) for i in range(WR)]
        WS = Stream(S, w_ring, "w_ring")
        KR = 3
        kv_ring = [T(f"kv_ring{i}", [128, 1024], BF16) for i in range(KR)]
        KS = Stream(S, kv_ring, "kv_ring", slack=1)
        xa_bf = xaT[:, :, :].rearrange("p k t -> p (k t)").bitcast(BF16)
        ada_ring = [xa_bf[:, 0:8192], xa_bf[:, 8192:16384], U[:, 8192:16384]]
        AS = Stream(S, ada_ring, "ada_ring")
        for q_ in b_xaT:
            q_.aliases.extend(AS.bufs[0:2])
        for q_ in (b_kT, b_qdT, b_rg):
            q_.aliases.append(AS.bufs[2])
        pT = [T(f"pT{i}", [128, 2, 512], BF16) for i in range(2)]
        b_pT = [S.buf(f"pT{i}") for i in range(2)]
        NTMP = 3
        tmpf = [T(f"tmpf{i}", [128, 512], F32) for i in range(NTMP)]
        b_tmpf = [S.buf(f"tmpf{i}") for i in range(NTMP)]
        NTB = 4
        tmpb = [T(f"tmpb{i}", [128, 512], BF16) for i in range(NTB)]
        b_tmpb = [S.buf(f"tmpb{i}") for i in range(NTB)]
        mean_sb = T("mean_sb", [128, 512], F32); b_mean = S.buf("mean_sb")
        rstd_sb = T("rstd_sb", [128, 512], F32); b_rstd = S.buf("rstd_sb")
        ckv_stg = [T(f"ckv_stg{i}", [128, 512], F32) for i in range(1)]
        b_ckv_stg = [S.buf(f"ckv_stg{i}", dma=True) for i in range(1)]
        kr_stg = [T(f"kr_stg{i}", [128, 64], F32) for i in range(1)]
        b_kr_stg = [S.buf(f"kr_stg{i}", dma=True) for i in range(1)]
        tok_b = [T(f"tok_b{i}", [128, 512], BF16) for i in range(2)]
        b_tok_b = [S.buf(f"tok_b{i}") for i in range(2)]
        sstat = T("sstat", [128, 8], F32); b_sstat = S.buf("sstat")
        cs_blk = [T(f"cs_blk{i}", [128, 4, 64], F32) for i in range(1)]
        b_csb = [S.buf(f"cs_blk{i}", dma=True) for i in range(1)]
        cur_cs = [0]
        cstage_raw = T("cstage_raw", [128, 2304], BF16)
        cache_stage = [cstage_raw[:, :].rearrange("p (t c) -> p t c", t=4),
                       x_stage[1][:, :].bitcast(BF16)[:, 0:2304].rearrange("p (t c) -> p t c", t=4)]
        b_cstage = [S.buf("cache_stage0", dma=True), b_xs[1]]
        ystg_all = cstage_raw[:, 0:2048].bitcast(F32)
        stg = [ystg_all[:, 0:512], ystg_all[:, 512:1024]]
        b_stg = [S.buf(f"ystg{i}", dma=True) for i in range(2)]
        for q_ in b_stg:
            q_.aliases.append(b_cstage[0])
            b_cstage[0].aliases.append(q_)
        S_f = T("S_f", [128, 4, 128], F32); b_Sf = S.buf("S_f", dma=True)
        S_bf = T("S_bf", [128, 5, 4, 128], BF16); b_Sbf = [S.buf(f"S_bf{i}") for i in range(5)]
        tab = T("tab", [128, NTAB], F32); b_const = S.buf("const", dma=True)
        ident = T("ident", [128, 128], F32)
        identb = T("identb", [128, 128], BF16)
        ones_b = T("ones_b", [128, 128], BF16)
        dt128 = T("dt128", [128, H, 128], BF16)
        qd128 = T("qd128", [128, 4, 128], F32)
        gckv = T("gckv", [128, KVL], F32)
        cT = mean_sb[:, 0:48].rearrange("p (k s) -> p k s", s=3)
        cTb = T("cTb", [128, 16, 3], BF16)
        modT = T("modT", [128, 96, 3], F32); b_mod = S.buf("modT")
        SC1P = T("SC1P", [128, 3, 16], F32)
        G2T = T("G2T", [128, 3, 16], F32)
        B2T = T("B2T", [128, 3, 16], F32)
        AG1 = T("AG1", [128, 16], F32)
        AB1 = T("AB1", [128, 16], F32)
        b_seqtab = S.buf("seqtab")

        def P(name):
            return st.enter_context(nc.psum_tensor(name, [128, 1024], F32))

        DB = [P(f"DB{i}") for i in range(4)]
        b_DB = [[S.buf(f"DB{i}_{j}") for j in range(2)] for i in range(4)]
        for r_ in b_DB:
            for q_ in r_:
                q_.excl = True

        def bank(i, j):
            return DB[i][:, j * 512:(j + 1) * 512]

        mmrot = [0]

        ROT = [(0, 0), (0, 1), (1, 0), (1, 1), (3, 0), (3, 1)]
        rotN = [4]

        def next_bank():
            r = mmrot[0] % rotN[0]
            mmrot[0] = (r + 1) % rotN[0]
            return ROT[r]

        tmpi = {}

        def rot(idx, n):
            v = tmpi.get((idx, n), 0)
            tmpi[(idx, n)] = (v + 1) % n
            return v

        b_wscr = {}
        b_kscr = [S.buf(f"kscr{i}") for i in range(NKB)]
        b_vscr = [S.buf(f"vscr{i}") for i in range(NKB)]
        b_out = S.buf("outputs")
        d2d_sems = [S.dma_sem("d2dA"), S.dma_sem("d2dB"), S.dma_sem("d2dC")]

        def cload(dst, src):
            S.dma(SP, C("dma_start", out=dst, in_=src), b_const.dsem, writes=[b_const])

        cload(tab[:], tab_d)
        S.dma(SP, C("dma_start", out=cT, in_=cT_d), b_const.dsem, writes=[b_const, b_mean])
        cload(ident[:], ident_d)
        cload(dt128[:], dt128_d)
        cload(qd128[:], qd128_d)
        cload(gckv[:], gckv_d)
        b_c2 = S.buf("const2")
        S.op(DVE, C("tensor_copy", out=identb[:], in_=ident[:]), reads=[b_const], writes=[b_c2])
        S.op(DVE, C("memset", ones_b[:], 1.0), writes=[b_c2])
        S.op(ACT, C("activation", out=cTb[:], in_=cT, func=AF.Silu), reads=[b_const, b_mean], writes=[b_c2])
        CONST = [b_const, b_c2]

        def slab_from_scratch(scr, key, i, L):
            def load(tile, buf, swsem):
                S.dma(SP, C("dma_start", out=tile[:, 0:L], in_=scr[i]), buf.dsem,
                      reads=[b_wscr[(key, i)]], writes=[buf])
            return load

        def slab_ada(i):
            def load(tile, buf, swsem):
                S.dma(POOL, C("dma_start", out=tile[:, :], in_=wada_d[i]), swsem, writes=[buf])
            return load

        groups = {"tm": (wtm_d, wtm_s, 7, 8192), "rg": (wrg_d, wrg_s, 2, 8192), "mla": (wmla_d, wmla_s, 4, 4096),
                  "out": (wout_d, wout_s, 4, 8192), "gu": (wgu_d, wgu_s, 22, 8192), "d": (wd_d, wd_s, 16, 5632)}

        conv_groups = [[], [], []]

        def convert(key, idxs, grp):
            src, dst, n, L = groups[key]
            for i in idxs:
                b_wscr[(key, i)] = S.buf(f"wscr_{key}{i}")
                S.dma(POOL, C("dma_start", out=dst[i], in_=src[i]), d2d_sems[grp], writes=[b_wscr[(key, i)]])
                conv_groups[grp].append(b_wscr[(key, i)])

        def seal(grp):
            for b_ in conv_groups[grp]:
                b_.w = (d2d_sems[grp].key, d2d_sems[grp].cnt)

        def L_(key, i):
            return slab_from_scratch(groups[key][1], key, i, groups[key][3])

        PREFIX_SLABS = [("tm", 5), ("tm", 6), ("mla", 0), ("mla", 1), ("tm", 1), ("tm", 2), ("tm", 3)]
        MAIN_SLABS = ([("tm", 4), ("tm", 5), ("tm", 6), ("mla", 0), ("mla", 1), ("tm", 0), ("tm", 1), ("tm", 2), ("tm", 3),
                       ("rg", 0), ("rg", 1), ("mla", 2), ("mla", 3)]
                      + [("out", i) for i in range(4)] + [("gu", i) for i in range(22)] + [("d", i) for i in range(16)])
        convert("tm", [5, 6], 0)
        convert("mla", [0, 1], 0)
        convert("tm", [1, 2, 3], 0)
        seal(0)
        AS.extend([slab_ada(i) for i in range(24)])

        def ada_part(f0, f1):
            S.tag = "ada"
            bk = (3, 1)
            for sl in range(f0 // 4, f1 // 4):
                wt, wb = AS.next()
                for mm in range(4):
                    f = sl * 4 + mm
                    for k in range(16):
                        S.op(PE, C("matmul",
                            bank(*bk)[:, f * 3:(f + 1) * 3], lhsT=wt[:, (mm * 16 + k) * 128:(mm * 16 + k + 1) * 128],
                            rhs=cTb[:, k, :], start=(k == 0), stop=(k == 15)),
                            reads=[wb] + CONST, writes=[b_DB[3][1]], signal=(k == 15))
            S.op(DVE, C("tensor_tensor",
                out=modT[:, f0:f1, :], in0=bank(*bk)[:, f0 * 3:f1 * 3].rearrange("p (f s) -> p f s", s=3),
                in1=tab[:, T_BADA + f0:T_BADA + f1].unsqueeze(2).to_broadcast([128, f1 - f0, 3]), op=ALU.add),
                reads=[b_DB[3][1]] + CONST, writes=[b_mod])

        ada_part(0, 32)
        S.op(DVE, C("tensor_scalar", out=SC1P[:].rearrange("p s f -> p f s"), in0=modT[:, 16:32, :], scalar1=1.0, scalar2=None, op0=ALU.add),
             reads=[b_mod], writes=[b_seqtab])
        conv_pending = {1: [("tm", 4), ("tm", 0), ("rg", 0), ("rg", 1), ("mla", 2), ("mla", 3)] + [("out", i) for i in range(4)],
                        2: [("gu", i) for i in range(22)] + [("d", i) for i in range(16)]}

        def convert_some(n, grp=1):
            lst = conv_pending[grp]
            if not lst:
                return
            for _ in range(n):
                if lst:
                    k_, i_ = lst.pop(0)
                    convert(k_, [i_], grp)
            if not lst:
                seal(grp)

        def mm_group(out_ap, obuf, pairs, reads):
            n = len(pairs)
            for i, (l, r) in enumerate(pairs):
                S.op(PE, C("matmul", out_ap, lhsT=l, rhs=r, start=(i == 0), stop=(i == n - 1)),
                     reads=reads, writes=[obuf], signal=(i == n - 1))

        def rope(src, G, TT, cs, b_cs_, reads, outs):
            s3 = src.rearrange("p (g c) -> p g c", c=64)
            x1 = s3[:, :, 0:32]
            x2 = s3[:, :, 32:64]
            cosb = cs[0:TT, 0:32].unsqueeze(1).to_broadcast([TT, G, 32])
            sinb = cs[0:TT, 32:64].unsqueeze(1).to_broadcast([TT, G, 32])
            ia, ib = rot(0, NTMP), rot(0, NTMP)
            ta = tmpf[ia][0:TT, 0:G * 32].rearrange("p (g c) -> p g c", c=32)
            tb = tmpf[ia][0:TT, 256:256 + G * 32].rearrange("p (g c) -> p g c", c=32)
            tc = tmpf[ib][0:TT, 0:G * 32].rearrange("p (g c) -> p g c", c=32)
            td = tmpf[ib][0:TT, 256:256 + G * 32].rearrange("p (g c) -> p g c", c=32)
            rd = reads + [b_cs_]
            S.op(DVE, C("tensor_tensor", out=ta, in0=x1, in1=cosb, op=ALU.mult), reads=rd, writes=[b_tmpf[ia]])
            S.op(DVE, C("tensor_tensor", out=tb, in0=x2, in1=sinb, op=ALU.mult), reads=rd, writes=[b_tmpf[ia]])
            S.op(DVE, C("tensor_tensor", out=tc, in0=x2, in1=cosb, op=ALU.mult), reads=rd, writes=[b_tmpf[ib]])
            S.op(DVE, C("tensor_tensor", out=td, in0=x1, in1=sinb, op=ALU.mult), reads=rd, writes=[b_tmpf[ib]])
            for half, (pa, pb_, opc, bufx) in enumerate(((ta, tb, ALU.subtract, b_tmpf[ia]), (tc, td, ALU.add, b_tmpf[ib]))):
                for o_ in outs:
                    if len(o_) == 2:
                        dst, dbuf = o_
                        d3 = dst.rearrange("p (g c) -> p g c", c=64)[:, :, half * 32:(half + 1) * 32]
                        S.op(DVE, C("tensor_tensor", out=d3, in0=pa, in1=pb_, op=opc), reads=[bufx], writes=[dbuf])
                    else:
                        dfn, g0, g1, dbuf = o_
                        S.op(DVE, C("tensor_tensor", out=dfn(half), in0=pa[:, g0:g1, :], in1=pb_[:, g0:g1, :], op=opc), reads=[bufx], writes=[dbuf])

        def transposes_bf(src_tile, src_buf, TT, nblk, dst_fn, dst_bufs, eng_fn):
            bi = next_bank()
            pb = bank(*bi).bitcast(BF16)[:, 0:nblk * TT].rearrange("p (n t) -> p n t", t=TT)
            for n in range(nblk):
                S.op(PE, C("transpose", pb[:, n, :], src_tile[0:TT, n * 128:(n + 1) * 128], identb[0:TT, 0:TT]),
                     reads=[src_buf] + CONST, writes=[b_DB[bi[0]][bi[1]]], signal=(n == nblk - 1))
            eng_fn(pb, b_DB[bi[0]][bi[1]])

        def rmsnorm_rstd(ps, pbuf, TT):
            j = rot(1, NTMP)
            c = rot(2, 4)
            S.op(ACT, C("activation", out=tmpf[j][0:TT, :], in_=ps, func=AF.Square),
                 reads=[pbuf], writes=[b_tmpf[j]])
            S.op(DVE, C("reduce_sum", out=sstat[0:TT, 2 * c:2 * c + 1], in_=tmpf[j][0:TT, :], axis=mybir.AxisListType.X),
                 reads=[b_tmpf[j]], writes=[b_sstat])
            S.op(ACT, C("activation", out=sstat[0:TT, 2 * c + 1:2 * c + 2], in_=sstat[0:TT, 2 * c:2 * c + 1], func=AF.Sqrt,
                                             scale=1.0 / 512.0, bias=tab[0:TT, T_EPS:T_EPS + 1]),
                 reads=[b_sstat] + CONST, writes=[b_sstat])
            S.op(DVE, C("reciprocal", out=sstat[0:TT, 2 * c + 1:2 * c + 2], in_=sstat[0:TT, 2 * c + 1:2 * c + 2]),
                 reads=[b_sstat], writes=[b_sstat])
            return sstat[0:TT, 2 * c + 1:2 * c + 2]

        x_pref = [None]

        def cache_load(ci, ckv_src, kr_src):
            S.dma(POOL, C("dma_start", out=cache_stage[ci][:, :, 0:512], in_=ckv_src.rearrange("(t p) c -> p t c", p=128)), b_cstage[ci].dsem,
                  writes=[b_cstage[ci]])
            S.dma(POOL, C("dma_start", out=cache_stage[ci][:, :, 512:576], in_=kr_src.rearrange("(t p) c -> p t c", p=128)), b_cstage[ci].dsem,
                  writes=[b_cstage[ci]], skip_own=True)

        def front(full, TT, NT, seq, x_src, cs_rows, key_col0, kblk, C_tabs, ckv_out=None, kr_out=None,
                  cache=None, blk_id=None, next_x=None):
            TB = TT * NT
            DT, QD, KDc, CDc = C_tabs

            def kvgen():
                S.tag = "kvgen"
                kvgen_impl(TT, NT, kblk)
                S.tag = "front_win"
            S.tag = "front_x"
            rotN[0] = 6
            if cache is None:
                cur_cs[0] = 0
                cb = cur_cs[0]
                S.dma(POOL, C("dma_start", out=cs_blk[cb][0:TT, 0:NT, :], in_=cs_rows.rearrange("(t p) c -> p t c", p=TT)), b_csb[cb].dsem,
                      writes=[b_csb[cb]])
                for t in range(NT):
                    sx = t % 2
                    if not (x_pref[0] is not None and x_pref[0] == blk_id and t < 2):
                        S.dma(POOL, C("dma_start", out=x_stage[sx][0:TT, :], in_=x_src(t)), b_xs[sx].dsem,
                              writes=[b_xs[sx]])
                    for g in range(4):
                        bi = next_bank()
                        pb = bank(*bi)[:, 0:4 * TT].rearrange("p (k t) -> p k t", t=TT)
                        for kk in range(4):
                            k = g * 4 + kk
                            S.op(PE, C("transpose", pb[:, kk, :], x_stage[sx][0:TT, k * 128:(k + 1) * 128], ident[0:TT, 0:TT]),
                                 reads=[b_xs[sx]] + CONST, writes=[b_DB[bi[0]][bi[1]]], signal=(kk == 3))
                        pbuf = b_DB[bi[0]][bi[1]]
                        if full:
                            S.op(ACT, C("activation", out=xaT[:, g * 4:g * 4 + 4, t * TT:(t + 1) * TT], in_=pb, func=AF.Identity, scale=ALPHA),
                                 reads=[pbuf], writes=b_xaT[g * 4:g * 4 + 4])
                        j = rot(1, NTMP)
                        tv = tmpf[j][:, 0:4 * TT].rearrange("p (k t) -> p k t", t=TT)
                        S.op(DVE, C("tensor_tensor", out=tv, in0=pb, in1=SC1P[:, seq, g * 4:g * 4 + 4].unsqueeze(2).to_broadcast([128, 4, TT]), op=ALU.mult),
                             reads=[pbuf, b_seqtab], writes=[b_tmpf[j]])
                        S.op(DVE, C("tensor_tensor", out=actb[:, g * 4:g * 4 + 4, t * TT:(t + 1) * TT], in0=tv,
                                                                              in1=modT[:, g * 4:g * 4 + 4, seq].unsqueeze(2).to_broadcast([128, 4, TT]), op=ALU.add),
                             reads=[b_tmpf[j], b_mod], writes=b_hT[g * 4:g * 4 + 4])
                x_pref[0] = None
                if next_x is not None:
                    nid, nsrc, nTT, nNT = next_x
                    for t in range(min(2, nNT)):
                        S.dma(POOL, C("dma_start", out=x_stage[t][0:nTT, :], in_=nsrc(t)), b_xs[t].dsem, writes=[b_xs[t]])
                    x_pref[0] = nid
                S.tag = "front_win"
                chunks = [4, 5, 6, "kv", 0, 1, 2, 3] if full else [5, 6, "kv", 1, 2, 3]
                if full and sub < 0:
                    chunks = chunks[:-sub - 1]
                for c in chunks:
                    if c == "kv":
                        kvgen()
                        continue
                    wt, wb = WS.next()
                    ncol = 64 if c == 6 else 512
                    w3 = wt[:, 0:16 * ncol].rearrange("p (k n) -> p k n", n=ncol)
                    pend = []
                    for t in range(NT):
                        bi = next_bank()
                        ps = bank(*bi)[0:TT, 0:ncol]
                        pbuf = b_DB[bi[0]][bi[1]]
                        mm_group(ps, pbuf, [(actb[:, k, t * TT:(t + 1) * TT], w3[:, k, :]) for k in range(16)], b_hT + [wb])
                        def post(t=t, ps=ps, pbuf=pbuf, c=c):
                            cst = cs_blk[cb][:, t, :]
                            if c == 0:
                                j = rot(3, 2)
                                rope(ps, 8, TT, cst, b_csb[cb], [pbuf], [(tok_b[j][0:TT, :], b_tok_b[j])])

                                def ev(pb, pbb, t=t):
                                    S.op(ACT, C("activation", out=qT_ret[:, :, t * TT:(t + 1) * TT], in_=pb, func=AF.Identity, scale=RDK ** -0.5),
                                         reads=[pbb], writes=[b_qT])
                                    S.op(DVE, C("tensor_tensor", out=qdT_ret[:, :, t * TT:(t + 1) * TT], in0=pb, in1=QD[:, :, 0:TT], op=ALU.mult),
                                         reads=[pbb] + CONST, writes=[b_qdT])
                                transposes_bf(tok_b[j], b_tok_b[j], TT, 4, None, None, ev)
                            elif c == 1:
                                j = rot(3, 2)
                                rope(ps, 8, TT, cst, b_csb[cb], [pbuf], [(tok_b[j][0:TT, :], b_tok_b[j])])
                                S.op(DVE, C("tensor_tensor",
                                    out=kdec_tok[0:TT, t, :].rearrange("p (h c) -> p h c", c=64), in0=tok_b[j][0:TT, :].rearrange("p (h c) -> p h c", c=64),
                                    in1=tab[0:TT, KDc:KDc + 8].unsqueeze(2).to_broadcast([TT, 8, 64]), op=ALU.mult),
                                    reads=[b_tok_b[j]] + CONST, writes=[b_kdec])
                                if full:
                                    def ev(pb, pbb, t=t):
                                        S.op(ACT, C("activation", out=kT_ret[:, :, t * TT:(t + 1) * TT], in_=pb, func=AF.Copy),
                                             reads=[pbb], writes=[b_kT])
                                    transposes_bf(tok_b[j], b_tok_b[j], TT, 4, None, None, ev)
                            elif c in (2, 3):
                                S.op(ACT, C("activation", out=v_tok[0:TT, t, (c - 2) * 512:(c - 1) * 512], in_=ps, func=AF.Copy),
                                     reads=[pbuf], writes=[b_vtok])
                                if c == 3:
                                    state_step(t, TT, CDc)
                            elif c == 4:
                                rs = rmsnorm_rstd(ps, pbuf, TT)
                                j = rot(3, 2)
                                S.op(ACT, C("activation", out=tok_b[j][0:TT, :], in_=ps, func=AF.Identity, scale=rs),
                                     reads=[pbuf, b_sstat], writes=[b_tok_b[j]])

                                def ev(pb, pbb, t=t):
                                    for kk in range(4):
                                        S.op(ACT, C("activation", out=cqnT[:, kk, t * TT:(t + 1) * TT], in_=pb[:, kk, :], func=AF.Identity,
                                                                                scale=tab[:, T_GCQ + kk:T_GCQ + kk + 1]),
                                             reads=[pbb] + CONST, writes=[b_cqn])
                                transposes_bf(tok_b[j], b_tok_b[j], TT, 4, None, None, ev)
                            elif c == 5:
                                rs = rmsnorm_rstd(ps, pbuf, TT)
                                si = 0
                                S.op(DVE, C("scalar_tensor_tensor", out=ckv_stg[si][0:TT, :], in0=ps, scalar=rs, in1=gckv[0:TT, :], op0=ALU.mult, op1=ALU.mult),
                                     reads=[pbuf, b_sstat] + CONST, writes=[b_ckv_stg[si]])
                                j = rot(3, 2)
                                S.op(ACT, C("activation", out=tok_b[j][0:TT, :], in_=ckv_stg[si][0:TT, :], func=AF.Copy),
                                     reads=[b_ckv_stg[si]], writes=[b_tok_b[j]])
                                if ckv_out is not None:
                                    S.dma(POOL, C("dma_start", out=ckv_out(t), in_=ckv_stg[si][0:TT, :]), b_ckv_stg[si].dsem,
                                          reads=[b_ckv_stg[si]], writes=[b_out])

                                def ev(pb, pbb, t=t):
                                    S.op(ACT, C("activation", out=ckvT[:, :, t * TT:(t + 1) * TT], in_=pb, func=AF.Copy), reads=[pbb], writes=[b_ckvT])
                                transposes_bf(tok_b[j], b_tok_b[j], TT, 4, None, None, ev)
                            else:
                                si = 0
                                j = rot(3, 2)
                                rope(ps, 1, TT, cst, b_csb[cb], [pbuf],
                                     [(kr_stg[si][0:TT, :], b_kr_stg[si]), (tok_b[j][0:TT, 0:64], b_tok_b[j]), (tok_b[j][0:TT, 64:128], b_tok_b[j])])
                                if kr_out is not None:
                                    S.dma(POOL, C("dma_start", out=kr_out(t), in_=kr_stg[si][0:TT, :]), b_kr_stg[si].dsem,
                                          reads=[b_kr_stg[si]], writes=[b_out])

                                def ev(pb, pbb, t=t):
                                    S.op(ACT, C("activation", out=krT_all[:, key_col0 + t * TT:key_col0 + (t + 1) * TT], in_=pb[:, 0, :], func=AF.Copy),
                                         reads=[pbb], writes=[b_krT])
                                transposes_bf(tok_b[j], b_tok_b[j], TT, 1, None, None, ev)
                        pend.append(post)
                        if len(pend) > 1:
                            pend.pop(0)()
                    while pend:
                        pend.pop(0)()
                S.tag = "front_rg"
                if full and sub >= 0:
                    for sl in range(2):
                        wt, wb = WS.next()
                        for mm in range(4):
                            m = sl * 4 + mm
                            bi = next_bank()
                            ps = bank(*bi)[:, 0:TB]
                            pbuf = b_DB[bi[0]][bi[1]]
                            mm_group(ps, pbuf, [(wt[:, (mm * 16 + k) * 128:(mm * 16 + k + 1) * 128], actb[:, k, 0:TB]) for k in range(16)], b_hT + [wb])
                            S.op(ACT, C("activation", out=rgT[:, m, 0:TB], in_=ps, func=AF.Silu), reads=[pbuf], writes=[b_rg])
            else:
                ci = cache[2]
                for t in range(4):
                    def ev(pb, pbb, t=t):
                        S.op(ACT, C("activation", out=ckvT[:, :, t * 128:(t + 1) * 128], in_=pb, func=AF.Copy), reads=[pbb], writes=[b_ckvT])
                    transposes_bf(cache_stage[ci][:, t, 0:512], b_cstage[ci], 128, 4, None, None, ev)
                    j = rot(3, 2)
                    S.op(DVE, C("tensor_copy", out=tok_b[j][:, 0:64], in_=cache_stage[ci][:, t, 512:576]), reads=[b_cstage[ci]], writes=[b_tok_b[j]])
                    S.op(DVE, C("tensor_copy", out=tok_b[j][:, 64:128], in_=cache_stage[ci][:, t, 512:576]), reads=[b_cstage[ci]], writes=[b_tok_b[j]])

                    def ev2(pb, pbb, t=t):
                        S.op(ACT, C("activation", out=krT_all[:, key_col0 + t * 128:key_col0 + (t + 1) * 128], in_=pb[:, 0, :], func=AF.Copy),
                             reads=[pbb], writes=[b_krT])
                    transposes_bf(tok_b[j], b_tok_b[j], 128, 1, None, None, ev2)
            if cache is not None:
                kvgen()
            rotN[0] = 4
            mmrot[0] = 0

        def kvgen_impl(TT, NT, kblk):
            TB = TT * NT
            wt, wb = WS.next()
            for h in range(H):
                bi = next_bank()
                ps = bank(*bi)[:, 0:TB]
                pbuf = b_DB[bi[0]][bi[1]]
                mm_group(ps, pbuf, [(wt[:, (h * 4 + kk) * 128:(h * 4 + kk + 1) * 128], ckvT[:, kk, 0:TB]) for kk in range(4)], [b_ckvT, wb])
                eng = ACT if h % 2 == 0 else DVE
                if eng == ACT:
                    S.op(ACT, C("activation", out=knT_blk[:, h, 0:TB], in_=ps, func=AF.Copy), reads=[pbuf], writes=[b_knb])
                else:
                    S.op(DVE, C("tensor_copy", out=knT_blk[:, h, 0:TB], in_=ps), reads=[pbuf], writes=[b_knb])
            wt, wb = WS.next()
            w3 = wt[:, 0:4096].rearrange("p (k n) -> p k n", n=1024)
            for t in range(NT):
                for c in range(2):
                    bi = next_bank()
                    ps = bank(*bi)[0:TT, :]
                    pbuf = b_DB[bi[0]][bi[1]]
                    mm_group(ps, pbuf, [(ckvT[:, kk, t * TT:(t + 1) * TT], w3[:, kk, c * 512:(c + 1) * 512]) for kk in range(4)], [b_ckvT, wb])
                    dst = v_blk[0:TT, c * 4:c * 4 + 4, t, :]
                    src = ps.rearrange("p (h e) -> p h e", e=128)
                    if c == 0:
                        S.op(ACT, C("activation", out=dst, in_=src, func=AF.Copy), reads=[pbuf], writes=[b_vb])
                    else:
                        S.op(DVE, C("tensor_copy", out=dst, in_=src), reads=[pbuf], writes=[b_vb])
            S.dma(POOL, C("dma_start", out=kscr[kblk][:, :, 0:TB], in_=knT_blk[:, :, 0:TB]), b_knb.dsem, reads=[b_knb], writes=[b_kscr[kblk]])
            S.dma(POOL, C("dma_start", out=vscr[kblk][0:TT, :, 0:NT, :], in_=v_blk[0:TT, :, 0:NT, :]), b_vb.dsem, reads=[b_vb], writes=[b_vscr[kblk]])

        def state_step(t, TT, CDc):
            kvb = (2, 0)
            for m in range(4):
                hb = b_DB[2][m // 2]
                out = DB[2][:, m * 256:(m + 1) * 256]
                S.op(PE, C("matmul", out, lhsT=kdec_tok[0:TT, t, m * 128:(m + 1) * 128], rhs=v_tok[0:TT, t, m * 256:(m + 1) * 256], start=True, stop=True),
                     reads=[b_kdec, b_vtok], writes=[hb])
            kv3 = DB[2][:, :].rearrange("p (m c) -> p m c", c=256)
            S.op(DVE, C("tensor_tensor", out=S_f[:], in0=S_f[:], in1=tab[:, CDc:CDc + 4].unsqueeze(2).to_broadcast([128, 4, 128]), op=ALU.mult),
                 reads=[b_Sf] + CONST, writes=[b_Sf])
            S.op(DVE, C("tensor_tensor", out=S_f[0:64], in0=S_f[0:64], in1=kv3[0:64, :, 0:128], op=ALU.add),
                 reads=[b_Sf, b_DB[2][0], b_DB[2][1]], writes=[b_Sf])
            S.op(DVE, C("tensor_tensor", out=S_f[64:128], in0=S_f[64:128], in1=kv3[64:128, :, 128:256], op=ALU.add),
                 reads=[b_Sf, b_DB[2][0], b_DB[2][1]], writes=[b_Sf])
            S.op(POOL, C("tensor_copy", out=S_bf[:, t + 1], in_=S_f[:]), reads=[b_Sf], writes=[b_Sbf[t + 1]])

        def carry_state(NT):
            S.op(POOL, C("tensor_copy", out=S_bf[:, 0], in_=S_f[:]), reads=[b_Sf], writes=[b_Sbf[0]])

        def retention(TT, NT, C_tabs):
            S.tag = "retention"
            TB = TT * NT
            DT, QD, KDc, CDc = C_tabs
            ob = [(3, 0), (3, 1)]

            def emit_st(m):
                if mmrot[0] % 2:
                    next_bank()
                sb0 = next_bank()
                next_bank()
                dbi = sb0[0]
                pr = m % 2
                for t in range(NT):
                    for hh in range(2):
                        S.op(PE, C("matmul", DB[dbi][0:TT, hh * 512 + t * TT:hh * 512 + (t + 1) * TT],
                                   lhsT=kT_ret[hh * 64:(hh + 1) * 64, m, t * TT:(t + 1) * TT],
                                   rhs=qT_ret[hh * 64:(hh + 1) * 64, m, t * TT:(t + 1) * TT], start=True, stop=True),
                             reads=[b_kT, b_qT], writes=[b_DB[dbi][hh]])
                sps = DB[dbi][0:TT, :].rearrange("p (h c) -> p h c", c=512)[:, :, 0:TB].rearrange("p h (t i) -> p h t i", i=TT)
                dtb = DT[0:TT, 2 * m:2 * m + 2, 0:TT].unsqueeze(2).to_broadcast([TT, 2, NT, TT])
                outp = pT[pr][0:TT, :, 0:TB].rearrange("p h (t i) -> p h t i", i=TT)
                S.op(DVE, C("tensor_tensor", out=outp, in0=sps, in1=dtb, op=ALU.mult),
                     reads=[b_DB[dbi][0], b_DB[dbi][1]] + CONST, writes=[b_pT[pr]])

            def emit_o(m):
                pr = m % 2
                for t in range(NT):
                    for hh in range(2):
                        h = 2 * m + hh
                        o_ap = bank(*ob[hh])[:, t * TT:(t + 1) * TT]
                        obuf = b_DB[ob[hh][0]][ob[hh][1]]
                        S.op(PE, C("matmul", o_ap, lhsT=v_tok[0:TT, t, h * 128:(h + 1) * 128], rhs=pT[pr][0:TT, hh, t * TT:(t + 1) * TT], start=True, stop=False),
                             reads=[b_vtok, b_pT[pr]], writes=[obuf], signal=(TT < 128))
                        S.op(PE, C("matmul", o_ap, lhsT=S_bf[hh * 64:(hh + 1) * 64, t, m, :], rhs=qdT_ret[hh * 64:(hh + 1) * 64, m, t * TT:(t + 1) * TT], start=False, stop=True),
                             reads=[b_Sbf[t], b_qdT], writes=[obuf], self_sync=(TT < 128))
                for hh in range(2):
                    h = 2 * m + hh
                    groupnorm_gate(bank(*ob[hh])[:, 0:TB], b_DB[ob[hh][0]][ob[hh][1]], h, TB)
                S.tag = "retention"

            emit_st(0)
            for m in range(4):
                if m + 1 < 4:
                    emit_st(m + 1)
                emit_o(m)

        def stats_finish(mean_ps, mean_buf, ex2_ps, ex2_buf, TB, inv):
            S.op(ACT, C("activation", out=mean_sb[:, 0:TB], in_=mean_ps, func=AF.Identity, scale=inv), reads=[mean_buf], writes=[b_mean])
            j = rot(1, NTMP)
            S.op(DVE, C("tensor_tensor", out=tmpf[j][:, 0:TB], in0=mean_sb[:, 0:TB], in1=mean_sb[:, 0:TB], op=ALU.mult), reads=[b_mean], writes=[b_tmpf[j]])
            S.op(DVE, C("scalar_tensor_tensor", out=tmpf[j][:, 0:TB], in0=ex2_ps, scalar=inv, in1=tmpf[j][:, 0:TB], op0=ALU.mult, op1=ALU.subtract), reads=[ex2_buf, b_tmpf[j]], writes=[b_tmpf[j]])
            S.op(ACT, C("activation", out=rstd_sb[:, 0:TB], in_=tmpf[j][:, 0:TB], func=AF.Sqrt, bias=tab[:, T_EPS:T_EPS + 1]), reads=[b_tmpf[j]] + CONST, writes=[b_rstd])
            S.op(DVE, C("reciprocal", out=rstd_sb[:, 0:TB], in_=rstd_sb[:, 0:TB]), reads=[b_rstd], writes=[b_rstd])

        def groupnorm_gate(o_ps, obuf, h, TB):
            i1, i2 = rot(7, NTB), rot(7, NTB)
            j = rot(1, NTMP)
            S.op(ACT, C("activation", out=tmpf[j][:, 0:TB], in_=o_ps, func=AF.Copy), reads=[obuf], writes=[b_tmpf[j]])
            S.op(ACT, C("activation", out=tmpb[i1][:, 0:TB], in_=tmpf[j][:, 0:TB], func=AF.Copy), reads=[b_tmpf[j]], writes=[b_tmpb[i1]])
            S.op(ACT, C("activation", out=tmpb[i2][:, 0:TB], in_=tmpf[j][:, 0:TB], func=AF.Square), reads=[b_tmpf[j]], writes=[b_tmpb[i2]])
            mb, eb = (2, 0), (2, 1)
            S.op(PE, C("matmul", bank(*mb)[:, 0:TB], lhsT=ones_b[:], rhs=tmpb[i1][:, 0:TB], start=True, stop=True), reads=[b_tmpb[i1]] + CONST, writes=[b_DB[2][0]])
            S.op(PE, C("matmul", bank(*eb)[:, 0:TB], lhsT=ones_b[:], rhs=tmpb[i2][:, 0:TB], start=True, stop=True), reads=[b_tmpb[i2]] + CONST, writes=[b_DB[2][1]])
            stats_finish(bank(*mb)[:, 0:TB], b_DB[2][0], bank(*eb)[:, 0:TB], b_DB[2][1], TB, 1.0 / 128.0)
            S.op(DVE, C("tensor_tensor", out=tmpf[j][:, 0:TB], in0=tmpf[j][:, 0:TB], in1=mean_sb[:, 0:TB], op=ALU.subtract), reads=[b_tmpf[j], b_mean], writes=[b_tmpf[j]])
            S.op(DVE, C("tensor_tensor", out=tmpf[j][:, 0:TB], in0=tmpf[j][:, 0:TB], in1=rstd_sb[:, 0:TB], op=ALU.mult), reads=[b_tmpf[j], b_rstd], writes=[b_tmpf[j]])
            S.op(ACT, C("activation", out=tmpf[j][:, 0:TB], in_=tmpf[j][:, 0:TB], func=AF.Identity, scale=tab[:, T_GRET + h:T_GRET + h + 1], bias=tab[:, T_BRET + h:T_BRET + h + 1]),
                 reads=[b_tmpf[j]] + CONST, writes=[b_tmpf[j]])
            S.op(DVE, C("tensor_tensor", out=actb[:, h, 0:TB], in0=tmpf[j][:, 0:TB], in1=rgT[:, h, 0:TB], op=ALU.mult), reads=[b_tmpf[j], b_rg], writes=[b_hT[h]])

        def qgen(TT, NT):
            S.tag = "qgen"
            TB = TT * NT
            wt, wb = WS.next()
            for h in range(H):
                bi = next_bank()
                ps = bank(*bi)[:, 0:TB]
                pbuf = b_DB[bi[0]][bi[1]]
                mm_group(ps, pbuf, [(wt[:, (h * 4 + kk) * 128:(h * 4 + kk + 1) * 128], cqnT[:, kk, 0:TB]) for kk in range(4)], [b_cqn, wb])
                if h % 2 == 0:
                    S.op(ACT, C("activation", out=qnT[:, h, 0:TB], in_=ps, func=AF.Copy), reads=[pbuf], writes=[b_qn])
                else:
                    S.op(DVE, C("tensor_copy", out=qnT[:, h, 0:TB], in_=ps), reads=[pbuf], writes=[b_qn])
            wt, wb = WS.next()
            w3 = wt[:, 0:2048].rearrange("p (k n) -> p k n", n=512)
            qpend = []
            for t in range(NT):
                cb = cur_cs[0]
                cst = cs_blk[cb][:, t, :]
                bi = next_bank()
                ps = bank(*bi)[0:TT, :]
                pbuf = b_DB[bi[0]][bi[1]]
                mm_group(ps, pbuf, [(cqnT[:, kk, t * TT:(t + 1) * TT], w3[:, kk, :]) for kk in range(4)], [b_cqn, wb])
                outs = []
                for jj in range(2):
                    v4 = tok_b[jj][0:TT, :].rearrange("p (h c d) -> p h c d", c=2, d=64)
                    for cpy in range(2):
                        outs.append((lambda half, v4=v4, cpy=cpy: v4[:, :, cpy, half * 32:(half + 1) * 32], jj * 4, jj * 4 + 4, b_tok_b[jj]))
                def post(t=t, ps=ps, pbuf=pbuf, cst=cst, outs=outs):
                    rope(ps, 8, TT, cst, b_csb[cb], [pbuf], outs)
                    for jj in range(2):
                        def ev(pb, pbb, t=t, jj=jj):
                            S.op(ACT, C("activation", out=qrT[:, jj * 4:jj * 4 + 4, t * TT:(t + 1) * TT], in_=pb, func=AF.Copy), reads=[pbb], writes=[b_qr])
                        transposes_bf(tok_b[jj], b_tok_b[jj], TT, 4, None, None, ev)
                qpend.append(post)
                if len(qpend) > 1:
                    qpend.pop(0)()
            while qpend:
                qpend.pop(0)()

        def kv_loads(keyblocks):
            loads = []
            for h in range(H):
                for (kb, kc0, TTk, NTk, bias_col, diag) in keyblocks:
                    def load(tile, buf, swsem, kb=kb, h=h, TTk=TTk, NTk=NTk):
                        S.dma(SP, C("dma_start", out=tile[:, 0:TTk * NTk], in_=kscr[kb][:, h, 0:TTk * NTk]), buf.dsem, reads=[b_kscr[kb]], writes=[buf])
                        S.dma(SP, C("dma_start", out=tile[0:TTk, 512:512 + NTk * 128].rearrange("p (t e) -> p t e", e=128), in_=vscr[kb][0:TTk, h, 0:NTk, :]),
                              buf.dsem, reads=[b_vscr[kb], b_kscr[kb]], writes=[buf], skip_own=True)
                    loads.append(load)
            return loads

        def attention(TB, keyblocks):
            S.tag = "attn"
            groups = []
            for h in range(H):
                tiles = []
                for (kb, kc0, TTk, NTk, bias_col, diag) in keyblocks:
                    for kt in range(NTk):
                        tiles.append((kb, kc0, TTk, NTk, bias_col, diag, kt))
                i = 0
                hg = []
                while i < len(tiles):
                    grp = tiles[i:i + 2]
                    if len(grp) == 2 and grp[1][0] != grp[0][0]:
                        grp = grp[:1]
                    hg.append(dict(h=h, tiles=grp, first=(i == 0), last=False))
                    i += len(grp)
                hg[-1]["last"] = True
                groups.extend(hg)
            cur = [None]

            def emit_qk(g):
                h = g["h"]
                grp = g["tiles"]
                if grp[0][6] == 0:
                    cur[0] = KS.next()
                kvt, kvb = cur[0]
                g["kv"] = (kvt, kvb)
                di = rot(5, 2)
                g["di"] = di
                sdb = DB[di]
                TTk = grp[0][2]
                ng = len(grp)
                for gi, (kb, kc0, _, NTk, bias_col, diag, kt) in enumerate(grp):
                    sp = sdb[0:TTk, gi * 512:gi * 512 + TB]
                    S.op(PE, C("matmul", sp, lhsT=kvt[:, kt * TTk:(kt + 1) * TTk], rhs=qnT[:, h, 0:TB], start=True, stop=False),
                         reads=[kvb, b_qn], writes=[b_DB[di][gi]], signal=False)
                for gi, (kb, kc0, _, NTk, bias_col, diag, kt) in enumerate(grp):
                    sp = sdb[0:TTk, gi * 512:gi * 512 + TB]
                    r0 = gi * 64
                    S.op(PE, C("matmul", sp, lhsT=krT_all[r0:r0 + 64, kc0 + kt * TTk:kc0 + (kt + 1) * TTk], rhs=qrT[r0:r0 + 64, h, 0:TB], start=False, stop=True),
                         reads=[b_krT, b_qr], writes=[b_DB[di][gi]])
                pi = rot(4, 2)
                g["pi"] = pi
                bias_col = grp[0][4]
                src = sdb[0:TTk, :].rearrange("p (g c) -> p g c", c=512)[:, 0:ng, 0:TB]
                S.op(ACT, C("activation", out=pT[pi][0:TTk, 0:ng, 0:TB], in_=src, func=AF.Exp, scale=MLA_SCALE, bias=tab[0:TTk, bias_col:bias_col + 1]),
                     reads=[b_DB[di][g_] for g_ in range(ng)] + CONST, writes=[b_pT[pi]])
                for gi, (kb, kc0, _, NTk, bias_col, diag, kt) in enumerate(grp):
                    if diag:
                        if kt > 0:
                            S.op(POOL, C("memset", pT[pi][:, gi, 0:128 * kt], 0.0), writes=[b_pT[pi]])
                        S.op(POOL, C("memset", pT[pi][64:128, gi, 128 * kt:128 * kt + 64], 0.0), writes=[b_pT[pi]])

            def emit_pv(g):
                h = g["h"]
                grp = g["tiles"]
                kvt, kvb = g["kv"]
                pi = g["pi"]
                TTk = grp[0][2]
                set_ = 2 + (h % 2)
                o_ps = bank(set_, 0)[:, 0:TB]
                s_ps = bank(set_, 1)[:, 0:TB]
                obuf, sbuf_ = b_DB[set_][0], b_DB[set_][1]
                ng = len(grp)
                for gi, (kb, kc0, _, NTk, bias_col, diag, kt) in enumerate(grp):
                    first = g["first"] and gi == 0
                    last = g["last"] and gi == ng - 1
                    S.op(PE, C("matmul", o_ps, lhsT=kvt[0:TTk, 512 + kt * 128:512 + (kt + 1) * 128], rhs=pT[pi][0:TTk, gi, 0:TB], start=first, stop=last),
                         reads=[kvb, b_pT[pi]], writes=[obuf], signal=False)
                    S.op(PE, C("matmul", s_ps, lhsT=ones_b[0:TTk, :], rhs=pT[pi][0:TTk, gi, 0:TB], start=first, stop=last),
                         reads=[b_pT[pi]] + CONST, writes=[sbuf_])
                if g["last"]:
                    convert_some(5, 2)
                    j = rot(1, NTMP)
                    S.op(DVE, C("reciprocal", out=tmpf[j][:, 0:TB], in_=s_ps), reads=[sbuf_], writes=[b_tmpf[j]])
                    S.op(DVE, C("tensor_tensor", out=actb[:, 8 + h, 0:TB], in0=o_ps, in1=tmpf[j][:, 0:TB], op=ALU.mult), reads=[obuf, b_tmpf[j]], writes=[b_hT[8 + h]])

            G = len(groups)
            emit_qk(groups[0])
            for gi_ in range(G):
                if gi_ + 1 < G:
                    emit_qk(groups[gi_ + 1])
                emit_pv(groups[gi_])

        def ln_block(TB, seq, nslab, mt_per_slab, kt, src_of, gcol0, mode, out_fn=None):
            S.tag = "ln%d" % mode
            mb, eb = (3, 0), (3, 1)
            pend = []

            def stats_mm(m, i1, i2):
                S.op(PE, C("matmul", bank(*mb)[:, 0:TB], lhsT=ones_b[:], rhs=tmpb[i1][:, 0:TB], start=(m == 0), stop=(m == 15), skip_group_check=True),
                     reads=[b_tmpb[i1]] + CONST, writes=[b_DB[3][0]])
                S.op(PE, C("matmul", bank(*eb)[:, 0:TB], lhsT=ones_b[:], rhs=tmpb[i2][:, 0:TB], start=(m == 0), stop=(m == 15), skip_group_check=True),
                     reads=[b_tmpb[i2]] + CONST, writes=[b_DB[3][1]])
            for sl in range(nslab):
                wt, wb = WS.next()
                for mm in range(mt_per_slab):
                    m = sl * mt_per_slab + mm
                    bi = next_bank()
                    ps = bank(*bi)[:, 0:TB]
                    pbuf = b_DB[bi[0]][bi[1]]
                    rbuf = b_hT if mode == 1 else (b_act + [b_actw])
                    mm_group(ps, pbuf, [(wt[:, (mm * kt + k) * 128:(mm * kt + k + 1) * 128], src_of(k)) for k in range(kt)], rbuf + [wb])
                    S.op(DVE, C("scalar_tensor_tensor", out=xaT[:, m, 0:TB], in0=ps, scalar=modT[:, gcol0 + m, seq:seq + 1], in1=xaT[:, m, 0:TB], op0=ALU.mult, op1=ALU.add),
                         reads=[pbuf, b_mod, b_xaT[m]], writes=[b_xaT[m]])
                    i1, i2 = rot(7, NTB), rot(7, NTB)
                    S.op(ACT, C("activation", out=tmpb[i1][:, 0:TB], in_=xaT[:, m, 0:TB], func=AF.Copy), reads=[b_xaT[m]], writes=[b_tmpb[i1]])
                    S.op(ACT, C("activation", out=tmpb[i2][:, 0:TB], in_=xaT[:, m, 0:TB], func=AF.Square), reads=[b_xaT[m]], writes=[b_tmpb[i2]])
                    pend.append((m, i1, i2))
                    if len(pend) > 1:
                        stats_mm(*pend.pop(0))
            while pend:
                stats_mm(*pend.pop(0))
            stats_finish(bank(*mb)[:, 0:TB], b_DB[3][0], bank(*eb)[:, 0:TB], b_DB[3][1], TB, 1.0 / 2048.0)
            for m in range(16):
                j = rot(1, NTMP)
                S.op(POOL, C("tensor_tensor", out=tmpf[j][:, 0:TB], in0=xaT[:, m, 0:TB], in1=mean_sb[:, 0:TB], op=ALU.subtract), reads=[b_xaT[m], b_mean], writes=[b_tmpf[j]])
                S.op(DVE, C("tensor_tensor", out=tmpf[j][:, 0:TB], in0=tmpf[j][:, 0:TB], in1=rstd_sb[:, 0:TB], op=ALU.mult), reads=[b_tmpf[j], b_rstd], writes=[b_tmpf[j]])
                if mode == 1:
                    S.op(ACT, C("activation", out=actb[:, m, 0:TB], in_=tmpf[j][:, 0:TB], func=AF.Identity, scale=G2T[:, seq, m:m + 1], bias=B2T[:, seq, m:m + 1]),
                         reads=[b_tmpf[j], b_seqtab], writes=[b_hT[m]])
                    S.op(ACT, C("activation", out=xaT[:, m, 0:TB], in_=tmpf[j][:, 0:TB], func=AF.Identity, scale=AG1[:, m:m + 1], bias=AB1[:, m:m + 1]),
                         reads=[b_tmpf[j], b_seqtab], writes=[b_xaT[m]])
                else:
                    S.op(ACT, C("activation", out=xaT[:, m, 0:TB], in_=tmpf[j][:, 0:TB], func=AF.Identity, scale=tab[:, T_LN2G + m:T_LN2G + m + 1], bias=tab[:, T_LN2B + m:T_LN2B + m + 1]),
                         reads=[b_tmpf[j]] + CONST, writes=[b_xaT[m]])

        def ffn_up(TB):
            S.tag = "ffn_up"
            for sl in range(22):
                wt, wb = WS.next()
                for mm in range(2):
                    m = sl * 2 + mm
                    bg = next_bank()
                    gps = bank(*bg)[:, 0:TB]
                    mm_group(gps, b_DB[bg[0]][bg[1]], [(wt[:, (mm * 16 + k) * 128:(mm * 16 + k + 1) * 128], actb[:, k, 0:TB]) for k in range(16)], b_hT + [wb])
                    bu = next_bank()
                    ups = bank(*bu)[:, 0:TB]
                    mm_group(ups, b_DB[bu[0]][bu[1]], [(wt[:, ((2 + mm) * 16 + k) * 128:((2 + mm) * 16 + k + 1) * 128], actb[:, k, 0:TB]) for k in range(16)], b_hT + [wb])
                    j = rot(1, NTMP)
                    S.op(ACT, C("activation", out=tmpf[j][:, 0:TB], in_=gps, func=AF.Silu), reads=[b_DB[bg[0]][bg[1]]], writes=[b_tmpf[j]])
                    S.op(DVE, C("tensor_tensor", out=actT[:, m, 0:TB], in0=ups, in1=tmpf[j][:, 0:TB], op=ALU.mult), reads=[b_DB[bu[0]][bu[1]], b_tmpf[j]], writes=[b_act[m], b_actw])

        if len(stg) == 2:
            stg.append(ckv_stg[0][:, :])
            b_stg.append(b_ckv_stg[0])

        def output_y(TT, NT, y_dst):
            S.tag = "out_y"
            for t in range(NT):
                for g in range(4):
                    si = rot(8, 3)
                    bi = next_bank()
                    pb = bank(*bi)[0:TT, :].rearrange("p (k c) -> p k c", c=128)
                    for kk in range(4):
                        k = g * 4 + kk
                        S.op(PE, C("transpose", pb[:, kk, :], xaT[:, k, t * TT:(t + 1) * TT], ident[:]),
                             reads=[b_xaT[k]] + CONST, writes=[b_DB[bi[0]][bi[1]]], signal=(kk == 3))
                    if g % 2 == 0:
                        S.op(ACT, C("activation", out=stg[si][0:TT, :], in_=bank(*bi)[0:TT, :], func=AF.Copy), reads=[b_DB[bi[0]][bi[1]]], writes=[b_stg[si]])
                    else:
                        S.op(DVE, C("tensor_copy", out=stg[si][0:TT, :], in_=bank(*bi)[0:TT, :]), reads=[b_DB[bi[0]][bi[1]]], writes=[b_stg[si]])
                    S.dma(POOL, C("dma_start", out=y_dst(t)[:, g * 512:(g + 1) * 512], in_=stg[si][0:TT, :]), b_stg[si].dsem, reads=[b_stg[si]], writes=[b_out])

        def main_block(TT, NT, seq, x_src, cs_rows, key_col0, kblk, C_tabs, keyblocks, ckv_out, kr_out, y_dst, blk_id=None, next_x=None):
            TB = TT * NT
            front(True, TT, NT, seq, x_src, cs_rows, key_col0, kblk, C_tabs, ckv_out=ckv_out, kr_out=kr_out, blk_id=blk_id, next_x=next_x)
            if sub < 1:
                return
            retention(TT, NT, C_tabs)
            if sub < 2:
                return
            qgen(TT, NT)
            if sub < 3:
                return
            attention(TB, keyblocks)
            if sub < 4:
                return
            ln_block(TB, seq, 4, 4, 16, lambda k: actb[:, k, 0:TB], 32, 1)
            if sub < 5:
                return
            ffn_up(TB)
            if sub < 6:
                return
            ln_block(TB, seq, 16, 1, 44, lambda k: actT[:, k, 0:TB], 80, 2)
            if sub < 7:
                return
            output_y(TT, NT, y_dst)

        C128 = (dt128, qd128, T_KD128, T_CD128)
        C64 = (dt128, qd128, T_KD64, T_CD64)

        for b in range(NBLK):
            WS.extend([L_(k, i) for (k, i) in PREFIX_SLABS])
        for b in range(NBLK):
            WS.extend([L_(k, i) for (k, i) in MAIN_SLABS])
        for s in range(2):
            for b in range(8):
                WS.extend([L_("mla", 0), L_("mla", 1)])
            WS.extend([L_(k, i) for (k, i) in MAIN_SLABS])
        for i in range(NBLK):
            kbs = [(p, p * TBP, 128, 4, T_VF, False) for p in range(8)] + [(8 + q, HALF + q * TBP, 128, 4, T_ZERO, q == i) for q in range(i + 1)]
            KS.extend(kv_loads(kbs))
        for s in range(2):
            kbs = [(p, p * TBP, 128, 4, T_ZERO, False) for p in range(8)] + [(16, HALF, 64, 1, T_ZERO, False)]
            KS.extend(kv_loads(kbs))

        S.op(DVE, C("memset", S_f[:], 0.0), writes=[b_Sf])
        npre = 0 if stage < 1 else (1 if stage == 1 else NBLK)
        def xpre_src(b):
            return lambda t: xpre[b * TBP + t * 128:b * TBP + (t + 1) * 128, :]

        def xown_src(i):
            return lambda t: xown[i * TBP + t * 128:i * TBP + (t + 1) * 128, :]

        S.pfx = "P:"
        for b in range(npre):
            convert_some(2)
            carry_state(4)
            nx = (("pre", b + 1), xpre_src(b + 1), 128, 4) if b + 1 < NBLK else (("own", 0), xown_src(0), 128, 4)
            front(False, 128, 4, 0, xpre_src(b), cs_pre[b * TBP:(b + 1) * TBP, :], b * TBP, b, C128, blk_id=("pre", b), next_x=nx)
            if stage >= 3:
                ada_part(32 + 8 * b, 40 + 8 * b)
        convert_some(100)
        if stage < 4:
            convert_some(100, 2)
        S.op(DVE, C("tensor_scalar", out=S_f[:], in0=S_f[:], scalar1=tab[:, T_VF + 1:T_VF + 2], scalar2=None, op0=ALU.mult), reads=[b_Sf] + CONST, writes=[b_Sf])

        for s in range(3 if stage >= 3 else 0):
            j = rot(1, NTMP)
            S.op(DVE, C("tensor_scalar", out=tmpf[j][:, 0:16], in0=modT[:, 64:80, s], scalar1=1.0, scalar2=None, op0=ALU.add), reads=[b_mod], writes=[b_tmpf[j]])
            S.op(DVE, C("tensor_tensor", out=G2T[:, s, :], in0=tmpf[j][:, 0:16], in1=tab[:, T_LN1G:T_LN1G + 16], op=ALU.mult), reads=[b_tmpf[j]] + CONST, writes=[b_seqtab])
            S.op(DVE, C("tensor_tensor", out=tmpf[j][:, 0:16], in0=tmpf[j][:, 0:16], in1=tab[:, T_LN1B:T_LN1B + 16], op=ALU.mult), reads=[b_tmpf[j]] + CONST, writes=[b_tmpf[j]])
            S.op(DVE, C("tensor_tensor", out=B2T[:, s, :], in0=tmpf[j][:, 0:16], in1=modT[:, 48:64, s], op=ALU.add), reads=[b_tmpf[j], b_mod], writes=[b_seqtab])
        S.op(DVE, C("tensor_scalar", out=AG1[:], in0=tab[:, T_LN1G:T_LN1G + 16], scalar1=ALPHA, scalar2=None, op0=ALU.mult), reads=CONST, writes=[b_seqtab])
        S.op(DVE, C("tensor_scalar", out=AB1[:], in0=tab[:, T_LN1B:T_LN1B + 16], scalar1=ALPHA, scalar2=None, op0=ALU.mult), reads=CONST, writes=[b_seqtab])

        S.pfx = "M:"
        nmain = 0 if stage < 4 else (1 if stage == 4 else (2 if stage == 5 else NBLK))
        for i in range(nmain):
            carry_state(4)
            kbs = [(p, p * TBP, 128, 4, T_VF, False) for p in range(8)] + [(8 + q, HALF + q * TBP, 128, 4, T_ZERO, q == i) for q in range(i + 1)]
            nx = (("own", i + 1), xown_src(i + 1), 128, 4) if i + 1 < NBLK else None
            main_block(128, 4, 0,
                       xown_src(i),
                       cs_own[i * TBP:(i + 1) * TBP, :],
                       HALF + i * TBP, 8 + i, C128, kbs,
                       lambda t, i=i: ckv_own[i * TBP + t * 128:i * TBP + (t + 1) * 128, :],
                       lambda t, i=i: kr_own[i * TBP + t * 128:i * TBP + (t + 1) * 128, :],
                       lambda t, i=i: y_own[i * TBP + t * 128:i * TBP + (t + 1) * 128, :], blk_id=("own", i), next_x=nx)
        S.dma(POOL, C("dma_start", out=st_own, in_=S_f[:]), b_Sf.dsem, reads=[b_Sf], writes=[b_out])

        S.pfx = "S:"
        for s in range(2 if stage >= 7 else 0):
            cache_load(0, ckv_c[s, 0:TBP, :], kr_c[s, 0:TBP, :])
            for p in range(8):
                if p + 1 < 8:
                    cache_load((p + 1) % 2, ckv_c[s, (p + 1) * TBP:(p + 2) * TBP, :], kr_c[s, (p + 1) * TBP:(p + 2) * TBP, :])
                front(False, 128, 4, 0, None, None, p * TBP, p, C128, cache=(None, None, p % 2))
            S.dma(POOL, C("dma_start", out=S_f[:], in_=st_c[s]), b_Sf.dsem, writes=[b_Sf])
            carry_state(1)
            kbs = [(p, p * TBP, 128, 4, T_ZERO, False) for p in range(8)] + [(16, HALF, 64, 1, T_ZERO, False)]
            main_block(64, 1, 1 + s,
                       lambda t, s=s: xsmp[s],
                       cs_smp,
                       HALF, 16, C64, kbs,
                       lambda t, s=s: ckv_so[s],
                       lambda t, s=s: kr_so[s],
                       lambda t, s=s: y_smp[s])
            S.dma(POOL, C("dma_start", out=st_so[s], in_=S_f[:]), b_Sf.dsem, reads=[b_Sf], writes=[b_out])

        S.final_wait(POOL, [b_out])
        S.replay()
        import os
        if os.environ.get("MK_TAGS"):
            import pickle
            pickle.dump(S.pe_tags, open(os.environ["MK_TAGS"], "wb"))
        print("ops", S.n_ops, "waits", S.n_wait, "sems", S.nsem, {k: len(v) for k, v in S.prog.items()})
    return nc


_CACHE = {}


def _prep_shared(inp):
    f = np.float32
    w_in = inp["w_in"][0]
    w_ada = inp["w_ada"][0]
    sh = {}
    wa = w_ada.reshape(16, 128, 96, 128).transpose(2, 1, 0, 3)
    sh["wada"] = np.ascontiguousarray(wa.reshape(24, 4, 128, 16 * 128).transpose(0, 2, 1, 3)).reshape(24, 128, 8192)
    offs = [0, 512, 1024, 1536, 2048, 3072, 3584, 4096, 4160]
    chunks = [slice(0, 512), slice(512, 1024), slice(1024, 1536), slice(1536, 2048), slice(3072, 3584), slice(3584, 4096), slice(4096, 4160)]
    sh["wtm"] = np.stack([_pad(_tm_chunk(w_in, c), 8192) for c in chunks])

    def lhs_tiles(w, col0, ntile):
        kt = w.shape[0] // 128
        sub = w[:, col0:col0 + ntile * 128].reshape(kt, 128, ntile, 128).transpose(2, 1, 0, 3)
        return np.ascontiguousarray(sub)

    rg = lhs_tiles(w_in, 2048, 8)
    sh["wrg"] = np.ascontiguousarray(rg.reshape(2, 4, 128, 2048).transpose(0, 2, 1, 3)).reshape(2, 128, 8192)
    w_uq = inp["w_uq"][0]
    w_uk = inp["w_uk"][0].reshape(512, 1024)
    w_uv = inp["w_uv"][0].reshape(512, 1024)
    uk = lhs_tiles(w_uk, 0, 8)
    uqn = lhs_tiles(np.ascontiguousarray(w_uq[:, :, 0:128]).reshape(512, 1024), 0, 8)
    uqr = np.ascontiguousarray(w_uq[:, :, 128:192]).reshape(512, 512)
    sh["wmla"] = np.stack([
        np.ascontiguousarray(uk.transpose(1, 0, 2, 3)).reshape(128, 4096),
        _tm_chunk(w_uv, slice(0, 1024)),
        np.ascontiguousarray(uqn.transpose(1, 0, 2, 3)).reshape(128, 4096),
        _pad(_tm_chunk(uqr, slice(0, 512)), 4096)])
    wo = lhs_tiles(inp["w_out"][0], 0, 16)
    sh["wout"] = np.ascontiguousarray(wo.reshape(4, 4, 128, 2048).transpose(0, 2, 1, 3)).reshape(4, 128, 8192)
    wg = lhs_tiles(inp["w_gate"][0], 0, 44).reshape(22, 2, 128, 2048)
    wu = lhs_tiles(inp["w_up"][0], 0, 44).reshape(22, 2, 128, 2048)
    gu = np.concatenate([wg, wu], axis=1)
    sh["wgu"] = np.ascontiguousarray(gu.transpose(0, 2, 1, 3)).reshape(22, 128, 8192)
    wd = lhs_tiles(inp["w_down"][0], 0, 16)
    sh["wd"] = wd.reshape(16, 128, 5632)
    dec = _decay_tables()
    sh["ident"] = np.eye(128, dtype=f)
    sh["dt128"], sh["qd128"] = dec[128][0].astype(ml_dtypes.bfloat16), dec[128][1]
    sh["gckv"] = np.ascontiguousarray(np.broadcast_to(inp["g_ckv"][0][None, :], (128, KVL))).astype(f)
    tab = np.zeros((128, NTAB), f)
    tab[:, T_LN1G:T_LN1G + 16] = _fm(inp["ln1_g"][0])
    tab[:, T_LN1B:T_LN1B + 16] = _fm(inp["ln1_b"][0])
    tab[:, T_LN2G:T_LN2G + 16] = _fm(inp["ln2_g"][0])
    tab[:, T_LN2B:T_LN2B + 16] = _fm(inp["ln2_b"][0])
    tab[:, T_GRET:T_GRET + 8] = _fm(inp["g_ret"][0])
    tab[:, T_BRET:T_BRET + 8] = _fm(inp["b_ret"][0])
    tab[:, T_GCQ:T_GCQ + 4] = _fm(inp["g_cq"][0])
    tab[:, T_BADA:T_BADA + 96] = _fm(inp["b_ada"][0])
    tab[:, T_KD128:T_KD128 + 8] = dec[128][2]
    tab[:, T_KD64:T_KD64 + 8] = dec[64][2]
    tab[:, T_CD128:T_CD128 + 4] = dec[128][3]
    tab[:, T_CD64:T_CD64 + 4] = dec[64][3]
    tab[:, T_EPS] = EPS
    tab[:, T_ZERO] = 0.0
    sh["tab"] = tab
    sh["cs_all"] = _rope_table(np.arange(SEQ))
    sh["cs_smp"] = _rope_table(PAST + np.arange(SSEQ))
    return sh


def make_in_maps(inp):
    f = np.float32
    sh = _prep_shared(inp)
    in_maps = []
    pairlay = lambda s: np.ascontiguousarray(s.reshape(4, 2, 64, 128).transpose(1, 2, 0, 3)).reshape(128, 4, 128)
    for c in range(8):
        b, half = c // 2, c % 2
        tab = sh["tab"].copy()
        tab[:, T_VF] = 0.0 if half == 1 else NEG
        tab[:, T_VF + 1] = 1.0 if half == 1 else 0.0
        cs = np.stack([inp["c_prompt"][b], inp["c_sample"][2 * c], inp["c_sample"][2 * c + 1]], axis=1)
        m = {
            "xpre": inp["x_prompt"][b, 0:HALF], "xown": inp["x_prompt"][b, half * HALF:(half + 1) * HALF],
            "xsmp": inp["x_sample"][2 * c:2 * c + 2],
            "cT": np.ascontiguousarray(cs.reshape(16, 128, 3).transpose(1, 0, 2)),
            "tab": tab,
            "cs_pre": sh["cs_all"][0:HALF], "cs_own": sh["cs_all"][half * HALF:(half + 1) * HALF], "cs_smp": sh["cs_smp"],
            "ckv_c": inp["cache_mla_ckv"][0, 2 * c:2 * c + 2], "kr_c": inp["cache_mla_krope"][0, 2 * c:2 * c + 2],
            "st_c": np.stack([pairlay(inp["state_ret"][0, 2 * c + s]) for s in range(2)]),
        }
        for k in ("ident", "dt128", "qd128", "gckv", "wada", "wtm", "wrg", "wmla", "wout", "wgu", "wd"):
            m[k] = sh[k]
        in_maps.append({k: (np.ascontiguousarray(v) if k == "dt128" else np.ascontiguousarray(v, dtype=f)) for k, v in m.items()})
    return in_maps


def assemble(R):
    f = np.float32
    unpair = lambda s: np.ascontiguousarray(s.reshape(2, 64, 4, 128).transpose(2, 0, 1, 3)).reshape(8, 64, 128)
    yp = np.zeros((NB, SEQ, D), f)
    ckvp = np.zeros((1, NB, SEQ, KVL), f)
    krp = np.zeros((1, NB, SEQ, ROPE), f)
    rsp = np.zeros((1, NB, H, RDK, RDV), f)
    ys = np.zeros((NSB, SSEQ, D), f)
    ckvs = np.zeros((1, NSB, SSEQ, KVL), f)
    krs = np.zeros((1, NSB, SSEQ, ROPE), f)
    rss = np.zeros((1, NSB, H, RDK, RDV), f)
    for c in range(8):
        b, half = c // 2, c % 2
        sl = slice(half * HALF, (half + 1) * HALF)
        yp[b, sl] = R[c]["y_own"]
        ckvp[0, b, sl] = R[c]["ckv_own"]
        krp[0, b, sl] = R[c]["kr_own"]
        if half == 1:
            rsp[0, b] = unpair(R[c]["st_own"])
        ys[2 * c:2 * c + 2] = R[c]["y_smp"]
        ckvs[0, 2 * c:2 * c + 2] = R[c]["ckv_so"]
        krs[0, 2 * c:2 * c + 2] = R[c]["kr_so"]
        for s in range(2):
            rss[0, 2 * c + s] = unpair(R[c]["st_so"][s])
    return (yp, ys, ckvp, krp, rsp, ckvs, krs, rss)


def kernel(**inp):
    inp = {k: np.asarray(v) for k, v in inp.items()}
    in_maps = make_in_maps(inp)
    if "nc" not in _CACHE:
        _CACHE["nc"] = build_program()
    nc = _CACHE["nc"]
    res = run_bass_kernel_spmd(nc, in_maps, core_ids=list(range(8)))
    return assemble(res.results)
```

```python
import numpy as np
import ml_dtypes
import concourse.bass as bass
import concourse.mybir as mybir
from concourse.bass_utils import run_bass_kernel_spmd
from contextlib import ExitStack

F32 = mybir.dt.float32
BF16 = mybir.dt.bfloat16
ALU = mybir.AluOpType
AF = mybir.ActivationFunctionType

PE, ACT, DVE, POOL, SP = "pe", "act", "dve", "pool", "sp"


def C(name, *a, **k):
    return (name, a, k)

D = 2048
NB = 4
SEQ = 8192
NSB = 16
SSEQ = 64
PAST = 4096
H = 8
RDK = 64
RDV = 128
NOPE = 128
ROPE = 64
QL = 512
KVL = 512
DFF = 5632
ALPHA = 2.0 ** 0.25
EPS = 1e-5
MLA_SCALE = 192.0 ** -0.5
HALF = 4096
TBP = 512
NBLK = HALF // TBP
NEG = -30000.0


class Buf:
    __slots__ = ("name", "w", "r", "aliases", "dsem", "excl")

    def __init__(self, name, dsem=None):
        self.name = name
        self.w = None
        self.r = {}
        self.aliases = []
        self.dsem = dsem
        self.excl = False


class DmaSem:
    __slots__ = ("sem", "cnt", "key")

    def __init__(self, sem, key):
        self.sem = sem
        self.cnt = 0
        self.key = key


class Sched:
    def __init__(self, nc, stack, same_engine_sync=True):
        self.nc = nc
        self.stack = stack
        self.same_engine_sync = same_engine_sync
        self.sems = {}
        self.prog = {}
        self.cnt = {}
        self.seen = {}
        self.nsem = 0
        for k in (PE, ACT, DVE, POOL, SP):
            self.prog[k] = []
            self.cnt[k] = 0
            self.seen[k] = {}
            if k != SP:
                self.sems[k] = stack.enter_context(nc.semaphore("prog_" + k))
                self.nsem += 1
        self.n_wait = 0
        self.n_ops = 0
        self.dsems = []
        self.tag = ""
        self.pfx = ""
        self.pe_tags = []

    def dma_sem(self, name):
        key = "d_" + name
        self.sems[key] = self.stack.enter_context(self.nc.semaphore(key))
        self.nsem += 1
        ds = DmaSem(self.sems[key], key)
        self.dsems.append(ds)
        return ds

    def buf(self, name, dma=False):
        return Buf(name, self.dma_sem(name) if dma else None)

    def _collect(self, reads, writes, ek=None):
        deps = {}

        def add(ev):
            if ev is None:
                return
            k, v = ev
            if deps.get(k, 0) < v:
                deps[k] = v

        for b in reads:
            add(b.w)
            if b.excl:
                for k, v in b.r.items():
                    if k != ek:
                        add((k, v))
            for a in b.aliases:
                add(a.w)
        for b in writes:
            add(b.w)
            for k, v in b.r.items():
                add((k, v))
            for a in b.aliases:
                add(a.w)
                for k, v in a.r.items():
                    add((k, v))
        return deps

    def _waits(self, ek, deps, self_sync=False):
        seen = self.seen[ek]
        waits = []
        for k, v in deps.items():
            if k == ek and (ek == PE or not self.same_engine_sync) and not self_sync:
                continue
            if seen.get(k, 0) < v:
                seen[k] = v
                waits.append((k, v))
        self.n_wait += len(waits)
        return waits

    def _update(self, ev, reads, writes):
        k, v = ev
        for b in reads:
            if b.r.get(k, 0) < v:
                b.r[k] = v
        for b in writes:
            b.w = ev
            b.r = {}

    def op(self, ek, fn, reads=(), writes=(), signal=True, self_sync=False):
        if ek == PE:
            self.pe_tags.append(self.pfx + self.tag)
        deps = self._collect(reads, writes, ek)
        waits = self._waits(ek, deps, self_sync)
        if signal:
            self.cnt[ek] += 1
            ev = (ek, self.cnt[ek])
            self.prog[ek].append((waits, fn, (ek, 1)))
        else:
            ev = (ek, self.cnt[ek] + 1)
            self.prog[ek].append((waits, fn, None))
        self._update(ev, reads, writes)
        self.n_ops += 1
        return ev

    def dma(self, qk, fn, sem, reads=(), writes=(), skip_own=False):
        deps = self._collect(reads, writes)
        if skip_own:
            deps.pop(sem.key, None)
        waits = self._waits(qk, deps)
        sem.cnt += 16
        ev = (sem.key, sem.cnt)
        self.prog[qk].append((waits, fn, (sem.key, 16)))
        self._update(ev, reads, writes)
        self.n_ops += 1
        return ev

    def final_wait(self, ek, bufs):
        deps = self._collect((), bufs)
        for ds in self.dsems:
            if ds.cnt > 0:
                deps[ds.key] = max(deps.get(ds.key, 0), ds.cnt)
        waits = self._waits(ek, deps)
        self.prog[ek].append((waits, None, None))

    def replay(self):
        nc = self.nc
        sems = self.sems
        prog = self.prog

        def run(eng, items):
            for waits, fn, inc in items:
                for k, v in waits:
                    eng.wait_ge(sems[k], v)
                if fn is None:
                    continue
                ins = getattr(eng, fn[0])(*fn[1], **fn[2])
                if inc is not None:
                    ins.then_inc(sems[inc[0]], inc[1])

        with nc.Block() as block:
            @block.tensor
            def _(e):
                run(e, prog[PE])

            @block.scalar
            def _(e):
                run(e, prog[ACT])

            @block.vector
            def _(e):
                run(e, prog[DVE])

            @block.gpsimd
            def _(e):
                run(e, prog[POOL])

            @block.sync
            def _(e):
                run(e, prog[SP])


class Stream:
    def __init__(self, S, tiles, name, slack=0):
        self.S = S
        self.slack = slack
        self.tiles = tiles
        self.bufs = [S.buf(f"{name}{i}", dma=True) for i in range(len(tiles))]
        self.sw_sems = [S.dma_sem(f"{name}{i}_sw") for i in range(len(tiles))]
        self.plan = []
        self.issued = 0
        self.taken = 0

    def extend(self, loads):
        self.plan.extend(loads)

    def next(self):
        R = len(self.tiles)
        while self.issued < len(self.plan) and self.issued < max(self.taken + 1, self.taken + R - self.slack):
            i = self.issued
            s = i % R
            self.plan[i](self.tiles[s], self.bufs[s], self.sw_sems[s])
            self.issued += 1
        s = self.taken % R
        assert self.taken < self.issued
        self.taken += 1
        return self.tiles[s], self.bufs[s]


def _tm_chunk(w, cols):
    sub = w[:, cols]
    kt = sub.shape[0] // 128
    return np.ascontiguousarray(sub.reshape(kt, 128, sub.shape[1]).transpose(1, 0, 2)).reshape(128, -1)


def _pad(a, L):
    if a.shape[1] == L:
        return a
    out = np.zeros((a.shape[0], L), a.dtype)
    out[:, :a.shape[1]] = a
    return out


def _fm(v):
    return np.ascontiguousarray(v.reshape(-1, 128).T)


def _decay_tables():
    h = np.arange(H, dtype=np.float64)
    logg = np.log1p(-np.exp2(-5.0 - h))
    out = {}
    for C in (128, 64):
        i = np.arange(C, dtype=np.float64)
        diff = i[None, :] - i[:, None]
        dt = np.where(diff[:, None, :] >= 0, np.exp(np.maximum(diff, 0)[:, None, :] * logg[None, :, None]), 0.0)
        DT = np.zeros((128, H, C), np.float32)
        DT[:C] = dt
        qd = np.zeros((128, 4, C), np.float32)
        for m in range(4):
            for hh in range(2):
                qd[hh * 64:(hh + 1) * 64, m, :] = (RDK ** -0.5) * np.exp((i + 1.0) * logg[2 * m + hh])[None, :]
        kd = np.zeros((128, H), np.float32)
        kd[:C] = np.exp((C - 1.0 - i)[:, None] * logg[None, :])
        cd = np.zeros((128, 4), np.float32)
        for m in range(4):
            for hh in range(2):
                cd[hh * 64:(hh + 1) * 64, m] = np.exp(C * logg[2 * m + hh])
        out[C] = (DT, qd, kd, cd)
    return out


def _rope_table(pos):
    half = 32
    inv = (np.float32(10000.0) ** (-np.arange(half, dtype=np.float32) / np.float32(half))).astype(np.float32)
    ang = (pos.astype(np.float32)[:, None] * inv[None, :]).astype(np.float32)
    return np.concatenate([np.cos(ang.astype(np.float64)), np.sin(ang.astype(np.float64))], axis=1).astype(np.float32)


T_LN1G, T_LN1B, T_LN2G, T_LN2B = 0, 16, 32, 48
T_GRET, T_BRET, T_GCQ = 64, 72, 80
T_BADA = 84
T_VF = 180
T_KD128, T_KD64 = 182, 190
T_CD128, T_CD64 = 198, 202
T_EPS, T_ZERO = 206, 207
NTAB = 208


def build_program(stage=99, sub=99):
    nc = bass.Bass("TRN2", target_bir_lowering=False)

    def din(name, shape, dt=F32):
        return nc.dram_tensor(name, list(shape), dt, kind="ExternalInput").ap()

    def dout(name, shape, dt=F32):
        return nc.dram_tensor(name, list(shape), dt, kind="ExternalOutput").ap()

    def dscr(name, shape, dt=BF16):
        return nc.dram_tensor(name, list(shape), dt, kind="Internal").ap()

    xpre = din("xpre", [HALF, D])
    xown = din("xown", [HALF, D])
    xsmp = din("xsmp", [2, SSEQ, D])
    cT_d = din("cT", [128, 16, 3])
    tab_d = din("tab", [128, NTAB])
    cs_pre = din("cs_pre", [HALF, 64])
    cs_own = din("cs_own", [HALF, 64])
    cs_smp = din("cs_smp", [SSEQ, 64])
    ckv_c = din("ckv_c", [2, PAST, KVL])
    kr_c = din("kr_c", [2, PAST, ROPE])
    st_c = din("st_c", [2, 128, 4, 128])
    ident_d = din("ident", [128, 128])
    dt128_d = din("dt128", [128, H, 128], BF16)
    qd128_d = din("qd128", [128, 4, 128])
    gckv_d = din("gckv", [128, KVL])
    wada_d = din("wada", [24, 128, 8192])
    wtm_d = din("wtm", [7, 128, 8192])
    wrg_d = din("wrg", [2, 128, 8192])
    wmla_d = din("wmla", [4, 128, 4096])
    wout_d = din("wout", [4, 128, 8192])
    wgu_d = din("wgu", [22, 128, 8192])
    wd_d = din("wd", [16, 128, 5632])

    y_own = dout("y_own", [HALF, D])
    y_smp = dout("y_smp", [2, SSEQ, D])
    ckv_own = dout("ckv_own", [HALF, KVL])
    kr_own = dout("kr_own", [HALF, ROPE])
    st_own = dout("st_own", [128, 4, 128])
    ckv_so = dout("ckv_so", [2, SSEQ, KVL])
    kr_so = dout("kr_so", [2, SSEQ, ROPE])
    st_so = dout("st_so", [2, 128, 4, 128])

    wtm_s = dscr("wtm_s", [7, 128, 8192])
    wrg_s = dscr("wrg_s", [2, 128, 8192])
    wmla_s = dscr("wmla_s", [4, 128, 4096])
    wout_s = dscr("wout_s", [4, 128, 8192])
    wgu_s = dscr("wgu_s", [22, 128, 8192])
    wd_s = dscr("wd_s", [16, 128, 5632])
    NKB = 17
    kscr = dscr("kscr", [NKB, 128, H, TBP])
    vscr = dscr("vscr", [NKB, 128, H, 4, 128])

    with ExitStack() as st:
        S = Sched(nc, st)
        nalloc = [0]

        def T(name, shape, dt):
            return st.enter_context(nc.sbuf_tensor("s_" + name, list(shape), dt))

        x_stage = [T(f"x_stage{i}", [128, D], F32) for i in range(2)]
        b_xs = [S.buf(f"x_stage{i}", dma=True) for i in range(2)]

        xaT = T("xaT", [128, 16, TBP], F32); b_xaT = [S.buf(f"xaT{i}") for i in range(16)]
        actb = T("actb", [128, 16, TBP], BF16); b_hT = [S.buf(f"hT{i}") for i in range(16)]
        U = T("U", [128, 22528], BF16)
        uo = [0]

        def carve(n_elems, shape):
            a = U[:, uo[0]:uo[0] + n_elems]
            uo[0] += n_elems
            return a

        kdec_tok = U[:, 0:2048].rearrange("p (t c) -> p t c", t=4)
        v_tok = U[:, 2048:6144].rearrange("p (t c) -> p t c", t=4)
        qT_ret = U[:, 6144:8192].rearrange("p (m c) -> p m c", m=4)
        kT_ret = U[:, 8192:10240].rearrange("p (m c) -> p m c", m=4)
        qdT_ret = U[:, 10240:12288].rearrange("p (m c) -> p m c", m=4)
        rgT = U[:, 12288:16384].rearrange("p (m c) -> p m c", m=8)
        cqnT = U[:, 16384:18432].rearrange("p (m c) -> p m c", m=4)
        ckvT = U[:, 18432:20480].rearrange("p (m c) -> p m c", m=4)
        knT_blk = U[:, 0:4096].rearrange("p (h c) -> p h c", h=8)
        v_blk = U[:, 4096:8192].rearrange("p (h t e) -> p h t e", h=8, t=4)
        qnT = U[:, 0:4096].rearrange("p (h c) -> p h c", h=8)
        qrT = U[:, 4096:8192].rearrange("p (h c) -> p h c", h=8)
        actT = U[:, 0:44 * 512].rearrange("p (m c) -> p m c", m=44)
        b_U = S.buf("U")
        b_kdec = S.buf("kdec_tok"); b_vtok = S.buf("v_tok"); b_qT = S.buf("qT_ret"); b_kT = S.buf("kT_ret")
        b_qdT = S.buf("qdT_ret"); b_rg = S.buf("rgT"); b_cqn = S.buf("cqnT"); b_ckvT = S.buf("ckvT")
        b_knb = S.buf("knT_blk", dma=True); b_vb = S.buf("v_blk", dma=True)
        b_qn = S.buf("qnT"); b_qr = S.buf("qrT"); b_act = [S.buf(f"actT{i}") for i in range(44)]; b_actw = S.buf("actT_all")
        retb = [b_kdec, b_vtok, b_qT, b_kT, b_qdT]
        ag = [[b_knb, b_vb], [b_kdec, b_vtok, b_qT], [b_qn, b_qr]]
        for gi_, g_ in enumerate(ag):
            for x_ in g_:
                for gj_, h_ in enumerate(ag):
                    if gi_ != gj_:
                        x_.aliases.extend(h_)
        allA = retb + [b_rg, b_cqn, b_ckvT, b_knb, b_vb, b_qn, b_qr]
        b_actw.aliases = list(allA)
        for a_ in allA:
            a_.aliases.append(b_actw)

        krT_all = T("krT_all", [128, 8192 + 128], BF16); b_krT = S.buf("krT_all")
        WR = 2
        w_ring = [T(f"w_ring{i}", [128, 8192], BF16) for i in range(WR)]
        WS = Stream(S, w_ring, "w_ring")
        KR = 3
        kv_ring = [T(f"kv_ring{i}", [128, 1024], BF16) for i in range(KR)]
        KS = Stream(S, kv_ring, "kv_ring", slack=1)
        xa_bf = xaT[:, :, :].rearrange("p k t -> p (k t)").bitcast(BF16)
        ada_ring = [xa_bf[:, 0:8192], xa_bf[:, 8192:16384], U[:, 8192:16384]]
        AS = Stream(S, ada_ring, "ada_ring")
        for q_ in b_xaT:
            q_.aliases.extend(AS.bufs[0:2])
        for q_ in (b_kT, b_qdT, b_rg):
            q_.aliases.append(AS.bufs[2])
        pT = [T(f"pT{i}", [128, 2, 512], BF16) for i in range(2)]
        b_pT = [S.buf(f"pT{i}") for i in range(2)]
        NTMP = 3
        tmpf = [T(f"tmpf{i}", [128, 512], F32) for i in range(NTMP)]
        b_tmpf = [S.buf(f"tmpf{i}") for i in range(NTMP)]
        NTB = 4
        tmpb = [T(f"tmpb{i}", [128, 512], BF16) for i in range(NTB)]
        b_tmpb = [S.buf(f"tmpb{i}") for i in range(NTB)]
        mean_sb = T("mean_sb", [128, 512], F32); b_mean = S.buf("mean_sb")
        rstd_sb = T("rstd_sb", [128, 512], F32); b_rstd = S.buf("rstd_sb")
        ckv_stg = [T(f"ckv_stg{i}", [128, 512], F32) for i in range(1)]
        b_ckv_stg = [S.buf(f"ckv_stg{i}", dma=True) for i in range(1)]
        kr_stg = [T(f"kr_stg{i}", [128, 64], F32) for i in range(1)]
        b_kr_stg = [S.buf(f"kr_stg{i}", dma=True) for i in range(1)]
        tok_b = [T(f"tok_b{i}", [128, 512], BF16) for i in range(2)]
        b_tok_b = [S.buf(f"tok_b{i}") for i in range(2)]
        sstat = T("sstat", [128, 8], F32); b_sstat = S.buf("sstat")
        cs_blk = [T(f"cs_blk{i}", [128, 4, 64], F32) for i in range(1)]
        b_csb = [S.buf(f"cs_blk{i}", dma=True) for i in range(1)]
        cur_cs = [0]
        cstage_raw = T("cstage_raw", [128, 2304], BF16)
        cache_stage = [cstage_raw[:, :].rearrange("p (t c) -> p t c", t=4),
                       x_stage[1][:, :].bitcast(BF16)[:, 0:2304].rearrange("p (t c) -> p t c", t=4)]
        b_cstage = [S.buf("cache_stage0", dma=True), b_xs[1]]
        ystg_all = cstage_raw[:, 0:2048].bitcast(F32)
        stg = [ystg_all[:, 0:512], ystg_all[:, 512:1024]]
        b_stg = [S.buf(f"ystg{i}", dma=True) for i in range(2)]
        for q_ in b_stg:
            q_.aliases.append(b_cstage[0])
            b_cstage[0].aliases.append(q_)
        S_f = T("S_f", [128, 4, 128], F32); b_Sf = S.buf("S_f", dma=True)
        S_bf = T("S_bf", [128, 5, 4, 128], BF16); b_Sbf = [S.buf(f"S_bf{i}") for i in range(5)]
        tab = T("tab", [128, NTAB], F32); b_const = S.buf("const", dma=True)
        ident = T("ident", [128, 128], F32)
        identb = T("identb", [128, 128], BF16)
        ones_b = T("ones_b", [128, 128], BF16)
        dt128 = T("dt128", [128, H, 128], BF16)
        qd128 = T("qd128", [128, 4, 128], F32)
        gckv = T("gckv", [128, KVL], F32)
        cT = mean_sb[:, 0:48].rearrange("p (k s) -> p k s", s=3)
        cTb = T("cTb", [128, 16, 3], BF16)
        modT = T("modT", [128, 96, 3], F32); b_mod = S.buf("modT")
        SC1P = T("SC1P", [128, 3, 16], F32)
        G2T = T("G2T", [128, 3, 16], F32)
        B2T = T("B2T", [128, 3, 16], F32)
        AG1 = T("AG1", [128, 16], F32)
        AB1 = T("AB1", [128, 16], F32)
        b_seqtab = S.buf("seqtab")

        def P(name):
            return st.enter_context(nc.psum_tensor(name, [128, 1024], F32))

        DB = [P(f"DB{i}") for i in range(4)]
        b_DB = [[S.buf(f"DB{i}_{j}") for j in range(2)] for i in range(4)]
        for r_ in b_DB:
            for q_ in r_:
                q_.excl = True

        def bank(i, j):
            return DB[i][:, j * 512:(j + 1) * 512]

        mmrot = [0]

        ROT = [(0, 0), (0, 1), (1, 0), (1, 1), (3, 0), (3, 1), (2, 0), (2, 1)]
        rotN = [4]

        def next_bank():
            r = mmrot[0] % rotN[0]
            mmrot[0] = (r + 1) % rotN[0]
            return ROT[r]

        tmpi = {}

        def rot(idx, n):
            v = tmpi.get((idx, n), 0)
            tmpi[(idx, n)] = (v + 1) % n
            return v

        b_wscr = {}
        b_kscr = [S.buf(f"kscr{i}") for i in range(NKB)]
        b_vscr = [S.buf(f"vscr{i}") for i in range(NKB)]
        b_out = S.buf("outputs")
        d2d_sems = [S.dma_sem("d2dA"), S.dma_sem("d2dB"), S.dma_sem("d2dC")]

        def cload(dst, src):
            S.dma(SP, C("dma_start", out=dst, in_=src), b_const.dsem, writes=[b_const])

        cload(tab[:], tab_d)
        S.dma(SP, C("dma_start", out=cT, in_=cT_d), b_const.dsem, writes=[b_const, b_mean])
        cload(ident[:], ident_d)
        cload(dt128[:], dt128_d)
        cload(qd128[:], qd128_d)
        cload(gckv[:], gckv_d)
        b_c2 = S.buf("const2")
        S.op(DVE, C("tensor_copy", out=identb[:], in_=ident[:]), reads=[b_const], writes=[b_c2])
        S.op(DVE, C("memset", ones_b[:], 1.0), writes=[b_c2])
        S.op(ACT, C("activation", out=cTb[:], in_=cT, func=AF.Silu), reads=[b_const, b_mean], writes=[b_c2])
        CONST = [b_const, b_c2]

        def slab_from_scratch(scr, key, i, L):
            def load(tile, buf, swsem):
                S.dma(SP, C("dma_start", out=tile[:, 0:L], in_=scr[i]), buf.dsem,
                      reads=[b_wscr[(key, i)]], writes=[buf])
            return load

        def slab_ada(i):
            def load(tile, buf, swsem):
                S.dma(POOL, C("dma_start", out=tile[:, :], in_=wada_d[i]), swsem, writes=[buf])
            return load

        groups = {"tm": (wtm_d, wtm_s, 7, 8192), "rg": (wrg_d, wrg_s, 2, 8192), "mla": (wmla_d, wmla_s, 4, 4096),
                  "out": (wout_d, wout_s, 4, 8192), "gu": (wgu_d, wgu_s, 22, 8192), "d": (wd_d, wd_s, 16, 5632)}

        conv_groups = [[], [], []]

        def convert(key, idxs, grp):
            src, dst, n, L = groups[key]
            for i in idxs:
                b_wscr[(key, i)] = S.buf(f"wscr_{key}{i}")
                S.dma(POOL, C("dma_start", out=dst[i], in_=src[i]), d2d_sems[grp], writes=[b_wscr[(key, i)]])
                conv_groups[grp].append(b_wscr[(key, i)])

        def seal(grp):
            for b_ in conv_groups[grp]:
                b_.w = (d2d_sems[grp].key, d2d_sems[grp].cnt)

        def L_(key, i):
            return slab_from_scratch(groups[key][1], key, i, groups[key][3])

        PREFIX_SLABS = [("tm", 5), ("tm", 6), ("mla", 0), ("mla", 1), ("tm", 1), ("tm", 2), ("tm", 3)]
        MAIN_SLABS = ([("tm", 4), ("tm", 5), ("tm", 6), ("mla", 0), ("mla", 1), ("tm", 0), ("tm", 1), ("tm", 2), ("tm", 3),
                       ("rg", 0), ("rg", 1), ("mla", 2), ("mla", 3)]
                      + [("out", i) for i in range(4)] + [("gu", i) for i in range(22)] + [("d", i) for i in range(16)])
        convert("tm", [5, 6], 0)
        convert("mla", [0, 1], 0)
        convert("tm", [1, 2, 3], 0)
        seal(0)
        AS.extend([slab_ada(i) for i in range(24)])

        def ada_part(f0, f1):
            S.tag = "ada"
            bk = (3, 1)
            for sl in range(f0 // 4, f1 // 4):
                wt, wb = AS.next()
                for mm in range(4):
                    f = sl * 4 + mm
                    for k in range(16):
                        S.op(PE, C("matmul",
                            bank(*bk)[:, f * 3:(f + 1) * 3], lhsT=wt[:, (mm * 16 + k) * 128:(mm * 16 + k + 1) * 128],
                            rhs=cTb[:, k, :], start=(k == 0), stop=(k == 15)),
                            reads=[wb] + CONST, writes=[b_DB[3][1]], signal=(k == 15))
            S.op(DVE, C("tensor_tensor",
                out=modT[:, f0:f1, :], in0=bank(*bk)[:, f0 * 3:f1 * 3].rearrange("p (f s) -> p f s", s=3),
                in1=tab[:, T_BADA + f0:T_BADA + f1].unsqueeze(2).to_broadcast([128, f1 - f0, 3]), op=ALU.add),
                reads=[b_DB[3][1]] + CONST, writes=[b_mod])

        ada_part(0, 32)
        S.op(DVE, C("tensor_scalar", out=SC1P[:].rearrange("p s f -> p f s"), in0=modT[:, 16:32, :], scalar1=1.0, scalar2=None, op0=ALU.add),
             reads=[b_mod], writes=[b_seqtab])
        conv_pending = {1: [("tm", 4), ("tm", 0), ("rg", 0), ("rg", 1), ("mla", 2), ("mla", 3)] + [("out", i) for i in range(4)],
                        2: [("gu", i) for i in range(22)] + [("d", i) for i in range(16)]}

        def convert_some(n, grp=1):
            lst = conv_pending[grp]
            if not lst:
                return
            for _ in range(n):
                if lst:
                    k_, i_ = lst.pop(0)
                    convert(k_, [i_], grp)
            if not lst:
                seal(grp)

        def mm_group(out_ap, obuf, pairs, reads):
            n = len(pairs)
            for i, (l, r) in enumerate(pairs):
                S.op(PE, C("matmul", out_ap, lhsT=l, rhs=r, start=(i == 0), stop=(i == n - 1)),
                     reads=reads, writes=[obuf], signal=(i == n - 1))

        def rope(src, G, TT, cs, b_cs_, reads, outs):
            s3 = src.rearrange("p (g c) -> p g c", c=64)
            x1 = s3[:, :, 0:32]
            x2 = s3[:, :, 32:64]
            cosb = cs[0:TT, 0:32].unsqueeze(1).to_broadcast([TT, G, 32])
            sinb = cs[0:TT, 32:64].unsqueeze(1).to_broadcast([TT, G, 32])
            ia, ib = rot(0, NTMP), rot(0, NTMP)
            ta = tmpf[ia][0:TT, 0:G * 32].rearrange("p (g c) -> p g c", c=32)
            tb = tmpf[ia][0:TT, 256:256 + G * 32].rearrange("p (g c) -> p g c", c=32)
            tc = tmpf[ib][0:TT, 0:G * 32].rearrange("p (g c) -> p g c", c=32)
            td = tmpf[ib][0:TT, 256:256 + G * 32].rearrange("p (g c) -> p g c", c=32)
            rd = reads + [b_cs_]
            S.op(DVE, C("tensor_tensor", out=ta, in0=x1, in1=cosb, op=ALU.mult), reads=rd, writes=[b_tmpf[ia]])
            S.op(DVE, C("tensor_tensor", out=tb, in0=x2, in1=sinb, op=ALU.mult), reads=rd, writes=[b_tmpf[ia]])
            S.op(DVE, C("tensor_tensor", out=tc, in0=x2, in1=cosb, op=ALU.mult), reads=rd, writes=[b_tmpf[ib]])
            S.op(DVE, C("tensor_tensor", out=td, in0=x1, in1=sinb, op=ALU.mult), reads=rd, writes=[b_tmpf[ib]])
            for half, (pa, pb_, opc, bufx) in enumerate(((ta, tb, ALU.subtract, b_tmpf[ia]), (tc, td, ALU.add, b_tmpf[ib]))):
                for o_ in outs:
                    if len(o_) == 2:
                        dst, dbuf = o_
                        d3 = dst.rearrange("p (g c) -> p g c", c=64)[:, :, half * 32:(half + 1) * 32]
                        S.op(DVE, C("tensor_tensor", out=d3, in0=pa, in1=pb_, op=opc), reads=[bufx], writes=[dbuf])
                    else:
                        dfn, g0, g1, dbuf = o_
                        S.op(DVE, C("tensor_tensor", out=dfn(half), in0=pa[:, g0:g1, :], in1=pb_[:, g0:g1, :], op=opc), reads=[bufx], writes=[dbuf])

        def transposes_bf(src_tile, src_buf, TT, nblk, dst_fn, dst_bufs, eng_fn):
            bi = next_bank()
            pb = bank(*bi).bitcast(BF16)[:, 0:nblk * TT].rearrange("p (n t) -> p n t", t=TT)
            for n in range(nblk):
                S.op(PE, C("transpose", pb[:, n, :], src_tile[0:TT, n * 128:(n + 1) * 128], identb[0:TT, 0:TT]),
                     reads=[src_buf] + CONST, writes=[b_DB[bi[0]][bi[1]]], signal=(n == nblk - 1))
            eng_fn(pb, b_DB[bi[0]][bi[1]])

        def rmsnorm_rstd(ps, pbuf, TT):
            j = rot(1, NTMP)
            c = rot(2, 4)
            S.op(ACT, C("activation", out=tmpf[j][0:TT, :], in_=ps, func=AF.Square),
                 reads=[pbuf], writes=[b_tmpf[j]])
            S.op(DVE, C("reduce_sum", out=sstat[0:TT, 2 * c:2 * c + 1], in_=tmpf[j][0:TT, :], axis=mybir.AxisListType.X),
                 reads=[b_tmpf[j]], writes=[b_sstat])
            S.op(ACT, C("activation", out=sstat[0:TT, 2 * c + 1:2 * c + 2], in_=sstat[0:TT, 2 * c:2 * c + 1], func=AF.Sqrt,
                                             scale=1.0 / 512.0, bias=tab[0:TT, T_EPS:T_EPS + 1]),
                 reads=[b_sstat] + CONST, writes=[b_sstat])
            S.op(DVE, C("reciprocal", out=sstat[0:TT, 2 * c + 1:2 * c + 2], in_=sstat[0:TT, 2 * c + 1:2 * c + 2]),
                 reads=[b_sstat], writes=[b_sstat])
            return sstat[0:TT, 2 * c + 1:2 * c + 2]

        x_pref = [None]

        def cache_load(ci, ckv_src, kr_src):
            S.dma(POOL, C("dma_start", out=cache_stage[ci][:, :, 0:512], in_=ckv_src.rearrange("(t p) c -> p t c", p=128)), b_cstage[ci].dsem,
                  writes=[b_cstage[ci]])
            S.dma(POOL, C("dma_start", out=cache_stage[ci][:, :, 512:576], in_=kr_src.rearrange("(t p) c -> p t c", p=128)), b_cstage[ci].dsem,
                  writes=[b_cstage[ci]], skip_own=True)

        def front(full, TT, NT, seq, x_src, cs_rows, key_col0, kblk, C_tabs, ckv_out=None, kr_out=None,
                  cache=None, blk_id=None, next_x=None):
            TB = TT * NT
            DT, QD, KDc, CDc = C_tabs

            def kvgen():
                S.tag = "kvgen"
                kvgen_impl(TT, NT, kblk)
                S.tag = "front_win"
            S.tag = "front_x"
            rotN[0] = 6
            if cache is None:
                cur_cs[0] = 0
                cb = cur_cs[0]
                S.dma(POOL, C("dma_start", out=cs_blk[cb][0:TT, 0:NT, :], in_=cs_rows.rearrange("(t p) c -> p t c", p=TT)), b_csb[cb].dsem,
                      writes=[b_csb[cb]])
                for t in range(NT):
                    sx = t % 2
                    if not (x_pref[0] is not None and x_pref[0] == blk_id and t < 2):
                        S.dma(POOL, C("dma_start", out=x_stage[sx][0:TT, :], in_=x_src(t)), b_xs[sx].dsem,
                              writes=[b_xs[sx]])
                    for g in range(4):
                        bi = next_bank()
                        pb = bank(*bi)[:, 0:4 * TT].rearrange("p (k t) -> p k t", t=TT)
                        for kk in range(4):
                            k = g * 4 + kk
                            S.op(PE, C("transpose", pb[:, kk, :], x_stage[sx][0:TT, k * 128:(k + 1) * 128], ident[0:TT, 0:TT]),
                                 reads=[b_xs[sx]] + CONST, writes=[b_DB[bi[0]][bi[1]]], signal=(kk == 3))
                        pbuf = b_DB[bi[0]][bi[1]]
                        if full:
                            S.op(ACT, C("activation", out=xaT[:, g * 4:g * 4 + 4, t * TT:(t + 1) * TT], in_=pb, func=AF.Identity, scale=ALPHA),
                                 reads=[pbuf], writes=b_xaT[g * 4:g * 4 + 4])
                        j = rot(1, NTMP)
                        tv = tmpf[j][:, 0:4 * TT].rearrange("p (k t) -> p k t", t=TT)
                        S.op(DVE, C("tensor_tensor", out=tv, in0=pb, in1=SC1P[:, seq, g * 4:g * 4 + 4].unsqueeze(2).to_broadcast([128, 4, TT]), op=ALU.mult),
                             reads=[pbuf, b_seqtab], writes=[b_tmpf[j]])
                        S.op(DVE, C("tensor_tensor", out=actb[:, g * 4:g * 4 + 4, t * TT:(t + 1) * TT], in0=tv,
                                                                              in1=modT[:, g * 4:g * 4 + 4, seq].unsqueeze(2).to_broadcast([128, 4, TT]), op=ALU.add),
                             reads=[b_tmpf[j], b_mod], writes=b_hT[g * 4:g * 4 + 4])
                x_pref[0] = None
                if next_x is not None:
                    nid, nsrc, nTT, nNT = next_x
                    for t in range(min(2, nNT)):
                        S.dma(POOL, C("dma_start", out=x_stage[t][0:nTT, :], in_=nsrc(t)), b_xs[t].dsem, writes=[b_xs[t]])
                    x_pref[0] = nid
                S.tag = "front_win"
                chunks = [4, 5, 6, "kv", 0, 1, 2, 3] if full else [5, 6, "kv", 1, 2, 3]
                if full and sub < 0:
                    chunks = chunks[:-sub - 1]
                for c in chunks:
                    if c == "kv":
                        kvgen()
                        continue
                    wt, wb = WS.next()
                    ncol = 64 if c == 6 else 512
                    w3 = wt[:, 0:16 * ncol].rearrange("p (k n) -> p k n", n=ncol)
                    pend = []
                    for t in range(NT):
                        bi = next_bank()
                        ps = bank(*bi)[0:TT, 0:ncol]
                        pbuf = b_DB[bi[0]][bi[1]]
                        mm_group(ps, pbuf, [(actb[:, k, t * TT:(t + 1) * TT], w3[:, k, :]) for k in range(16)], b_hT + [wb])
                        def post(t=t, ps=ps, pbuf=pbuf, c=c):
                            cst = cs_blk[cb][:, t, :]
                            if c == 0:
                                j = rot(3, 2)
                                rope(ps, 8, TT, cst, b_csb[cb], [pbuf], [(tok_b[j][0:TT, :], b_tok_b[j])])

                                def ev(pb, pbb, t=t):
                                    S.op(ACT, C("activation", out=qT_ret[:, :, t * TT:(t + 1) * TT], in_=pb, func=AF.Identity, scale=RDK ** -0.5),
                                         reads=[pbb], writes=[b_qT])
                                    S.op(DVE, C("tensor_tensor", out=qdT_ret[:, :, t * TT:(t + 1) * TT], in0=pb, in1=QD[:, :, 0:TT], op=ALU.mult),
                                         reads=[pbb] + CONST, writes=[b_qdT])
                                transposes_bf(tok_b[j], b_tok_b[j], TT, 4, None, None, ev)
                            elif c == 1:
                                j = rot(3, 2)
                                rope(ps, 8, TT, cst, b_csb[cb], [pbuf], [(tok_b[j][0:TT, :], b_tok_b[j])])
                                S.op(DVE, C("tensor_tensor",
                                    out=kdec_tok[0:TT, t, :].rearrange("p (h c) -> p h c", c=64), in0=tok_b[j][0:TT, :].rearrange("p (h c) -> p h c", c=64),
                                    in1=tab[0:TT, KDc:KDc + 8].unsqueeze(2).to_broadcast([TT, 8, 64]), op=ALU.mult),
                                    reads=[b_tok_b[j]] + CONST, writes=[b_kdec])
                                if full:
                                    def ev(pb, pbb, t=t):
                                        S.op(ACT, C("activation", out=kT_ret[:, :, t * TT:(t + 1) * TT], in_=pb, func=AF.Copy),
                                             reads=[pbb], writes=[b_kT])
                                    transposes_bf(tok_b[j], b_tok_b[j], TT, 4, None, None, ev)
                            elif c in (2, 3):
                                S.op(ACT, C("activation", out=v_tok[0:TT, t, (c - 2) * 512:(c - 1) * 512], in_=ps, func=AF.Copy),
                                     reads=[pbuf], writes=[b_vtok])
                                if c == 3:
                                    state_step(t, TT, CDc)
                            elif c == 4:
                                rs = rmsnorm_rstd(ps, pbuf, TT)
                                j = rot(3, 2)
                                S.op(ACT, C("activation", out=tok_b[j][0:TT, :], in_=ps, func=AF.Identity, scale=rs),
                                     reads=[pbuf, b_sstat], writes=[b_tok_b[j]])

                                def ev(pb, pbb, t=t):
                                    for kk in range(4):
                                        S.op(ACT, C("activation", out=cqnT[:, kk, t * TT:(t + 1) * TT], in_=pb[:, kk, :], func=AF.Identity,
                                                                                scale=tab[:, T_GCQ + kk:T_GCQ + kk + 1]),
                                             reads=[pbb] + CONST, writes=[b_cqn])
                                transposes_bf(tok_b[j], b_tok_b[j], TT, 4, None, None, ev)
                            elif c == 5:
                                rs = rmsnorm_rstd(ps, pbuf, TT)
                                si = 0
                                S.op(DVE, C("scalar_tensor_tensor", out=ckv_stg[si][0:TT, :], in0=ps, scalar=rs, in1=gckv[0:TT, :], op0=ALU.mult, op1=ALU.mult),
                                     reads=[pbuf, b_sstat] + CONST, writes=[b_ckv_stg[si]])
                                j = rot(3, 2)
                                S.op(ACT, C("activation", out=tok_b[j][0:TT, :], in_=ckv_stg[si][0:TT, :], func=AF.Copy),
                                     reads=[b_ckv_stg[si]], writes=[b_tok_b[j]])
                                if ckv_out is not None:
                                    S.dma(POOL, C("dma_start", out=ckv_out(t), in_=ckv_stg[si][0:TT, :]), b_ckv_stg[si].dsem,
                                          reads=[b_ckv_stg[si]], writes=[b_out])

                                def ev(pb, pbb, t=t):
                                    S.op(ACT, C("activation", out=ckvT[:, :, t * TT:(t + 1) * TT], in_=pb, func=AF.Copy), reads=[pbb], writes=[b_ckvT])
                                transposes_bf(tok_b[j], b_tok_b[j], TT, 4, None, None, ev)
                            else:
                                si = 0
                                j = rot(3, 2)
                                rope(ps, 1, TT, cst, b_csb[cb], [pbuf],
                                     [(kr_stg[si][0:TT, :], b_kr_stg[si]), (tok_b[j][0:TT, 0:64], b_tok_b[j]), (tok_b[j][0:TT, 64:128], b_tok_b[j])])
                                if kr_out is not None:
                                    S.dma(POOL, C("dma_start", out=kr_out(t), in_=kr_stg[si][0:TT, :]), b_kr_stg[si].dsem,
                                          reads=[b_kr_stg[si]], writes=[b_out])

                                def ev(pb, pbb, t=t):
                                    S.op(ACT, C("activation", out=krT_all[:, key_col0 + t * TT:key_col0 + (t + 1) * TT], in_=pb[:, 0, :], func=AF.Copy),
                                         reads=[pbb], writes=[b_krT])
                                transposes_bf(tok_b[j], b_tok_b[j], TT, 1, None, None, ev)
                        pend.append(post)
                        if len(pend) > 1:
                            pend.pop(0)()
                    while pend:
                        pend.pop(0)()
                S.tag = "front_rg"
                if full and sub >= 0:
                    for sl in range(2):
                        wt, wb = WS.next()
                        for mm in range(4):
                            m = sl * 4 + mm
                            bi = next_bank()
                            ps = bank(*bi)[:, 0:TB]
                            pbuf = b_DB[bi[0]][bi[1]]
                            mm_group(ps, pbuf, [(wt[:, (mm * 16 + k) * 128:(mm * 16 + k + 1) * 128], actb[:, k, 0:TB]) for k in range(16)], b_hT + [wb])
                            S.op(ACT, C("activation", out=rgT[:, m, 0:TB], in_=ps, func=AF.Silu), reads=[pbuf], writes=[b_rg])
            else:
                ci = cache[2]
                for t in range(4):
                    def ev(pb, pbb, t=t):
                        S.op(ACT, C("activation", out=ckvT[:, :, t * 128:(t + 1) * 128], in_=pb, func=AF.Copy), reads=[pbb], writes=[b_ckvT])
                    transposes_bf(cache_stage[ci][:, t, 0:512], b_cstage[ci], 128, 4, None, None, ev)
                    j = rot(3, 2)
                    S.op(DVE, C("tensor_copy", out=tok_b[j][:, 0:64], in_=cache_stage[ci][:, t, 512:576]), reads=[b_cstage[ci]], writes=[b_tok_b[j]])
                    S.op(DVE, C("tensor_copy", out=tok_b[j][:, 64:128], in_=cache_stage[ci][:, t, 512:576]), reads=[b_cstage[ci]], writes=[b_tok_b[j]])

                    def ev2(pb, pbb, t=t):
                        S.op(ACT, C("activation", out=krT_all[:, key_col0 + t * 128:key_col0 + (t + 1) * 128], in_=pb[:, 0, :], func=AF.Copy),
                             reads=[pbb], writes=[b_krT])
                    transposes_bf(tok_b[j], b_tok_b[j], 128, 1, None, None, ev2)
            if cache is not None:
                kvgen()
            rotN[0] = 4
            mmrot[0] = 0

        def kvgen_impl(TT, NT, kblk):
            TB = TT * NT
            wt, wb = WS.next()
            for h in range(H):
                bi = next_bank()
                ps = bank(*bi)[:, 0:TB]
                pbuf = b_DB[bi[0]][bi[1]]
                mm_group(ps, pbuf, [(wt[:, (h * 4 + kk) * 128:(h * 4 + kk + 1) * 128], ckvT[:, kk, 0:TB]) for kk in range(4)], [b_ckvT, wb])
                eng = ACT if h % 2 == 0 else DVE
                if eng == ACT:
                    S.op(ACT, C("activation", out=knT_blk[:, h, 0:TB], in_=ps, func=AF.Copy), reads=[pbuf], writes=[b_knb])
                else:
                    S.op(DVE, C("tensor_copy", out=knT_blk[:, h, 0:TB], in_=ps), reads=[pbuf], writes=[b_knb])
            wt, wb = WS.next()
            w3 = wt[:, 0:4096].rearrange("p (k n) -> p k n", n=1024)
            for t in range(NT):
                for c in range(2):
                    bi = next_bank()
                    ps = bank(*bi)[0:TT, :]
                    pbuf = b_DB[bi[0]][bi[1]]
                    mm_group(ps, pbuf, [(ckvT[:, kk, t * TT:(t + 1) * TT], w3[:, kk, c * 512:(c + 1) * 512]) for kk in range(4)], [b_ckvT, wb])
                    dst = v_blk[0:TT, c * 4:c * 4 + 4, t, :]
                    src = ps.rearrange("p (h e) -> p h e", e=128)
                    if c == 0:
                        S.op(ACT, C("activation", out=dst, in_=src, func=AF.Copy), reads=[pbuf], writes=[b_vb])
                    else:
                        S.op(DVE, C("tensor_copy", out=dst, in_=src), reads=[pbuf], writes=[b_vb])
            S.dma(POOL, C("dma_start", out=kscr[kblk][:, :, 0:TB], in_=knT_blk[:, :, 0:TB]), b_knb.dsem, reads=[b_knb], writes=[b_kscr[kblk]])
            S.dma(POOL, C("dma_start", out=vscr[kblk][0:TT, :, 0:NT, :], in_=v_blk[0:TT, :, 0:NT, :]), b_vb.dsem, reads=[b_vb], writes=[b_vscr[kblk]])

        def state_step(t, TT, CDc):
            kvb = (2, 0)
            for m in range(4):
                hb = b_DB[2][m // 2]
                out = DB[2][:, m * 256:(m + 1) * 256]
                S.op(PE, C("matmul", out, lhsT=kdec_tok[0:TT, t, m * 128:(m + 1) * 128], rhs=v_tok[0:TT, t, m * 256:(m + 1) * 256], start=True, stop=True),
                     reads=[b_kdec, b_vtok], writes=[hb])
            kv3 = DB[2][:, :].rearrange("p (m c) -> p m c", c=256)
            S.op(DVE, C("tensor_tensor", out=S_f[:], in0=S_f[:], in1=tab[:, CDc:CDc + 4].unsqueeze(2).to_broadcast([128, 4, 128]), op=ALU.mult),
                 reads=[b_Sf] + CONST, writes=[b_Sf])
            S.op(DVE, C("tensor_tensor", out=S_f[0:64], in0=S_f[0:64], in1=kv3[0:64, :, 0:128], op=ALU.add),
                 reads=[b_Sf, b_DB[2][0], b_DB[2][1]], writes=[b_Sf])
            S.op(DVE, C("tensor_tensor", out=S_f[64:128], in0=S_f[64:128], in1=kv3[64:128, :, 128:256], op=ALU.add),
                 reads=[b_Sf, b_DB[2][0], b_DB[2][1]], writes=[b_Sf])
            S.op(POOL, C("tensor_copy", out=S_bf[:, t + 1], in_=S_f[:]), reads=[b_Sf], writes=[b_Sbf[t + 1]])

        def carry_state(NT):
            S.op(POOL, C("tensor_copy", out=S_bf[:, 0], in_=S_f[:]), reads=[b_Sf], writes=[b_Sbf[0]])

        def retention(TT, NT, C_tabs):
            S.tag = "retention"
            TB = TT * NT
            DT, QD, KDc, CDc = C_tabs
            ob = [(3, 0), (3, 1)]

            def emit_st(m):
                if mmrot[0] % 2:
                    next_bank()
                sb0 = next_bank()
                next_bank()
                dbi = sb0[0]
                pr = m % 2
                for t in range(NT):
                    for hh in range(2):
                        S.op(PE, C("matmul", DB[dbi][0:TT, hh * 512 + t * TT:hh * 512 + (t + 1) * TT],
                                   lhsT=kT_ret[hh * 64:(hh + 1) * 64, m, t * TT:(t + 1) * TT],
                                   rhs=qT_ret[hh * 64:(hh + 1) * 64, m, t * TT:(t + 1) * TT], start=True, stop=True),
                             reads=[b_kT, b_qT], writes=[b_DB[dbi][hh]])
                sps = DB[dbi][0:TT, :].rearrange("p (h c) -> p h c", c=512)[:, :, 0:TB].rearrange("p h (t i) -> p h t i", i=TT)
                dtb = DT[0:TT, 2 * m:2 * m + 2, 0:TT].unsqueeze(2).to_broadcast([TT, 2, NT, TT])
                outp = pT[pr][0:TT, :, 0:TB].rearrange("p h (t i) -> p h t i", i=TT)
                S.op(DVE, C("tensor_tensor", out=outp, in0=sps, in1=dtb, op=ALU.mult),
                     reads=[b_DB[dbi][0], b_DB[dbi][1]] + CONST, writes=[b_pT[pr]])

            def emit_o(m):
                pr = m % 2
                for t in range(NT):
                    for hh in range(2):
                        h = 2 * m + hh
                        o_ap = bank(*ob[hh])[:, t * TT:(t + 1) * TT]
                        obuf = b_DB[ob[hh][0]][ob[hh][1]]
                        S.op(PE, C("matmul", o_ap, lhsT=v_tok[0:TT, t, h * 128:(h + 1) * 128], rhs=pT[pr][0:TT, hh, t * TT:(t + 1) * TT], start=True, stop=False),
                             reads=[b_vtok, b_pT[pr]], writes=[obuf], signal=(TT < 128))
                        S.op(PE, C("matmul", o_ap, lhsT=S_bf[hh * 64:(hh + 1) * 64, t, m, :], rhs=qdT_ret[hh * 64:(hh + 1) * 64, m, t * TT:(t + 1) * TT], start=False, stop=True),
                             reads=[b_Sbf[t], b_qdT], writes=[obuf], self_sync=(TT < 128))
                for hh in range(2):
                    h = 2 * m + hh
                    groupnorm_gate(bank(*ob[hh])[:, 0:TB], b_DB[ob[hh][0]][ob[hh][1]], h, TB)
                S.tag = "retention"

            emit_st(0)
            for m in range(4):
                if m + 1 < 4:
                    emit_st(m + 1)
                emit_o(m)

        def stats_finish(mean_ps, mean_buf, ex2_ps, ex2_buf, TB, inv):
            S.op(ACT, C("activation", out=mean_sb[:, 0:TB], in_=mean_ps, func=AF.Identity, scale=inv), reads=[mean_buf], writes=[b_mean])
            j = rot(1, NTMP)
            S.op(DVE, C("tensor_tensor", out=tmpf[j][:, 0:TB], in0=mean_sb[:, 0:TB], in1=mean_sb[:, 0:TB], op=ALU.mult), reads=[b_mean], writes=[b_tmpf[j]])
            S.op(DVE, C("scalar_tensor_tensor", out=tmpf[j][:, 0:TB], in0=ex2_ps, scalar=inv, in1=tmpf[j][:, 0:TB], op0=ALU.mult, op1=ALU.subtract), reads=[ex2_buf, b_tmpf[j]], writes=[b_tmpf[j]])
            S.op(ACT, C("activation", out=rstd_sb[:, 0:TB], in_=tmpf[j][:, 0:TB], func=AF.Sqrt, bias=tab[:, T_EPS:T_EPS + 1]), reads=[b_tmpf[j]] + CONST, writes=[b_rstd])
            S.op(DVE, C("reciprocal", out=rstd_sb[:, 0:TB], in_=rstd_sb[:, 0:TB]), reads=[b_rstd], writes=[b_rstd])

        def groupnorm_gate(o_ps, obuf, h, TB):
            i1, i2 = rot(7, NTB), rot(7, NTB)
            j = rot(1, NTMP)
            S.op(ACT, C("activation", out=tmpf[j][:, 0:TB], in_=o_ps, func=AF.Copy), reads=[obuf], writes=[b_tmpf[j]])
            S.op(ACT, C("activation", out=tmpb[i1][:, 0:TB], in_=tmpf[j][:, 0:TB], func=AF.Copy), reads=[b_tmpf[j]], writes=[b_tmpb[i1]])
            S.op(ACT, C("activation", out=tmpb[i2][:, 0:TB], in_=tmpf[j][:, 0:TB], func=AF.Square), reads=[b_tmpf[j]], writes=[b_tmpb[i2]])
            mb, eb = (2, 0), (2, 1)
            S.op(PE, C("matmul", bank(*mb)[:, 0:TB], lhsT=ones_b[:], rhs=tmpb[i1][:, 0:TB], start=True, stop=True), reads=[b_tmpb[i1]] + CONST, writes=[b_DB[2][0]])
            S.op(PE, C("matmul", bank(*eb)[:, 0:TB], lhsT=ones_b[:], rhs=tmpb[i2][:, 0:TB], start=True, stop=True), reads=[b_tmpb[i2]] + CONST, writes=[b_DB[2][1]])
            stats_finish(bank(*mb)[:, 0:TB], b_DB[2][0], bank(*eb)[:, 0:TB], b_DB[2][1], TB, 1.0 / 128.0)
            S.op(DVE, C("tensor_tensor", out=tmpf[j][:, 0:TB], in0=tmpf[j][:, 0:TB], in1=mean_sb[:, 0:TB], op=ALU.subtract), reads=[b_tmpf[j], b_mean], writes=[b_tmpf[j]])
            S.op(DVE, C("tensor_tensor", out=tmpf[j][:, 0:TB], in0=tmpf[j][:, 0:TB], in1=rstd_sb[:, 0:TB], op=ALU.mult), reads=[b_tmpf[j], b_rstd], writes=[b_tmpf[j]])
            S.op(ACT, C("activation", out=tmpf[j][:, 0:TB], in_=tmpf[j][:, 0:TB], func=AF.Identity, scale=tab[:, T_GRET + h:T_GRET + h + 1], bias=tab[:, T_BRET + h:T_BRET + h + 1]),
                 reads=[b_tmpf[j]] + CONST, writes=[b_tmpf[j]])
            S.op(DVE, C("tensor_tensor", out=actb[:, h, 0:TB], in0=tmpf[j][:, 0:TB], in1=rgT[:, h, 0:TB], op=ALU.mult), reads=[b_tmpf[j], b_rg], writes=[b_hT[h]])

        def qgen(TT, NT):
            S.tag = "qgen"
            TB = TT * NT
            rotN[0] = 8
            wt, wb = WS.next()
            for h in range(H):
                bi = next_bank()
                ps = bank(*bi)[:, 0:TB]
                pbuf = b_DB[bi[0]][bi[1]]
                mm_group(ps, pbuf, [(wt[:, (h * 4 + kk) * 128:(h * 4 + kk + 1) * 128], cqnT[:, kk, 0:TB]) for kk in range(4)], [b_cqn, wb])
                if h % 2 == 0:
                    S.op(ACT, C("activation", out=qnT[:, h, 0:TB], in_=ps, func=AF.Copy), reads=[pbuf], writes=[b_qn])
                else:
                    S.op(DVE, C("tensor_copy", out=qnT[:, h, 0:TB], in_=ps), reads=[pbuf], writes=[b_qn])
            wt, wb = WS.next()
            w3 = wt[:, 0:2048].rearrange("p (k n) -> p k n", n=512)
            qpend = []
            for t in range(NT):
                cb = cur_cs[0]
                cst = cs_blk[cb][:, t, :]
                bi = next_bank()
                ps = bank(*bi)[0:TT, :]
                pbuf = b_DB[bi[0]][bi[1]]
                mm_group(ps, pbuf, [(cqnT[:, kk, t * TT:(t + 1) * TT], w3[:, kk, :]) for kk in range(4)], [b_cqn, wb])
                outs = []
                for jj in range(2):
                    v4 = tok_b[jj][0:TT, :].rearrange("p (h c d) -> p h c d", c=2, d=64)
                    for cpy in range(2):
                        outs.append((lambda half, v4=v4, cpy=cpy: v4[:, :, cpy, half * 32:(half + 1) * 32], jj * 4, jj * 4 + 4, b_tok_b[jj]))
                def post(t=t, ps=ps, pbuf=pbuf, cst=cst, outs=outs):
                    rope(ps, 8, TT, cst, b_csb[cb], [pbuf], outs)
                    for jj in range(2):
                        def ev(pb, pbb, t=t, jj=jj):
                            S.op(ACT, C("activation", out=qrT[:, jj * 4:jj * 4 + 4, t * TT:(t + 1) * TT], in_=pb, func=AF.Copy), reads=[pbb], writes=[b_qr])
                        transposes_bf(tok_b[jj], b_tok_b[jj], TT, 4, None, None, ev)
                qpend.append(post)
                if len(qpend) > 1:
                    qpend.pop(0)()
            while qpend:
                qpend.pop(0)()
            rotN[0] = 4
            mmrot[0] = 0

        def kv_loads(keyblocks):
            loads = []
            for h in range(H):
                for (kb, kc0, TTk, NTk, bias_col, diag) in keyblocks:
                    def load(tile, buf, swsem, kb=kb, h=h, TTk=TTk, NTk=NTk):
                        S.dma(SP, C("dma_start", out=tile[:, 0:TTk * NTk], in_=kscr[kb][:, h, 0:TTk * NTk]), buf.dsem, reads=[b_kscr[kb]], writes=[buf])
                        S.dma(SP, C("dma_start", out=tile[0:TTk, 512:512 + NTk * 128].rearrange("p (t e) -> p t e", e=128), in_=vscr[kb][0:TTk, h, 0:NTk, :]),
                              buf.dsem, reads=[b_vscr[kb], b_kscr[kb]], writes=[buf], skip_own=True)
                    loads.append(load)
            return loads

        def attention(TB, keyblocks):
            S.tag = "attn"
            groups = []
            for h in range(H):
                tiles = []
                for (kb, kc0, TTk, NTk, bias_col, diag) in keyblocks:
                    for kt in range(NTk):
                        tiles.append((kb, kc0, TTk, NTk, bias_col, diag, kt))
                i = 0
                hg = []
                while i < len(tiles):
                    grp = tiles[i:i + 2]
                    if len(grp) == 2 and grp[1][0] != grp[0][0]:
                        grp = grp[:1]
                    hg.append(dict(h=h, tiles=grp, first=(i == 0), last=False))
                    i += len(grp)
                hg[-1]["last"] = True
                groups.extend(hg)
            cur = [None]

            def emit_qk(g):
                h = g["h"]
                grp = g["tiles"]
                if grp[0][6] == 0:
                    cur[0] = KS.next()
                kvt, kvb = cur[0]
                g["kv"] = (kvt, kvb)
                di = rot(5, 2)
                g["di"] = di
                sdb = DB[di]
                TTk = grp[0][2]
                ng = len(grp)
                for gi, (kb, kc0, _, NTk, bias_col, diag, kt) in enumerate(grp):
                    sp = sdb[0:TTk, gi * 512:gi * 512 + TB]
                    S.op(PE, C("matmul", sp, lhsT=kvt[:, kt * TTk:(kt + 1) * TTk], rhs=qnT[:, h, 0:TB], start=True, stop=False),
                         reads=[kvb, b_qn], writes=[b_DB[di][gi]], signal=False)
                for gi, (kb, kc0, _, NTk, bias_col, diag, kt) in enumerate(grp):
                    sp = sdb[0:TTk, gi * 512:gi * 512 + TB]
                    r0 = gi * 64
                    S.op(PE, C("matmul", sp, lhsT=krT_all[r0:r0 + 64, kc0 + kt * TTk:kc0 + (kt + 1) * TTk], rhs=qrT[r0:r0 + 64, h, 0:TB], start=False, stop=True),
                         reads=[b_krT, b_qr], writes=[b_DB[di][gi]])
                pi = rot(4, 2)
                g["pi"] = pi
                bias_col = grp[0][4]
                src = sdb[0:TTk, :].rearrange("p (g c) -> p g c", c=512)[:, 0:ng, 0:TB]
                S.op(ACT, C("activation", out=pT[pi][0:TTk, 0:ng, 0:TB], in_=src, func=AF.Exp, scale=MLA_SCALE, bias=tab[0:TTk, bias_col:bias_col + 1]),
                     reads=[b_DB[di][g_] for g_ in range(ng)] + CONST, writes=[b_pT[pi]])
                for gi, (kb, kc0, _, NTk, bias_col, diag, kt) in enumerate(grp):
                    if diag:
                        if kt > 0:
                            S.op(POOL, C("memset", pT[pi][:, gi, 0:128 * kt], 0.0), writes=[b_pT[pi]])
                        S.op(POOL, C("memset", pT[pi][64:128, gi, 128 * kt:128 * kt + 64], 0.0), writes=[b_pT[pi]])

            def emit_pv(g):
                h = g["h"]
                grp = g["tiles"]
                kvt, kvb = g["kv"]
                pi = g["pi"]
                TTk = grp[0][2]
                set_ = 2 + (h % 2)
                o_ps = bank(set_, 0)[:, 0:TB]
                s_ps = bank(set_, 1)[:, 0:TB]
                obuf, sbuf_ = b_DB[set_][0], b_DB[set_][1]
                ng = len(grp)
                for gi, (kb, kc0, _, NTk, bias_col, diag, kt) in enumerate(grp):
                    first = g["first"] and gi == 0
                    last = g["last"] and gi == ng - 1
                    S.op(PE, C("matmul", o_ps, lhsT=kvt[0:TTk, 512 + kt * 128:512 + (kt + 1) * 128], rhs=pT[pi][0:TTk, gi, 0:TB], start=first, stop=last),
                         reads=[kvb, b_pT[pi]], writes=[obuf], signal=False)
                    S.op(PE, C("matmul", s_ps, lhsT=ones_b[0:TTk, :], rhs=pT[pi][0:TTk, gi, 0:TB], start=first, stop=last),
                         reads=[b_pT[pi]] + CONST, writes=[sbuf_])
                if g["last"]:
                    convert_some(5, 2)
                    j = rot(1, NTMP)
                    S.op(DVE, C("reciprocal", out=tmpf[j][:, 0:TB], in_=s_ps), reads=[sbuf_], writes=[b_tmpf[j]])
                    S.op(DVE, C("tensor_tensor", out=actb[:, 8 + h, 0:TB], in0=o_ps, in1=tmpf[j][:, 0:TB], op=ALU.mult), reads=[obuf, b_tmpf[j]], writes=[b_hT[8 + h]])

            G = len(groups)
            emit_qk(groups[0])
            for gi_ in range(G):
                if gi_ + 1 < G:
                    emit_qk(groups[gi_ + 1])
                emit_pv(groups[gi_])

        def ln_block(TB, seq, nslab, mt_per_slab, kt, src_of, gcol0, mode, out_fn=None):
            S.tag = "ln%d" % mode
            mb, eb = (3, 0), (3, 1)
            pend = []

            def stats_mm(m, i1, i2):
                S.op(PE, C("matmul", bank(*mb)[:, 0:TB], lhsT=ones_b[:], rhs=tmpb[i1][:, 0:TB], start=(m == 0), stop=(m == 15), skip_group_check=True),
                     reads=[b_tmpb[i1]] + CONST, writes=[b_DB[3][0]])
                S.op(PE, C("matmul", bank(*eb)[:, 0:TB], lhsT=ones_b[:], rhs=tmpb[i2][:, 0:TB], start=(m == 0), stop=(m == 15), skip_group_check=True),
                     reads=[b_tmpb[i2]] + CONST, writes=[b_DB[3][1]])
            for sl in range(nslab):
                wt, wb = WS.next()
                for mm in range(mt_per_slab):
                    m = sl * mt_per_slab + mm
                    bi = next_bank()
                    ps = bank(*bi)[:, 0:TB]
                    pbuf = b_DB[bi[0]][bi[1]]
                    rbuf = b_hT if mode == 1 else (b_act + [b_actw])
                    mm_group(ps, pbuf, [(wt[:, (mm * kt + k) * 128:(mm * kt + k + 1) * 128], src_of(k)) for k in range(kt)], rbuf + [wb])
                    S.op(DVE, C("scalar_tensor_tensor", out=xaT[:, m, 0:TB], in0=ps, scalar=modT[:, gcol0 + m, seq:seq + 1], in1=xaT[:, m, 0:TB], op0=ALU.mult, op1=ALU.add),
                         reads=[pbuf, b_mod, b_xaT[m]], writes=[b_xaT[m]])
                    i1, i2 = rot(7, NTB), rot(7, NTB)
                    S.op(ACT, C("activation", out=tmpb[i1][:, 0:TB], in_=xaT[:, m, 0:TB], func=AF.Copy), reads=[b_xaT[m]], writes=[b_tmpb[i1]])
                    S.op(ACT, C("activation", out=tmpb[i2][:, 0:TB], in_=xaT[:, m, 0:TB], func=AF.Square), reads=[b_xaT[m]], writes=[b_tmpb[i2]])
                    pend.append((m, i1, i2))
                    if len(pend) > 1:
                        stats_mm(*pend.pop(0))
            while pend:
                stats_mm(*pend.pop(0))
            stats_finish(bank(*mb)[:, 0:TB], b_DB[3][0], bank(*eb)[:, 0:TB], b_DB[3][1], TB, 1.0 / 2048.0)
            for m in range(16):
                j = rot(1, NTMP)
                S.op(POOL, C("tensor_tensor", out=tmpf[j][:, 0:TB], in0=xaT[:, m, 0:TB], in1=mean_sb[:, 0:TB], op=ALU.subtract), reads=[b_xaT[m], b_mean], writes=[b_tmpf[j]])
                S.op(DVE, C("tensor_tensor", out=tmpf[j][:, 0:TB], in0=tmpf[j][:, 0:TB], in1=rstd_sb[:, 0:TB], op=ALU.mult), reads=[b_tmpf[j], b_rstd], writes=[b_tmpf[j]])
                if mode == 1:
                    S.op(ACT, C("activation", out=actb[:, m, 0:TB], in_=tmpf[j][:, 0:TB], func=AF.Identity, scale=G2T[:, seq, m:m + 1], bias=B2T[:, seq, m:m + 1]),
                         reads=[b_tmpf[j], b_seqtab], writes=[b_hT[m]])
                    S.op(ACT, C("activation", out=xaT[:, m, 0:TB], in_=tmpf[j][:, 0:TB], func=AF.Identity, scale=AG1[:, m:m + 1], bias=AB1[:, m:m + 1]),
                         reads=[b_tmpf[j], b_seqtab], writes=[b_xaT[m]])
                else:
                    S.op(ACT, C("activation", out=xaT[:, m, 0:TB], in_=tmpf[j][:, 0:TB], func=AF.Identity, scale=tab[:, T_LN2G + m:T_LN2G + m + 1], bias=tab[:, T_LN2B + m:T_LN2B + m + 1]),
                         reads=[b_tmpf[j]] + CONST, writes=[b_xaT[m]])

        def ffn_up(TB):
            S.tag = "ffn_up"
            for sl in range(22):
                wt, wb = WS.next()
                for mm in range(2):
                    m = sl * 2 + mm
                    bg = next_bank()
                    gps = bank(*bg)[:, 0:TB]
                    mm_group(gps, b_DB[bg[0]][bg[1]], [(wt[:, (mm * 16 + k) * 128:(mm * 16 + k + 1) * 128], actb[:, k, 0:TB]) for k in range(16)], b_hT + [wb])
                    bu = next_bank()
                    ups = bank(*bu)[:, 0:TB]
                    mm_group(ups, b_DB[bu[0]][bu[1]], [(wt[:, ((2 + mm) * 16 + k) * 128:((2 + mm) * 16 + k + 1) * 128], actb[:, k, 0:TB]) for k in range(16)], b_hT + [wb])
                    j = rot(1, NTMP)
                    S.op(ACT, C("activation", out=tmpf[j][:, 0:TB], in_=gps, func=AF.Silu), reads=[b_DB[bg[0]][bg[1]]], writes=[b_tmpf[j]])
                    S.op(DVE, C("tensor_tensor", out=actT[:, m, 0:TB], in0=ups, in1=tmpf[j][:, 0:TB], op=ALU.mult), reads=[b_DB[bu[0]][bu[1]], b_tmpf[j]], writes=[b_act[m], b_actw])

        if len(stg) == 2:
            stg.append(ckv_stg[0][:, :])
            b_stg.append(b_ckv_stg[0])

        def output_y(TT, NT, y_dst):
            S.tag = "out_y"
            rotN[0] = 8
            for t in range(NT):
                for g in range(4):
                    si = rot(8, 3)
                    bi = next_bank()
                    pb = bank(*bi)[0:TT, :].rearrange("p (k c) -> p k c", c=128)
                    for kk in range(4):
                        k = g * 4 + kk
                        S.op(PE, C("transpose", pb[:, kk, :], xaT[:, k, t * TT:(t + 1) * TT], ident[:]),
                             reads=[b_xaT[k]] + CONST, writes=[b_DB[bi[0]][bi[1]]], signal=(kk == 3))
                    if g % 2 == 0:
                        S.op(ACT, C("activation", out=stg[si][0:TT, :], in_=bank(*bi)[0:TT, :], func=AF.Copy), reads=[b_DB[bi[0]][bi[1]]], writes=[b_stg[si]])
                    else:
                        S.op(DVE, C("tensor_copy", out=stg[si][0:TT, :], in_=bank(*bi)[0:TT, :]), reads=[b_DB[bi[0]][bi[1]]], writes=[b_stg[si]])
                    S.dma(POOL, C("dma_start", out=y_dst(t)[:, g * 512:(g + 1) * 512], in_=stg[si][0:TT, :]), b_stg[si].dsem, reads=[b_stg[si]], writes=[b_out])
            rotN[0] = 4
            mmrot[0] = 0

        def main_block(TT, NT, seq, x_src, cs_rows, key_col0, kblk, C_tabs, keyblocks, ckv_out, kr_out, y_dst, blk_id=None, next_x=None):
            TB = TT * NT
            front(True, TT, NT, seq, x_src, cs_rows, key_col0, kblk, C_tabs, ckv_out=ckv_out, kr_out=kr_out, blk_id=blk_id, next_x=next_x)
            if sub < 1:
                return
            retention(TT, NT, C_tabs)
            if sub < 2:
                return
            qgen(TT, NT)
            if sub < 3:
                return
            attention(TB, keyblocks)
            if sub < 4:
                return
            ln_block(TB, seq, 4, 4, 16, lambda k: actb[:, k, 0:TB], 32, 1)
            if sub < 5:
                return
            ffn_up(TB)
            if sub < 6:
                return
            ln_block(TB, seq, 16, 1, 44, lambda k: actT[:, k, 0:TB], 80, 2)
            if sub < 7:
                return
            output_y(TT, NT, y_dst)

        C128 = (dt128, qd128, T_KD128, T_CD128)
        C64 = (dt128, qd128, T_KD64, T_CD64)

        for b in range(NBLK):
            WS.extend([L_(k, i) for (k, i) in PREFIX_SLABS])
        for b in range(NBLK):
            WS.extend([L_(k, i) for (k, i) in MAIN_SLABS])
        for s in range(2):
            for b in range(8):
                WS.extend([L_("mla", 0), L_("mla", 1)])
            WS.extend([L_(k, i) for (k, i) in MAIN_SLABS])
        for i in range(NBLK):
            kbs = [(p, p * TBP, 128, 4, T_VF, False) for p in range(8)] + [(8 + q, HALF + q * TBP, 128, 4, T_ZERO, q == i) for q in range(i + 1)]
            KS.extend(kv_loads(kbs))
        for s in range(2):
            kbs = [(p, p * TBP, 128, 4, T_ZERO, False) for p in range(8)] + [(16, HALF, 64, 1, T_ZERO, False)]
            KS.extend(kv_loads(kbs))

        S.op(DVE, C("memset", S_f[:], 0.0), writes=[b_Sf])
        npre = 0 if stage < 1 else (1 if stage == 1 else NBLK)
        def xpre_src(b):
            return lambda t: xpre[b * TBP + t * 128:b * TBP + (t + 1) * 128, :]

        def xown_src(i):
            return lambda t: xown[i * TBP + t * 128:i * TBP + (t + 1) * 128, :]

        S.pfx = "P:"
        for b in range(npre):
            convert_some(2)
            carry_state(4)
            nx = (("pre", b + 1), xpre_src(b + 1), 128, 4) if b + 1 < NBLK else (("own", 0), xown_src(0), 128, 4)
            front(False, 128, 4, 0, xpre_src(b), cs_pre[b * TBP:(b + 1) * TBP, :], b * TBP, b, C128, blk_id=("pre", b), next_x=nx)
            if stage >= 3:
                ada_part(32 + 8 * b, 40 + 8 * b)
        convert_some(100)
        if stage < 4:
            convert_some(100, 2)
        S.op(DVE, C("tensor_scalar", out=S_f[:], in0=S_f[:], scalar1=tab[:, T_VF + 1:T_VF + 2], scalar2=None, op0=ALU.mult), reads=[b_Sf] + CONST, writes=[b_Sf])

        for s in range(3 if stage >= 3 else 0):
            j = rot(1, NTMP)
            S.op(DVE, C("tensor_scalar", out=tmpf[j][:, 0:16], in0=modT[:, 64:80, s], scalar1=1.0, scalar2=None, op0=ALU.add), reads=[b_mod], writes=[b_tmpf[j]])
            S.op(DVE, C("tensor_tensor", out=G2T[:, s, :], in0=tmpf[j][:, 0:16], in1=tab[:, T_LN1G:T_LN1G + 16], op=ALU.mult), reads=[b_tmpf[j]] + CONST, writes=[b_seqtab])
            S.op(DVE, C("tensor_tensor", out=tmpf[j][:, 0:16], in0=tmpf[j][:, 0:16], in1=tab[:, T_LN1B:T_LN1B + 16], op=ALU.mult), reads=[b_tmpf[j]] + CONST, writes=[b_tmpf[j]])
            S.op(DVE, C("tensor_tensor", out=B2T[:, s, :], in0=tmpf[j][:, 0:16], in1=modT[:, 48:64, s], op=ALU.add), reads=[b_tmpf[j], b_mod], writes=[b_seqtab])
        S.op(DVE, C("tensor_scalar", out=AG1[:], in0=tab[:, T_LN1G:T_LN1G + 16], scalar1=ALPHA, scalar2=None, op0=ALU.mult), reads=CONST, writes=[b_seqtab])
        S.op(DVE, C("tensor_scalar", out=AB1[:], in0=tab[:, T_LN1B:T_LN1B + 16], scalar1=ALPHA, scalar2=None, op0=ALU.mult), reads=CONST, writes=[b_seqtab])

        S.pfx = "M:"
        nmain = 0 if stage < 4 else (1 if stage == 4 else (2 if stage == 5 else NBLK))
        for i in range(nmain):
            carry_state(4)
            kbs = [(p, p * TBP, 128, 4, T_VF, False) for p in range(8)] + [(8 + q, HALF + q * TBP, 128, 4, T_ZERO, q == i) for q in range(i + 1)]
            nx = (("own", i + 1), xown_src(i + 1), 128, 4) if i + 1 < NBLK else None
            main_block(128, 4, 0,
                       xown_src(i),
                       cs_own[i * TBP:(i + 1) * TBP, :],
                       HALF + i * TBP, 8 + i, C128, kbs,
                       lambda t, i=i: ckv_own[i * TBP + t * 128:i * TBP + (t + 1) * 128, :],
                       lambda t, i=i: kr_own[i * TBP + t * 128:i * TBP + (t + 1) * 128, :],
                       lambda t, i=i: y_own[i * TBP + t * 128:i * TBP + (t + 1) * 128, :], blk_id=("own", i), next_x=nx)
        S.dma(POOL, C("dma_start", out=st_own, in_=S_f[:]), b_Sf.dsem, reads=[b_Sf], writes=[b_out])

        S.pfx = "S:"
        for s in range(2 if stage >= 7 else 0):
            cache_load(0, ckv_c[s, 0:TBP, :], kr_c[s, 0:TBP, :])
            for p in range(8):
                if p + 1 < 8:
                    cache_load((p + 1) % 2, ckv_c[s, (p + 1) * TBP:(p + 2) * TBP, :], kr_c[s, (p + 1) * TBP:(p + 2) * TBP, :])
                front(False, 128, 4, 0, None, None, p * TBP, p, C128, cache=(None, None, p % 2))
            S.dma(POOL, C("dma_start", out=S_f[:], in_=st_c[s]), b_Sf.dsem, writes=[b_Sf])
            carry_state(1)
            kbs = [(p, p * TBP, 128, 4, T_ZERO, False) for p in range(8)] + [(16, HALF, 64, 1, T_ZERO, False)]
            main_block(64, 1, 1 + s,
                       lambda t, s=s: xsmp[s],
                       cs_smp,
                       HALF, 16, C64, kbs,
                       lambda t, s=s: ckv_so[s],
                       lambda t, s=s: kr_so[s],
                       lambda t, s=s: y_smp[s])
            S.dma(POOL, C("dma_start", out=st_so[s], in_=S_f[:]), b_Sf.dsem, reads=[b_Sf], writes=[b_out])

        S.final_wait(POOL, [b_out])
        S.replay()
        import os
        if os.environ.get("MK_TAGS"):
            import pickle
            pickle.dump(S.pe_tags, open(os.environ["MK_TAGS"], "wb"))
        print("ops", S.n_ops, "waits", S.n_wait, "sems", S.nsem, {k: len(v) for k, v in S.prog.items()})
    return nc


_CACHE = {}


def _prep_shared(inp):
    f = np.float32
    w_in = inp["w_in"][0]
    w_ada = inp["w_ada"][0]
    sh = {}
    wa = w_ada.reshape(16, 128, 96, 128).transpose(2, 1, 0, 3)
    sh["wada"] = np.ascontiguousarray(wa.reshape(24, 4, 128, 16 * 128).transpose(0, 2, 1, 3)).reshape(24, 128, 8192)
    offs = [0, 512, 1024, 1536, 2048, 3072, 3584, 4096, 4160]
    chunks = [slice(0, 512), slice(512, 1024), slice(1024, 1536), slice(1536, 2048), slice(3072, 3584), slice(3584, 4096), slice(4096, 4160)]
    sh["wtm"] = np.stack([_pad(_tm_chunk(w_in, c), 8192) for c in chunks])

    def lhs_tiles(w, col0, ntile):
        kt = w.shape[0] // 128
        sub = w[:, col0:col0 + ntile * 128].reshape(kt, 128, ntile, 128).transpose(2, 1, 0, 3)
        return np.ascontiguousarray(sub)

    rg = lhs_tiles(w_in, 2048, 8)
    sh["wrg"] = np.ascontiguousarray(rg.reshape(2, 4, 128, 2048).transpose(0, 2, 1, 3)).reshape(2, 128, 8192)
    w_uq = inp["w_uq"][0]
    w_uk = inp["w_uk"][0].reshape(512, 1024)
    w_uv = inp["w_uv"][0].reshape(512, 1024)
    uk = lhs_tiles(w_uk, 0, 8)
    uqn = lhs_tiles(np.ascontiguousarray(w_uq[:, :, 0:128]).reshape(512, 1024), 0, 8)
    uqr = np.ascontiguousarray(w_uq[:, :, 128:192]).reshape(512, 512)
    sh["wmla"] = np.stack([
        np.ascontiguousarray(uk.transpose(1, 0, 2, 3)).reshape(128, 4096),
        _tm_chunk(w_uv, slice(0, 1024)),
        np.ascontiguousarray(uqn.transpose(1, 0, 2, 3)).reshape(128, 4096),
        _pad(_tm_chunk(uqr, slice(0, 512)), 4096)])
    wo = lhs_tiles(inp["w_out"][0], 0, 16)
    sh["wout"] = np.ascontiguousarray(wo.reshape(4, 4, 128, 2048).transpose(0, 2, 1, 3)).reshape(4, 128, 8192)
    wg = lhs_tiles(inp["w_gate"][0], 0, 44).reshape(22, 2, 128, 2048)
    wu = lhs_tiles(inp["w_up"][0], 0, 44).reshape(22, 2, 128, 2048)
    gu = np.concatenate([wg, wu], axis=1)
    sh["wgu"] = np.ascontiguousarray(gu.transpose(0, 2, 1, 3)).reshape(22, 128, 8192)
    wd = lhs_tiles(inp["w_down"][0], 0, 16)
    sh["wd"] = wd.reshape(16, 128, 5632)
    dec = _decay_tables()
    sh["ident"] = np.eye(128, dtype=f)
    sh["dt128"], sh["qd128"] = dec[128][0].astype(ml_dtypes.bfloat16), dec[128][1]
    sh["gckv"] = np.ascontiguousarray(np.broadcast_to(inp["g_ckv"][0][None, :], (128, KVL))).astype(f)
    tab = np.zeros((128, NTAB), f)
    tab[:, T_LN1G:T_LN1G + 16] = _fm(inp["ln1_g"][0])
    tab[:, T_LN1B:T_LN1B + 16] = _fm(inp["ln1_b"][0])
    tab[:, T_LN2G:T_LN2G + 16] = _fm(inp["ln2_g"][0])
    tab[:, T_LN2B:T_LN2B + 16] = _fm(inp["ln2_b"][0])
    tab[:, T_GRET:T_GRET + 8] = _fm(inp["g_ret"][0])
    tab[:, T_BRET:T_BRET + 8] = _fm(inp["b_ret"][0])
    tab[:, T_GCQ:T_GCQ + 4] = _fm(inp["g_cq"][0])
    tab[:, T_BADA:T_BADA + 96] = _fm(inp["b_ada"][0])
    tab[:, T_KD128:T_KD128 + 8] = dec[128][2]
    tab[:, T_KD64:T_KD64 + 8] = dec[64][2]
    tab[:, T_CD128:T_CD128 + 4] = dec[128][3]
    tab[:, T_CD64:T_CD64 + 4] = dec[64][3]
    tab[:, T_EPS] = EPS
    tab[:, T_ZERO] = 0.0
    sh["tab"] = tab
    sh["cs_all"] = _rope_table(np.arange(SEQ))
    sh["cs_smp"] = _rope_table(PAST + np.arange(SSEQ))
    return sh


def make_in_maps(inp):
    f = np.float32
    sh = _prep_shared(inp)
    in_maps = []
    pairlay = lambda s: np.ascontiguousarray(s.reshape(4, 2, 64, 128).transpose(1, 2, 0, 3)).reshape(128, 4, 128)
    for c in range(8):
        b, half = c // 2, c % 2
        tab = sh["tab"].copy()
        tab[:, T_VF] = 0.0 if half == 1 else NEG
        tab[:, T_VF + 1] = 1.0 if half == 1 else 0.0
        cs = np.stack([inp["c_prompt"][b], inp["c_sample"][2 * c], inp["c_sample"][2 * c + 1]], axis=1)
        m = {
            "xpre": inp["x_prompt"][b, 0:HALF], "xown": inp["x_prompt"][b, half * HALF:(half + 1) * HALF],
            "xsmp": inp["x_sample"][2 * c:2 * c + 2],
            "cT": np.ascontiguousarray(cs.reshape(16, 128, 3).transpose(1, 0, 2)),
            "tab": tab,
            "cs_pre": sh["cs_all"][0:HALF], "cs_own": sh["cs_all"][half * HALF:(half + 1) * HALF], "cs_smp": sh["cs_smp"],
            "ckv_c": inp["cache_mla_ckv"][0, 2 * c:2 * c + 2], "kr_c": inp["cache_mla_krope"][0, 2 * c:2 * c + 2],
            "st_c": np.stack([pairlay(inp["state_ret"][0, 2 * c + s]) for s in range(2)]),
        }
        for k in ("ident", "dt128", "qd128", "gckv", "wada", "wtm", "wrg", "wmla", "wout", "wgu", "wd"):
            m[k] = sh[k]
        in_maps.append({k: (np.ascontiguousarray(v) if k == "dt128" else np.ascontiguousarray(v, dtype=f)) for k, v in m.items()})
    return in_maps


def assemble(R):
    f = np.float32
    unpair = lambda s: np.ascontiguousarray(s.reshape(2, 64, 4, 128).transpose(2, 0, 1, 3)).reshape(8, 64, 128)
    yp = np.zeros((NB, SEQ, D), f)
    ckvp = np.zeros((1, NB, SEQ, KVL), f)
    krp = np.zeros((1, NB, SEQ, ROPE), f)
    rsp = np.zeros((1, NB, H, RDK, RDV), f)
    ys = np.zeros((NSB, SSEQ, D), f)
    ckvs = np.zeros((1, NSB, SSEQ, KVL), f)
    krs = np.zeros((1, NSB, SSEQ, ROPE), f)
    rss = np.zeros((1, NSB, H, RDK, RDV), f)
    for c in range(8):
        b, half = c // 2, c % 2
        sl = slice(half * HALF, (half + 1) * HALF)
        yp[b, sl] = R[c]["y_own"]
        ckvp[0, b, sl] = R[c]["ckv_own"]
        krp[0, b, sl] = R[c]["kr_own"]
        if half == 1:
            rsp[0, b] = unpair(R[c]["st_own"])
        ys[2 * c:2 * c + 2] = R[c]["y_smp"]
        ckvs[0, 2 * c:2 * c + 2] = R[c]["ckv_so"]
        krs[0, 2 * c:2 * c + 2] = R[c]["kr_so"]
        for s in range(2):
            rss[0, 2 * c + s] = unpair(R[c]["st_so"][s])
    return (yp, ys, ckvp, krp, rsp, ckvs, krs, rss)


def kernel(**inp):
    inp = {k: np.asarray(v) for k, v in inp.items()}
    in_maps = make_in_maps(inp)
    if "nc" not in _CACHE:
        _CACHE["nc"] = build_program()
    nc = _CACHE["nc"]
    res = run_bass_kernel_spmd(nc, in_maps, core_ids=list(range(8)))
    return assemble(res.results)
```

```python
import numpy as np
import ml_dtypes
import concourse.bass as bass
import concourse.mybir as mybir
from concourse.bass_utils import run_bass_kernel_spmd
from contextlib import ExitStack

F32 = mybir.dt.float32
BF16 = mybir.dt.bfloat16
ALU = mybir.AluOpType
AF = mybir.ActivationFunctionType

PE, ACT, DVE, POOL, SP = "pe", "act", "dve", "pool", "sp"


def C(name, *a, **k):
    return (name, a, k)

D = 2048
NB = 4
SEQ = 8192
NSB = 16
SSEQ = 64
PAST = 4096
H = 8
RDK = 64
RDV = 128
NOPE = 128
ROPE = 64
QL = 512
KVL = 512
DFF = 5632
ALPHA = 2.0 ** 0.25
EPS = 1e-5
MLA_SCALE = 192.0 ** -0.5
HALF = 4096
TBP = 512
NBLK = HALF // TBP
NEG = -30000.0


class Buf:
    __slots__ = ("name", "w", "r", "aliases", "dsem", "excl")

    def __init__(self, name, dsem=None):
        self.name = name
        self.w = None
        self.r = {}
        self.aliases = []
        self.dsem = dsem
        self.excl = False


class DmaSem:
    __slots__ = ("sem", "cnt", "key")

    def __init__(self, sem, key):
        self.sem = sem
        self.cnt = 0
        self.key = key


class Sched:
    def __init__(self, nc, stack, same_engine_sync=True):
        self.nc = nc
        self.stack = stack
        self.same_engine_sync = same_engine_sync
        self.sems = {}
        self.prog = {}
        self.cnt = {}
        self.seen = {}
        self.nsem = 0
        for k in (PE, ACT, DVE, POOL, SP):
            self.prog[k] = []
            self.cnt[k] = 0
            self.seen[k] = {}
            if k != SP:
                self.sems[k] = stack.enter_context(nc.semaphore("prog_" + k))
                self.nsem += 1
        self.n_wait = 0
        self.n_ops = 0
        self.dsems = []
        self.tag = ""
        self.pfx = ""
        self.pe_tags = []

    def dma_sem(self, name):
        key = "d_" + name
        self.sems[key] = self.stack.enter_context(self.nc.semaphore(key))
        self.nsem += 1
        ds = DmaSem(self.sems[key], key)
        self.dsems.append(ds)
        return ds

    def buf(self, name, dma=False):
        return Buf(name, self.dma_sem(name) if dma else None)

    def _collect(self, reads, writes, ek=None):
        deps = {}

        def add(ev):
            if ev is None:
                return
            k, v = ev
            if deps.get(k, 0) < v:
                deps[k] = v

        for b in reads:
            add(b.w)
            if b.excl:
                for k, v in b.r.items():
                    if k != ek:
                        add((k, v))
            for a in b.aliases:
                add(a.w)
        for b in writes:
            add(b.w)
            for k, v in b.r.items():
                add((k, v))
            for a in b.aliases:
                add(a.w)
                for k, v in a.r.items():
                    add((k, v))
        return deps

    def _waits(self, ek, deps, self_sync=False):
        seen = self.seen[ek]
        waits = []
        for k, v in deps.items():
            if k == ek and (ek == PE or not self.same_engine_sync) and not self_sync:
                continue
            if seen.get(k, 0) < v:
                seen[k] = v
                waits.append((k, v))
        self.n_wait += len(waits)
        return waits

    def _update(self, ev, reads, writes):
        k, v = ev
        for b in reads:
            if b.r.get(k, 0) < v:
                b.r[k] = v
        for b in writes:
            b.w = ev
            b.r = {}

    def op(self, ek, fn, reads=(), writes=(), signal=True, self_sync=False):
        if ek == PE:
            self.pe_tags.append(self.pfx + self.tag)
        deps = self._collect(reads, writes, ek)
        waits = self._waits(ek, deps, self_sync)
        if signal:
            self.cnt[ek] += 1
            ev = (ek, self.cnt[ek])
            self.prog[ek].append((waits, fn, (ek, 1)))
        else:
            ev = (ek, self.cnt[ek] + 1)
            self.prog[ek].append((waits, fn, None))
        self._update(ev, reads, writes)
        self.n_ops += 1
        return ev

    def dma(self, qk, fn, sem, reads=(), writes=(), skip_own=False):
        deps = self._collect(reads, writes)
        if skip_own:
            deps.pop(sem.key, None)
        waits = self._waits(qk, deps)
        sem.cnt += 16
        ev = (sem.key, sem.cnt)
        self.prog[qk].append((waits, fn, (sem.key, 16)))
        self._update(ev, reads, writes)
        self.n_ops += 1
        return ev

    def final_wait(self, ek, bufs):
        deps = self._collect((), bufs)
        for ds in self.dsems:
            if ds.cnt > 0:
                deps[ds.key] = max(deps.get(ds.key, 0), ds.cnt)
        waits = self._waits(ek, deps)
        self.prog[ek].append((waits, None, None))

    def replay(self):
        nc = self.nc
        sems = self.sems
        prog = self.prog

        def run(eng, items):
            for waits, fn, inc in items:
                for k, v in waits:
                    eng.wait_ge(sems[k], v)
                if fn is None:
                    continue
                ins = getattr(eng, fn[0])(*fn[1], **fn[2])
                if inc is not None:
                    ins.then_inc(sems[inc[0]], inc[1])

        with nc.Block() as block:
            @block.tensor
            def _(e):
                run(e, prog[PE])

            @block.scalar
            def _(e):
                run(e, prog[ACT])

            @block.vector
            def _(e):
                run(e, prog[DVE])

            @block.gpsimd
            def _(e):
                run(e, prog[POOL])

            @block.sync
            def _(e):
                run(e, prog[SP])


class Stream:
    def __init__(self, S, tiles, name, slack=0):
        self.S = S
        self.slack = slack
        self.tiles = tiles
        self.bufs = [S.buf(f"{name}{i}", dma=True) for i in range(len(tiles))]
        self.sw_sems = [S.dma_sem(f"{name}{i}_sw") for i in range(len(tiles))]
        self.plan = []
        self.issued = 0
        self.taken = 0

    def extend(self, loads):
        self.plan.extend(loads)

    def next(self):
        R = len(self.tiles)
        while self.issued < len(self.plan) and self.issued < max(self.taken + 1, self.taken + R - self.slack):
            i = self.issued
            s = i % R
            self.plan[i](self.tiles[s], self.bufs[s], self.sw_sems[s])
            self.issued += 1
        s = self.taken % R
        assert self.taken < self.issued
        self.taken += 1
        return self.tiles[s], self.bufs[s]


def _tm_chunk(w, cols):
    sub = w[:, cols]
    kt = sub.shape[0] // 128
    return np.ascontiguousarray(sub.reshape(kt, 128, sub.shape[1]).transpose(1, 0, 2)).reshape(128, -1)


def _pad(a, L):
    if a.shape[1] == L:
        return a
    out = np.zeros((a.shape[0], L), a.dtype)
    out[:, :a.shape[1]] = a
    return out


def _fm(v):
    return np.ascontiguousarray(v.reshape(-1, 128).T)


def _decay_tables():
    h = np.arange(H, dtype=np.float64)
    logg = np.log1p(-np.exp2(-5.0 - h))
    out = {}
    for C in (128, 64):
        i = np.arange(C, dtype=np.float64)
        diff = i[None, :] - i[:, None]
        dt = np.where(diff[:, None, :] >= 0, np.exp(np.maximum(diff, 0)[:, None, :] * logg[None, :, None]), 0.0)
        DT = np.zeros((128, H, C), np.float32)
        DT[:C] = dt
        qd = np.zeros((128, 4, C), np.float32)
        for m in range(4):
            for hh in range(2):
                qd[hh * 64:(hh + 1) * 64, m, :] = (RDK ** -0.5) * np.exp((i + 1.0) * logg[2 * m + hh])[None, :]
        kd = np.zeros((128, H), np.float32)
        kd[:C] = np.exp((C - 1.0 - i)[:, None] * logg[None, :])
        cd = np.zeros((128, 4), np.float32)
        for m in range(4):
            for hh in range(2):
                cd[hh * 64:(hh + 1) * 64, m] = np.exp(C * logg[2 * m + hh])
        out[C] = (DT, qd, kd, cd)
    return out


def _rope_table(pos):
    half = 32
    inv = (np.float32(10000.0) ** (-np.arange(half, dtype=np.float32) / np.float32(half))).astype(np.float32)
    ang = (pos.astype(np.float32)[:, None] * inv[None, :]).astype(np.float32)
    return np.concatenate([np.cos(ang.astype(np.float64)), np.sin(ang.astype(np.float64))], axis=1).astype(np.float32)


T_LN1G, T_LN1B, T_LN2G, T_LN2B = 0, 16, 32, 48
T_GRET, T_BRET, T_GCQ = 64, 72, 80
T_BADA = 84
T_VF = 180
T_KD128, T_KD64 = 182, 190
T_CD128, T_CD64 = 198, 202
T_EPS, T_ZERO = 206, 207
NTAB = 208


def build_program(stage=99, sub=99):
    nc = bass.Bass("TRN2", target_bir_lowering=False)

    def din(name, shape, dt=F32):
        return nc.dram_tensor(name, list(shape), dt, kind="ExternalInput").ap()

    def dout(name, shape, dt=F32):
        return nc.dram_tensor(name, list(shape), dt, kind="ExternalOutput").ap()

    def dscr(name, shape, dt=BF16):
        return nc.dram_tensor(name, list(shape), dt, kind="Internal").ap()

    xpre = din("xpre", [HALF, D])
    xown = din("xown", [HALF, D])
    xsmp = din("xsmp", [2, SSEQ, D])
    cT_d = din("cT", [128, 16, 3])
    tab_d = din("tab", [128, NTAB])
    cs_pre = din("cs_pre", [HALF, 64])
    cs_own = din("cs_own", [HALF, 64])
    cs_smp = din("cs_smp", [SSEQ, 64])
    ckv_c = din("ckv_c", [2, PAST, KVL])
    kr_c = din("kr_c", [2, PAST, ROPE])
    st_c = din("st_c", [2, 128, 4, 128])
    ident_d = din("ident", [128, 128])
    dt128_d = din("dt128", [128, H, 128], BF16)
    qd128_d = din("qd128", [128, 4, 128])
    gckv_d = din("gckv", [128, KVL])
    wada_d = din("wada", [24, 128, 8192])
    wtm_d = din("wtm", [7, 128, 8192])
    wrg_d = din("wrg", [2, 128, 8192])
    wmla_d = din("wmla", [4, 128, 4096])
    wout_d = din("wout", [4, 128, 8192])
    wgu_d = din("wgu", [22, 128, 8192])
    wd_d = din("wd", [16, 128, 5632])

    y_own = dout("y_own", [HALF, D])
    y_smp = dout("y_smp", [2, SSEQ, D])
    ckv_own = dout("ckv_own", [HALF, KVL])
    kr_own = dout("kr_own", [HALF, ROPE])
    st_own = dout("st_own", [128, 4, 128])
    ckv_so = dout("ckv_so", [2, SSEQ, KVL])
    kr_so = dout("kr_so", [2, SSEQ, ROPE])
    st_so = dout("st_so", [2, 128, 4, 128])

    wtm_s = dscr("wtm_s", [7, 128, 8192])
    wrg_s = dscr("wrg_s", [2, 128, 8192])
    wmla_s = dscr("wmla_s", [4, 128, 4096])
    wout_s = dscr("wout_s", [4, 128, 8192])
    wgu_s = dscr("wgu_s", [22, 128, 8192])
    wd_s = dscr("wd_s", [16, 128, 5632])
    NKB = 17
    kscr = dscr("kscr", [NKB, 128, H, TBP])
    vscr = dscr("vscr", [NKB, 128, H, 4, 128])

    with ExitStack() as st:
        S = Sched(nc, st)
        nalloc = [0]

        def T(name, shape, dt):
            return st.enter_context(nc.sbuf_tensor("s_" + name, list(shape), dt))

        x_stage = [T(f"x_stage{i}", [128, D], F32) for i in range(2)]
        b_xs = [S.buf(f"x_stage{i}", dma=True) for i in range(2)]

        xaT = T("xaT", [128, 16, TBP], F32); b_xaT = [S.buf(f"xaT{i}") for i in range(16)]
        actb = T("actb", [128, 16, TBP], BF16); b_hT = [S.buf(f"hT{i}") for i in range(16)]
        U = T("U", [128, 22528], BF16)
        uo = [0]

        def carve(n_elems, shape):
            a = U[:, uo[0]:uo[0] + n_elems]
            uo[0] += n_elems
            return a

        kdec_tok = U[:, 0:2048].rearrange("p (t c) -> p t c", t=4)
        v_tok = U[:, 2048:6144].rearrange("p (t c) -> p t c", t=4)
        qT_ret = U[:, 6144:8192].rearrange("p (m c) -> p m c", m=4)
        kT_ret = U[:, 8192:10240].rearrange("p (m c) -> p m c", m=4)
        qdT_ret = U[:, 10240:12288].rearrange("p (m c) -> p m c", m=4)
        rgT = U[:, 12288:16384].rearrange("p (m c) -> p m c", m=8)
        cqnT = U[:, 16384:18432].rearrange("p (m c) -> p m c", m=4)
        ckvT = U[:, 18432:20480].rearrange("p (m c) -> p m c", m=4)
        knT_blk = U[:, 0:4096].rearrange("p (h c) -> p h c", h=8)
        v_blk = U[:, 4096:8192].rearrange("p (h t e) -> p h t e", h=8, t=4)
        qnT = U[:, 0:4096].rearrange("p (h c) -> p h c", h=8)
        qrT = U[:, 4096:8192].rearrange("p (h c) -> p h c", h=8)
        actT = U[:, 0:44 * 512].rearrange("p (m c) -> p m c", m=44)
        b_U = S.buf("U")
        b_kdec = S.buf("kdec_tok"); b_vtok = S.buf("v_tok"); b_qT = S.buf("qT_ret"); b_kT = S.buf("kT_ret")
        b_qdT = S.buf("qdT_ret"); b_rg = S.buf("rgT"); b_cqn = S.buf("cqnT"); b_ckvT = S.buf("ckvT")
        b_knb = S.buf("knT_blk", dma=True); b_vb = S.buf("v_blk", dma=True)
        b_qn = S.buf("qnT"); b_qr = S.buf("qrT"); b_act = [S.buf(f"actT{i}") for i in range(44)]; b_actw = S.buf("actT_all")
        retb = [b_kdec, b_vtok, b_qT, b_kT, b_qdT]
        ag = [[b_knb, b_vb], [b_kdec, b_vtok, b_qT], [b_qn, b_qr]]
        for gi_, g_ in enumerate(ag):
            for x_ in g_:
                for gj_, h_ in enumerate(ag):
                    if gi_ != gj_:
                        x_.aliases.extend(h_)
        allA = retb + [b_rg, b_cqn, b_ckvT, b_knb, b_vb, b_qn, b_qr]
        b_actw.aliases = list(allA)
        for a_ in allA:
            a_.aliases.append(b_actw)

        krT_all = T("krT_all", [128, 8192 + 128], BF16); b_krT = S.buf("krT_all")
        WR = 2
        w_ring = [T(f"w_ring{i}", [128, 8192], BF16) for i in range(WR)]
        WS = Stream(S, w_ring, "w_ring")
        KR = 3
        kv_ring = [T(f"kv_ring{i}", [128, 1024], BF16) for i in range(KR)]
        KS = Stream(S, kv_ring, "kv_ring", slack=1)
        xa_bf = xaT[:, :, :].rearrange("p k t -> p (k t)").bitcast(BF16)
        ada_ring = [xa_bf[:, 0:8192], xa_bf[:, 8192:16384], U[:, 8192:16384]]
        AS = Stream(S, ada_ring, "ada_ring")
        for q_ in b_xaT:
            q_.aliases.extend(AS.bufs[0:2])
        for q_ in (b_kT, b_qdT, b_rg):
            q_.aliases.append(AS.bufs[2])
        pT = [T(f"pT{i}", [128, 2, 512], BF16) for i in range(2)]
        b_pT = [S.buf(f"pT{i}") for i in range(2)]
        NTMP = 3
        tmpf = [T(f"tmpf{i}", [128, 512], F32) for i in range(NTMP)]
        b_tmpf = [S.buf(f"tmpf{i}") for i in range(NTMP)]
        NTB = 4
        tmpb = [T(f"tmpb{i}", [128, 512], BF16) for i in range(NTB)]
        b_tmpb = [S.buf(f"tmpb{i}") for i in range(NTB)]
        mean_sb = T("mean_sb", [128, 512], F32); b_mean = S.buf("mean_sb")
        rstd_sb = T("rstd_sb", [128, 512], F32); b_rstd = S.buf("rstd_sb")
        ckv_stg = [T(f"ckv_stg{i}", [128, 512], F32) for i in range(1)]
        b_ckv_stg = [S.buf(f"ckv_stg{i}", dma=True) for i in range(1)]
        kr_stg = [T(f"kr_stg{i}", [128, 64], F32) for i in range(1)]
        b_kr_stg = [S.buf(f"kr_stg{i}", dma=True) for i in range(1)]
        tok_b = [T(f"tok_b{i}", [128, 512], BF16) for i in range(2)]
        b_tok_b = [S.buf(f"tok_b{i}") for i in range(2)]
        sstat = T("sstat", [128, 8], F32); b_sstat = S.buf("sstat")
        cs_blk = [T(f"cs_blk{i}", [128, 4, 64], F32) for i in range(1)]
        b_csb = [S.buf(f"cs_blk{i}", dma=True) for i in range(1)]
        cur_cs = [0]
        cstage_raw = T("cstage_raw", [128, 2304], BF16)
        cache_stage = [cstage_raw[:, :].rearrange("p (t c) -> p t c", t=4),
                       x_stage[1][:, :].bitcast(BF16)[:, 0:2304].rearrange("p (t c) -> p t c", t=4)]
        b_cstage = [S.buf("cache_stage0", dma=True), b_xs[1]]
        ystg_all = cstage_raw[:, 0:2048].bitcast(F32)
        stg = [ystg_all[:, 0:512], ystg_all[:, 512:1024]]
        b_stg = [S.buf(f"ystg{i}", dma=True) for i in range(2)]
        for q_ in b_stg:
            q_.aliases.append(b_cstage[0])
            b_cstage[0].aliases.append(q_)
        S_f = T("S_f", [128, 4, 128], F32); b_Sf = S.buf("S_f", dma=True)
        S_bf = T("S_bf", [128, 5, 4, 128], BF16); b_Sbf = [S.buf(f"S_bf{i}") for i in range(5)]
        tab = T("tab", [128, NTAB], F32); b_const = S.buf("const", dma=True)
        ident = T("ident", [128, 128], F32)
        identb = T("identb", [128, 128], BF16)
        ones_b = T("ones_b", [128, 128], BF16)
        dt128 = T("dt128", [128, H, 128], BF16)
        qd128 = T("qd128", [128, 4, 128], F32)
        gckv = T("gckv", [128, KVL], F32)
        cT = mean_sb[:, 0:48].rearrange("p (k s) -> p k s", s=3)
        cTb = T("cTb", [128, 16, 3], BF16)
        modT = T("modT", [128, 96, 3], F32); b_mod = S.buf("modT")
        SC1P = T("SC1P", [128, 3, 16], F32)
        G2T = T("G2T", [128, 3, 16], F32)
        B2T = T("B2T", [128, 3, 16], F32)
        AG1 = T("AG1", [128, 16], F32)
        AB1 = T("AB1", [128, 16], F32)
        b_seqtab = S.buf("seqtab")

        def P(name):
            return st.enter_context(nc.psum_tensor(name, [128, 1024], F32))

        DB = [P(f"DB{i}") for i in range(4)]
        b_DB = [[S.buf(f"DB{i}_{j}") for j in range(2)] for i in range(4)]
        for r_ in b_DB:
            for q_ in r_:
                q_.excl = True

        def bank(i, j):
            return DB[i][:, j * 512:(j + 1) * 512]

        mmrot = [0]

        ROT = [(0, 0), (0, 1), (1, 0), (1, 1), (3, 0), (3, 1)]
        rotN = [4]

        def next_bank():
            r = mmrot[0] % rotN[0]
            mmrot[0] = (r + 1) % rotN[0]
            return ROT[r]

        tmpi = {}

        def rot(idx, n):
            v = tmpi.get((idx, n), 0)
            tmpi[(idx, n)] = (v + 1) % n
            return v

        b_wscr = {}
        b_kscr = [S.buf(f"kscr{i}") for i in range(NKB)]
        b_vscr = [S.buf(f"vscr{i}") for i in range(NKB)]
        b_out = S.buf("outputs")
        d2d_sems = [S.dma_sem("d2dA"), S.dma_sem("d2dB"), S.dma_sem("d2dC")]

        def cload(dst, src):
            S.dma(SP, C("dma_start", out=dst, in_=src), b_const.dsem, writes=[b_const])

        cload(tab[:], tab_d)
        S.dma(SP, C("dma_start", out=cT, in_=cT_d), b_const.dsem, writes=[b_const, b_mean])
        cload(ident[:], ident_d)
        cload(dt128[:], dt128_d)
        cload(qd128[:], qd128_d)
        cload(gckv[:], gckv_d)
        b_c2 = S.buf("const2")
        S.op(DVE, C("tensor_copy", out=identb[:], in_=ident[:]), reads=[b_const], writes=[b_c2])
        S.op(DVE, C("memset", ones_b[:], 1.0), writes=[b_c2])
        S.op(ACT, C("activation", out=cTb[:], in_=cT, func=AF.Silu), reads=[b_const, b_mean], writes=[b_c2])
        CONST = [b_const, b_c2]

        def slab_from_scratch(scr, key, i, L):
            def load(tile, buf, swsem):
                S.dma(SP, C("dma_start", out=tile[:, 0:L], in_=scr[i]), buf.dsem,
                      reads=[b_wscr[(key, i)]], writes=[buf])
            return load

        def slab_ada(i):
            def load(tile, buf, swsem):
                S.dma(POOL, C("dma_start", out=tile[:, :], in_=wada_d[i]), swsem, writes=[buf])
            return load

        groups = {"tm": (wtm_d, wtm_s, 7, 8192), "rg": (wrg_d, wrg_s, 2, 8192), "mla": (wmla_d, wmla_s, 4, 4096),
                  "out": (wout_d, wout_s, 4, 8192), "gu": (wgu_d, wgu_s, 22, 8192), "d": (wd_d, wd_s, 16, 5632)}

        conv_groups = [[], [], []]

        def convert(key, idxs, grp):
            src, dst, n, L = groups[key]
            for i in idxs:
                b_wscr[(key, i)] = S.buf(f"wscr_{key}{i}")
                S.dma(POOL, C("dma_start", out=dst[i], in_=src[i]), d2d_sems[grp], writes=[b_wscr[(key, i)]])
                conv_groups[grp].append(b_wscr[(key, i)])

        def seal(grp):
            for b_ in conv_groups[grp]:
                b_.w = (d2d_sems[grp].key, d2d_sems[grp].cnt)

        def L_(key, i):
            return slab_from_scratch(groups[key][1], key, i, groups[key][3])

        PREFIX_SLABS = [("tm", 5), ("tm", 6), ("mla", 0), ("mla", 1), ("tm", 1), ("tm", 2), ("tm", 3)]
        MAIN_SLABS = ([("tm", 4), ("tm", 5), ("tm", 6), ("mla", 0), ("mla", 1), ("tm", 0), ("tm", 1), ("tm", 2), ("tm", 3),
                       ("rg", 0), ("rg", 1), ("mla", 2), ("mla", 3)]
                      + [("out", i) for i in range(4)] + [("gu", i) for i in range(22)] + [("d", i) for i in range(16)])
        convert("tm", [5, 6], 0)
        convert("mla", [0, 1], 0)
        convert("tm", [1, 2, 3], 0)
        seal(0)
        AS.extend([slab_ada(i) for i in range(24)])

        def ada_part(f0, f1):
            S.tag = "ada"
            bk = (3, 1)
            for sl in range(f0 // 4, f1 // 4):
                wt, wb = AS.next()
                for mm in range(4):
                    f = sl * 4 + mm
                    for k in range(16):
                        S.op(PE, C("matmul",
                            bank(*bk)[:, f * 3:(f + 1) * 3], lhsT=wt[:, (mm * 16 + k) * 128:(mm * 16 + k + 1) * 128],
                            rhs=cTb[:, k, :], start=(k == 0), stop=(k == 15)),
                            reads=[wb] + CONST, writes=[b_DB[3][1]], signal=(k == 15))
            S.op(DVE, C("tensor_tensor",
                out=modT[:, f0:f1, :], in0=bank(*bk)[:, f0 * 3:f1 * 3].rearrange("p (f s) -> p f s", s=3),
                in1=tab[:, T_BADA + f0:T_BADA + f1].unsqueeze(2).to_broadcast([128, f1 - f0, 3]), op=ALU.add),
                reads=[b_DB[3][1]] + CONST, writes=[b_mod])

        ada_part(0, 32)
        S.op(DVE, C("tensor_scalar", out=SC1P[:].rearrange("p s f -> p f s"), in0=modT[:, 16:32, :], scalar1=1.0, scalar2=None, op0=ALU.add),
             reads=[b_mod], writes=[b_seqtab])
        conv_pending = {1: [("tm", 4), ("tm", 0), ("rg", 0), ("rg", 1), ("mla", 2), ("mla", 3)] + [("out", i) for i in range(4)],
                        2: [("gu", i) for i in range(22)] + [("d", i) for i in range(16)]}

        def convert_some(n, grp=1):
            lst = conv_pending[grp]
            if not lst:
                return
            for _ in range(n):
                if lst:
                    k_, i_ = lst.pop(0)
                    convert(k_, [i_], grp)
            if not lst:
                seal(grp)

        def mm_group(out_ap, obuf, pairs, reads):
            n = len(pairs)
            for i, (l, r) in enumerate(pairs):
                S.op(PE, C("matmul", out_ap, lhsT=l, rhs=r, start=(i == 0), stop=(i == n - 1)),
                     reads=reads, writes=[obuf], signal=(i == n - 1))

        def rope(src, G, TT, cs, b_cs_, reads, outs):
            s3 = src.rearrange("p (g c) -> p g c", c=64)
            x1 = s3[:, :, 0:32]
            x2 = s3[:, :, 32:64]
            cosb = cs[0:TT, 0:32].unsqueeze(1).to_broadcast([TT, G, 32])
            sinb = cs[0:TT, 32:64].unsqueeze(1).to_broadcast([TT, G, 32])
            ia, ib = rot(0, NTMP), rot(0, NTMP)
            ta = tmpf[ia][0:TT, 0:G * 32].rearrange("p (g c) -> p g c", c=32)
            tb = tmpf[ia][0:TT, 256:256 + G * 32].rearrange("p (g c) -> p g c", c=32)
            tc = tmpf[ib][0:TT, 0:G * 32].rearrange("p (g c) -> p g c", c=32)
            td = tmpf[ib][0:TT, 256:256 + G * 32].rearrange("p (g c) -> p g c", c=32)
            rd = reads + [b_cs_]
            S.op(DVE, C("tensor_tensor", out=ta, in0=x1, in1=cosb, op=ALU.mult), reads=rd, writes=[b_tmpf[ia]])
            S.op(DVE, C("tensor_tensor", out=tb, in0=x2, in1=sinb, op=ALU.mult), reads=rd, writes=[b_tmpf[ia]])
            S.op(DVE, C("tensor_tensor", out=tc, in0=x2, in1=cosb, op=ALU.mult), reads=rd, writes=[b_tmpf[ib]])
            S.op(DVE, C("tensor_tensor", out=td, in0=x1, in1=sinb, op=ALU.mult), reads=rd, writes=[b_tmpf[ib]])
            for half, (pa, pb_, opc, bufx) in enumerate(((ta, tb, ALU.subtract, b_tmpf[ia]), (tc, td, ALU.add, b_tmpf[ib]))):
                for o_ in outs:
                    if len(o_) == 2:
                        dst, dbuf = o_
                        d3 = dst.rearrange("p (g c) -> p g c", c=64)[:, :, half * 32:(half + 1) * 32]
                        S.op(DVE, C("tensor_tensor", out=d3, in0=pa, in1=pb_, op=opc), reads=[bufx], writes=[dbuf])
                    else:
                        dfn, g0, g1, dbuf = o_
                        S.op(DVE, C("tensor_tensor", out=dfn(half), in0=pa[:, g0:g1, :], in1=pb_[:, g0:g1, :], op=opc), reads=[bufx], writes=[dbuf])

        def transposes_bf(src_tile, src_buf, TT, nblk, dst_fn, dst_bufs, eng_fn):
            bi = next_bank()
            pb = bank(*bi).bitcast(BF16)[:, 0:nblk * TT].rearrange("p (n t) -> p n t", t=TT)
            for n in range(nblk):
                S.op(PE, C("transpose", pb[:, n, :], src_tile[0:TT, n * 128:(n + 1) * 128], identb[0:TT, 0:TT]),
                     reads=[src_buf] + CONST, writes=[b_DB[bi[0]][bi[1]]], signal=(n == nblk - 1))
            eng_fn(pb, b_DB[bi[0]][bi[1]])

        def rmsnorm_rstd(ps, pbuf, TT):
            j = rot(1, NTMP)
            c = rot(2, 4)
            S.op(ACT, C("activation", out=tmpf[j][0:TT, :], in_=ps, func=AF.Square),
                 reads=[pbuf], writes=[b_tmpf[j]])
            S.op(DVE, C("reduce_sum", out=sstat[0:TT, 2 * c:2 * c + 1], in_=tmpf[j][0:TT, :], axis=mybir.AxisListType.X),
                 reads=[b_tmpf[j]], writes=[b_sstat])
            S.op(ACT, C("activation", out=sstat[0:TT, 2 * c + 1:2 * c + 2], in_=sstat[0:TT, 2 * c:2 * c + 1], func=AF.Sqrt,
                                             scale=1.0 / 512.0, bias=tab[0:TT, T_EPS:T_EPS + 1]),
                 reads=[b_sstat] + CONST, writes=[b_sstat])
            S.op(DVE, C("reciprocal", out=sstat[0:TT, 2 * c + 1:2 * c + 2], in_=sstat[0:TT, 2 * c + 1:2 * c + 2]),
                 reads=[b_sstat], writes=[b_sstat])
            return sstat[0:TT, 2 * c + 1:2 * c + 2]

        x_pref = [None]

        def cache_load(ci, ckv_src, kr_src):
            S.dma(POOL, C("dma_start", out=cache_stage[ci][:, :, 0:512], in_=ckv_src.rearrange("(t p) c -> p t c", p=128)), b_cstage[ci].dsem,
                  writes=[b_cstage[ci]])
            S.dma(POOL, C("dma_start", out=cache_stage[ci][:, :, 512:576], in_=kr_src.rearrange("(t p) c -> p t c", p=128)), b_cstage[ci].dsem,
                  writes=[b_cstage[ci]], skip_own=True)

        def front(full, TT, NT, seq, x_src, cs_rows, key_col0, kblk, C_tabs, ckv_out=None, kr_out=None,
                  cache=None, blk_id=None, next_x=None):
            TB = TT * NT
            DT, QD, KDc, CDc = C_tabs

            def kvgen():
                S.tag = "kvgen"
                kvgen_impl(TT, NT, kblk)
                S.tag = "front_win"
            S.tag = "front_x"
            rotN[0] = 6
            if cache is None:
                cur_cs[0] = 0
                cb = cur_cs[0]
                S.dma(POOL, C("dma_start", out=cs_blk[cb][0:TT, 0:NT, :], in_=cs_rows.rearrange("(t p) c -> p t c", p=TT)), b_csb[cb].dsem,
                      writes=[b_csb[cb]])
                for t in range(NT):
                    sx = t % 2
                    if not (x_pref[0] is not None and x_pref[0] == blk_id and t < 2):
                        S.dma(POOL, C("dma_start", out=x_stage[sx][0:TT, :], in_=x_src(t)), b_xs[sx].dsem,
                              writes=[b_xs[sx]])
                    for g in range(4):
                        bi = next_bank()
                        pb = bank(*bi)[:, 0:4 * TT].rearrange("p (k t) -> p k t", t=TT)
                        for kk in range(4):
                            k = g * 4 + kk
                            S.op(PE, C("transpose", pb[:, kk, :], x_stage[sx][0:TT, k * 128:(k + 1) * 128], ident[0:TT, 0:TT]),
                                 reads=[b_xs[sx]] + CONST, writes=[b_DB[bi[0]][bi[1]]], signal=(kk == 3))
                        pbuf = b_DB[bi[0]][bi[1]]
                        if full:
                            S.op(ACT, C("activation", out=xaT[:, g * 4:g * 4 + 4, t * TT:(t + 1) * TT], in_=pb, func=AF.Identity, scale=ALPHA),
                                 reads=[pbuf], writes=b_xaT[g * 4:g * 4 + 4])
                        j = rot(1, NTMP)
                        tv = tmpf[j][:, 0:4 * TT].rearrange("p (k t) -> p k t", t=TT)
                        S.op(DVE, C("tensor_tensor", out=tv, in0=pb, in1=SC1P[:, seq, g * 4:g * 4 + 4].unsqueeze(2).to_broadcast([128, 4, TT]), op=ALU.mult),
                             reads=[pbuf, b_seqtab], writes=[b_tmpf[j]])
                        S.op(DVE, C("tensor_tensor", out=actb[:, g * 4:g * 4 + 4, t * TT:(t + 1) * TT], in0=tv,
                                                                              in1=modT[:, g * 4:g * 4 + 4, seq].unsqueeze(2).to_broadcast([128, 4, TT]), op=ALU.add),
                             reads=[b_tmpf[j], b_mod], writes=b_hT[g * 4:g * 4 + 4])
                x_pref[0] = None
                if next_x is not None:
                    nid, nsrc, nTT, nNT = next_x
                    for t in range(min(2, nNT)):
                        S.dma(POOL, C("dma_start", out=x_stage[t][0:nTT, :], in_=nsrc(t)), b_xs[t].dsem, writes=[b_xs[t]])
                    x_pref[0] = nid
                S.tag = "front_win"
                chunks = [4, 5, 6, "kv", 0, 1, 2, 3] if full else [5, 6, "kv", 1, 2, 3]
                if full and sub < 0:
                    chunks = chunks[:-sub - 1]
                for c in chunks:
                    if c == "kv":
                        kvgen()
                        continue
                    wt, wb = WS.next()
                    ncol = 64 if c == 6 else 512
                    w3 = wt[:, 0:16 * ncol].rearrange("p (k n) -> p k n", n=ncol)
                    pend = []
                    for t in range(NT):
                        bi = next_bank()
                        ps = bank(*bi)[0:TT, 0:ncol]
                        pbuf = b_DB[bi[0]][bi[1]]
                        mm_group(ps, pbuf, [(actb[:, k, t * TT:(t + 1) * TT], w3[:, k, :]) for k in range(16)], b_hT + [wb])
                        def post(t=t, ps=ps, pbuf=pbuf, c=c):
                            cst = cs_blk[cb][:, t, :]
                            if c == 0:
                                j = rot(3, 2)
                                rope(ps, 8, TT, cst, b_csb[cb], [pbuf], [(tok_b[j][0:TT, :], b_tok_b[j])])

                                def ev(pb, pbb, t=t):
                                    S.op(ACT, C("activation", out=qT_ret[:, :, t * TT:(t + 1) * TT], in_=pb, func=AF.Identity, scale=RDK ** -0.5),
                                         reads=[pbb], writes=[b_qT])
                                    S.op(DVE, C("tensor_tensor", out=qdT_ret[:, :, t * TT:(t + 1) * TT], in0=pb, in1=QD[:, :, 0:TT], op=ALU.mult),
                                         reads=[pbb] + CONST, writes=[b_qdT])
                                transposes_bf(tok_b[j], b_tok_b[j], TT, 4, None, None, ev)
                            elif c == 1:
                                j = rot(3, 2)
                                rope(ps, 8, TT, cst, b_csb[cb], [pbuf], [(tok_b[j][0:TT, :], b_tok_b[j])])
                                S.op(DVE, C("tensor_tensor",
                                    out=kdec_tok[0:TT, t, :].rearrange("p (h c) -> p h c", c=64), in0=tok_b[j][0:TT, :].rearrange("p (h c) -> p h c", c=64),
                                    in1=tab[0:TT, KDc:KDc + 8].unsqueeze(2).to_broadcast([TT, 8, 64]), op=ALU.mult),
                                    reads=[b_tok_b[j]] + CONST, writes=[b_kdec])
                                if full:
                                    def ev(pb, pbb, t=t):
                                        S.op(ACT, C("activation", out=kT_ret[:, :, t * TT:(t + 1) * TT], in_=pb, func=AF.Copy),
                                             reads=[pbb], writes=[b_kT])
                                    transposes_bf(tok_b[j], b_tok_b[j], TT, 4, None, None, ev)
                            elif c in (2, 3):
                                S.op(ACT, C("activation", out=v_tok[0:TT, t, (c - 2) * 512:(c - 1) * 512], in_=ps, func=AF.Copy),
                                     reads=[pbuf], writes=[b_vtok])
                                if c == 3:
                                    state_step(t, TT, CDc)
                            elif c == 4:
                                rs = rmsnorm_rstd(ps, pbuf, TT)
                                j = rot(3, 2)
                                S.op(ACT, C("activation", out=tok_b[j][0:TT, :], in_=ps, func=AF.Identity, scale=rs),
                                     reads=[pbuf, b_sstat], writes=[b_tok_b[j]])

                                def ev(pb, pbb, t=t):
                                    for kk in range(4):
                                        S.op(ACT, C("activation", out=cqnT[:, kk, t * TT:(t + 1) * TT], in_=pb[:, kk, :], func=AF.Identity,
                                                                                scale=tab[:, T_GCQ + kk:T_GCQ + kk + 1]),
                                             reads=[pbb] + CONST, writes=[b_cqn])
                                transposes_bf(tok_b[j], b_tok_b[j], TT, 4, None, None, ev)
                            elif c == 5:
                                rs = rmsnorm_rstd(ps, pbuf, TT)
                                si = 0
                                S.op(DVE, C("scalar_tensor_tensor", out=ckv_stg[si][0:TT, :], in0=ps, scalar=rs, in1=gckv[0:TT, :], op0=ALU.mult, op1=ALU.mult),
                                     reads=[pbuf, b_sstat] + CONST, writes=[b_ckv_stg[si]])
                                j = rot(3, 2)
                                S.op(ACT, C("activation", out=tok_b[j][0:TT, :], in_=ckv_stg[si][0:TT, :], func=AF.Copy),
                                     reads=[b_ckv_stg[si]], writes=[b_tok_b[j]])
                                if ckv_out is not None:
                                    S.dma(POOL, C("dma_start", out=ckv_out(t), in_=ckv_stg[si][0:TT, :]), b_ckv_stg[si].dsem,
                                          reads=[b_ckv_stg[si]], writes=[b_out])

                                def ev(pb, pbb, t=t):
                                    S.op(ACT, C("activation", out=ckvT[:, :, t * TT:(t + 1) * TT], in_=pb, func=AF.Copy), reads=[pbb], writes=[b_ckvT])
                                transposes_bf(tok_b[j], b_tok_b[j], TT, 4, None, None, ev)
                            else:
                                si = 0
                                j = rot(3, 2)
                                rope(ps, 1, TT, cst, b_csb[cb], [pbuf],
                                     [(kr_stg[si][0:TT, :], b_kr_stg[si]), (tok_b[j][0:TT, 0:64], b_tok_b[j]), (tok_b[j][0:TT, 64:128], b_tok_b[j])])
                                if kr_out is not None:
                                    S.dma(POOL, C("dma_start", out=kr_out(t), in_=kr_stg[si][0:TT, :]), b_kr_stg[si].dsem,
                                          reads=[b_kr_stg[si]], writes=[b_out])

                                def ev(pb, pbb, t=t):
                                    S.op(ACT, C("activation", out=krT_all[:, key_col0 + t * TT:key_col0 + (t + 1) * TT], in_=pb[:, 0, :], func=AF.Copy),
                                         reads=[pbb], writes=[b_krT])
                                transposes_bf(tok_b[j], b_tok_b[j], TT, 1, None, None, ev)
                        pend.append(post)
                        if len(pend) > 1:
                            pend.pop(0)()
                    while pend:
                        pend.pop(0)()
                S.tag = "front_rg"
                if full and sub >= 0:
                    for sl in range(2):
                        wt, wb = WS.next()
                        for mm in range(4):
                            m = sl * 4 + mm
                            bi = next_bank()
                            ps = bank(*bi)[:, 0:TB]
                            pbuf = b_DB[bi[0]][bi[1]]
                            mm_group(ps, pbuf, [(wt[:, (mm * 16 + k) * 128:(mm * 16 + k + 1) * 128], actb[:, k, 0:TB]) for k in range(16)], b_hT + [wb])
                            S.op(ACT, C("activation", out=rgT[:, m, 0:TB], in_=ps, func=AF.Silu), reads=[pbuf], writes=[b_rg])
            else:
                ci = cache[2]
                for t in range(4):
                    def ev(pb, pbb, t=t):
                        S.op(ACT, C("activation", out=ckvT[:, :, t * 128:(t + 1) * 128], in_=pb, func=AF.Copy), reads=[pbb], writes=[b_ckvT])
                    transposes_bf(cache_stage[ci][:, t, 0:512], b_cstage[ci], 128, 4, None, None, ev)
                    j = rot(3, 2)
                    S.op(DVE, C("tensor_copy", out=tok_b[j][:, 0:64], in_=cache_stage[ci][:, t, 512:576]), reads=[b_cstage[ci]], writes=[b_tok_b[j]])
                    S.op(DVE, C("tensor_copy", out=tok_b[j][:, 64:128], in_=cache_stage[ci][:, t, 512:576]), reads=[b_cstage[ci]], writes=[b_tok_b[j]])

                    def ev2(pb, pbb, t=t):
                        S.op(ACT, C("activation", out=krT_all[:, key_col0 + t * 128:key_col0 + (t + 1) * 128], in_=pb[:, 0, :], func=AF.Copy),
                             reads=[pbb], writes=[b_krT])
                    transposes_bf(tok_b[j], b_tok_b[j], 128, 1, None, None, ev2)
            if cache is not None:
                kvgen()
            rotN[0] = 4
            mmrot[0] = 0

        def kvgen_impl(TT, NT, kblk):
            TB = TT * NT
            wt, wb = WS.next()
            for h in range(H):
                bi = next_bank()
                ps = bank(*bi)[:, 0:TB]
                pbuf = b_DB[bi[0]][bi[1]]
                mm_group(ps, pbuf, [(wt[:, (h * 4 + kk) * 128:(h * 4 + kk + 1) * 128], ckvT[:, kk, 0:TB]) for kk in range(4)], [b_ckvT, wb])
                eng = ACT if h % 2 == 0 else DVE
                if eng == ACT:
                    S.op(ACT, C("activation", out=knT_blk[:, h, 0:TB], in_=ps, func=AF.Copy), reads=[pbuf], writes=[b_knb])
                else:
                    S.op(DVE, C("tensor_copy", out=knT_blk[:, h, 0:TB], in_=ps), reads=[pbuf], writes=[b_knb])
            wt, wb = WS.next()
            w3 = wt[:, 0:4096].rearrange("p (k n) -> p k n", n=1024)
            for t in range(NT):
                for c in range(2):
                    bi = next_bank()
                    ps = bank(*bi)[0:TT, :]
                    pbuf = b_DB[bi[0]][bi[1]]
                    mm_group(ps, pbuf, [(ckvT[:, kk, t * TT:(t + 1) * TT], w3[:, kk, c * 512:(c + 1) * 512]) for kk in range(4)], [b_ckvT, wb])
                    dst = v_blk[0:TT, c * 4:c * 4 + 4, t, :]
                    src = ps.rearrange("p (h e) -> p h e", e=128)
                    if c == 0:
                        S.op(ACT, C("activation", out=dst, in_=src, func=AF.Copy), reads=[pbuf], writes=[b_vb])
                    else:
                        S.op(DVE, C("tensor_copy", out=dst, in_=src), reads=[pbuf], writes=[b_vb])
            S.dma(POOL, C("dma_start", out=kscr[kblk][:, :, 0:TB], in_=knT_blk[:, :, 0:TB]), b_knb.dsem, reads=[b_knb], writes=[b_kscr[kblk]])
            S.dma(POOL, C("dma_start", out=vscr[kblk][0:TT, :, 0:NT, :], in_=v_blk[0:TT, :, 0:NT, :]), b_vb.dsem, reads=[b_vb], writes=[b_vscr[kblk]])

        def state_step(t, TT, CDc):
            kvb = (2, 0)
            for m in range(4):
                hb = b_DB[2][m // 2]
                out = DB[2][:, m * 256:(m + 1) * 256]
                S.op(PE, C("matmul", out, lhsT=kdec_tok[0:TT, t, m * 128:(m + 1) * 128], rhs=v_tok[0:TT, t, m * 256:(m + 1) * 256], start=True, stop=True),
                     reads=[b_kdec, b_vtok], writes=[hb])
            kv3 = DB[2][:, :].rearrange("p (m c) -> p m c", c=256)
            S.op(DVE, C("tensor_tensor", out=S_f[:], in0=S_f[:], in1=tab[:, CDc:CDc + 4].unsqueeze(2).to_broadcast([128, 4, 128]), op=ALU.mult),
                 reads=[b_Sf] + CONST, writes=[b_Sf])
            S.op(DVE, C("tensor_tensor", out=S_f[0:64], in0=S_f[0:64], in1=kv3[0:64, :, 0:128], op=ALU.add),
                 reads=[b_Sf, b_DB[2][0], b_DB[2][1]], writes=[b_Sf])
            S.op(DVE, C("tensor_tensor", out=S_f[64:128], in0=S_f[64:128], in1=kv3[64:128, :, 128:256], op=ALU.add),
                 reads=[b_Sf, b_DB[2][0], b_DB[2][1]], writes=[b_Sf])
            S.op(POOL, C("tensor_copy", out=S_bf[:, t + 1], in_=S_f[:]), reads=[b_Sf], writes=[b_Sbf[t + 1]])

        def carry_state(NT):
            S.op(POOL, C("tensor_copy", out=S_bf[:, 0], in_=S_f[:]), reads=[b_Sf], writes=[b_Sbf[0]])

        def retention(TT, NT, C_tabs):
            S.tag = "retention"
            TB = TT * NT
            DT, QD, KDc, CDc = C_tabs
            ob = [(3, 0), (3, 1)]

            def emit_st(m):
                if mmrot[0] % 2:
                    next_bank()
                sb0 = next_bank()
                next_bank()
                dbi = sb0[0]
                pr = m % 2
                for t in range(NT):
                    for hh in range(2):
                        S.op(PE, C("matmul", DB[dbi][0:TT, hh * 512 + t * TT:hh * 512 + (t + 1) * TT],
                                   lhsT=kT_ret[hh * 64:(hh + 1) * 64, m, t * TT:(t + 1) * TT],
                                   rhs=qT_ret[hh * 64:(hh + 1) * 64, m, t * TT:(t + 1) * TT], start=True, stop=True),
                             reads=[b_kT, b_qT], writes=[b_DB[dbi][hh]])
                sps = DB[dbi][0:TT, :].rearrange("p (h c) -> p h c", c=512)[:, :, 0:TB].rearrange("p h (t i) -> p h t i", i=TT)
                dtb = DT[0:TT, 2 * m:2 * m + 2, 0:TT].unsqueeze(2).to_broadcast([TT, 2, NT, TT])
                outp = pT[pr][0:TT, :, 0:TB].rearrange("p h (t i) -> p h t i", i=TT)
                S.op(DVE, C("tensor_tensor", out=outp, in0=sps, in1=dtb, op=ALU.mult),
                     reads=[b_DB[dbi][0], b_DB[dbi][1]] + CONST, writes=[b_pT[pr]])

            def emit_o(m):
                pr = m % 2
                for t in range(NT):
                    for hh in range(2):
                        h = 2 * m + hh
                        o_ap = bank(*ob[hh])[:, t * TT:(t + 1) * TT]
                        obuf = b_DB[ob[hh][0]][ob[hh][1]]
                        S.op(PE, C("matmul", o_ap, lhsT=v_tok[0:TT, t, h * 128:(h + 1) * 128], rhs=pT[pr][0:TT, hh, t * TT:(t + 1) * TT], start=True, stop=False),
                             reads=[b_vtok, b_pT[pr]], writes=[obuf], signal=(TT < 128))
                        S.op(PE, C("matmul", o_ap, lhsT=S_bf[hh * 64:(hh + 1) * 64, t, m, :], rhs=qdT_ret[hh * 64:(hh + 1) * 64, m, t * TT:(t + 1) * TT], start=False, stop=True),
                             reads=[b_Sbf[t], b_qdT], writes=[obuf], self_sync=(TT < 128))
                for hh in range(2):
                    h = 2 * m + hh
                    groupnorm_gate(bank(*ob[hh])[:, 0:TB], b_DB[ob[hh][0]][ob[hh][1]], h, TB)
                S.tag = "retention"

            emit_st(0)
            for m in range(4):
                if m + 1 < 4:
                    emit_st(m + 1)
                emit_o(m)

        def stats_finish(mean_ps, mean_buf, ex2_ps, ex2_buf, TB, inv):
            S.op(ACT, C("activation", out=mean_sb[:, 0:TB], in_=mean_ps, func=AF.Identity, scale=inv), reads=[mean_buf], writes=[b_mean])
            j = rot(1, NTMP)
            S.op(DVE, C("tensor_tensor", out=tmpf[j][:, 0:TB], in0=mean_sb[:, 0:TB], in1=mean_sb[:, 0:TB], op=ALU.mult), reads=[b_mean], writes=[b_tmpf[j]])
            S.op(DVE, C("scalar_tensor_tensor", out=tmpf[j][:, 0:TB], in0=ex2_ps, scalar=inv, in1=tmpf[j][:, 0:TB], op0=ALU.mult, op1=ALU.subtract), reads=[ex2_buf, b_tmpf[j]], writes=[b_tmpf[j]])
            S.op(ACT, C("activation", out=rstd_sb[:, 0:TB], in_=tmpf[j][:, 0:TB], func=AF.Sqrt, bias=tab[:, T_EPS:T_EPS + 1]), reads=[b_tmpf[j]] + CONST, writes=[b_rstd])
            S.op(DVE, C("reciprocal", out=rstd_sb[:, 0:TB], in_=rstd_sb[:, 0:TB]), reads=[b_rstd], writes=[b_rstd])

        def groupnorm_gate(o_ps, obuf, h, TB):
            i1, i2 = rot(7, NTB), rot(7, NTB)
            j = rot(1, NTMP)
            S.op(ACT, C("activation", out=tmpf[j][:, 0:TB], in_=o_ps, func=AF.Copy), reads=[obuf], writes=[b_tmpf[j]])
            S.op(ACT, C("activation", out=tmpb[i1][:, 0:TB], in_=tmpf[j][:, 0:TB], func=AF.Copy), reads=[b_tmpf[j]], writes=[b_tmpb[i1]])
            S.op(ACT, C("activation", out=tmpb[i2][:, 0:TB], in_=tmpf[j][:, 0:TB], func=AF.Square), reads=[b_tmpf[j]], writes=[b_tmpb[i2]])
            mb, eb = (2, 0), (2, 1)
            S.op(PE, C("matmul", bank(*mb)[:, 0:TB], lhsT=ones_b[:], rhs=tmpb[i1][:, 0:TB], start=True, stop=True), reads=[b_tmpb[i1]] + CONST, writes=[b_DB[2][0]])
            S.op(PE, C("matmul", bank(*eb)[:, 0:TB], lhsT=ones_b[:], rhs=tmpb[i2][:, 0:TB], start=True, stop=True), reads=[b_tmpb[i2]] + CONST, writes=[b_DB[2][1]])
            stats_finish(bank(*mb)[:, 0:TB], b_DB[2][0], bank(*eb)[:, 0:TB], b_DB[2][1], TB, 1.0 / 128.0)
            S.op(DVE, C("tensor_tensor", out=tmpf[j][:, 0:TB], in0=tmpf[j][:, 0:TB], in1=mean_sb[:, 0:TB], op=ALU.subtract), reads=[b_tmpf[j], b_mean], writes=[b_tmpf[j]])
            S.op(DVE, C("tensor_tensor", out=tmpf[j][:, 0:TB], in0=tmpf[j][:, 0:TB], in1=rstd_sb[:, 0:TB], op=ALU.mult), reads=[b_tmpf[j], b_rstd], writes=[b_tmpf[j]])
            S.op(ACT, C("activation", out=tmpf[j][:, 0:TB], in_=tmpf[j][:, 0:TB], func=AF.Identity, scale=tab[:, T_GRET + h:T_GRET + h + 1], bias=tab[:, T_BRET + h:T_BRET + h + 1]),
                 reads=[b_tmpf[j]] + CONST, writes=[b_tmpf[j]])
            S.op(DVE, C("tensor_tensor", out=actb[:, h, 0:TB], in0=tmpf[j][:, 0:TB], in1=rgT[:, h, 0:TB], op=ALU.mult), reads=[b_tmpf[j], b_rg], writes=[b_hT[h]])

        def qgen(TT, NT):
            S.tag = "qgen"
            TB = TT * NT
            wt, wb = WS.next()
            for h in range(H):
                bi = next_bank()
                ps = bank(*bi)[:, 0:TB]
                pbuf = b_DB[bi[0]][bi[1]]
                mm_group(ps, pbuf, [(wt[:, (h * 4 + kk) * 128:(h * 4 + kk + 1) * 128], cqnT[:, kk, 0:TB]) for kk in range(4)], [b_cqn, wb])
                if h % 2 == 0:
                    S.op(ACT, C("activation", out=qnT[:, h, 0:TB], in_=ps, func=AF.Copy), reads=[pbuf], writes=[b_qn])
                else:
                    S.op(DVE, C("tensor_copy", out=qnT[:, h, 0:TB], in_=ps), reads=[pbuf], writes=[b_qn])
            wt, wb = WS.next()
            w3 = wt[:, 0:2048].rearrange("p (k n) -> p k n", n=512)
            qpend = []
            for t in range(NT):
                cb = cur_cs[0]
                cst = cs_blk[cb][:, t, :]
                bi = next_bank()
                ps = bank(*bi)[0:TT, :]
                pbuf = b_DB[bi[0]][bi[1]]
                mm_group(ps, pbuf, [(cqnT[:, kk, t * TT:(t + 1) * TT], w3[:, kk, :]) for kk in range(4)], [b_cqn, wb])
                outs = []
                for jj in range(2):
                    v4 = tok_b[jj][0:TT, :].rearrange("p (h c d) -> p h c d", c=2, d=64)
                    for cpy in range(2):
                        outs.append((lambda half, v4=v4, cpy=cpy: v4[:, :, cpy, half * 32:(half + 1) * 32], jj * 4, jj * 4 + 4, b_tok_b[jj]))
                def post(t=t, ps=ps, pbuf=pbuf, cst=cst, outs=outs):
                    rope(ps, 8, TT, cst, b_csb[cb], [pbuf], outs)
                    for jj in range(2):
                        def ev(pb, pbb, t=t, jj=jj):
                            S.op(ACT, C("activation", out=qrT[:, jj * 4:jj * 4 + 4, t * TT:(t + 1) * TT], in_=pb, func=AF.Copy), reads=[pbb], writes=[b_qr])
                        transposes_bf(tok_b[jj], b_tok_b[jj], TT, 4, None, None, ev)
                qpend.append(post)
                if len(qpend) > 1:
                    qpend.pop(0)()
            while qpend:
                qpend.pop(0)()

        def kv_loads(keyblocks):
            loads = []
            for h in range(H):
                for (kb, kc0, TTk, NTk, bias_col, diag) in keyblocks:
                    def load(tile, buf, swsem, kb=kb, h=h, TTk=TTk, NTk=NTk):
                        S.dma(SP, C("dma_start", out=tile[:, 0:TTk * NTk], in_=kscr[kb][:, h, 0:TTk * NTk]), buf.dsem, reads=[b_kscr[kb]], writes=[buf])
                        S.dma(SP, C("dma_start", out=tile[0:TTk, 512:512 + NTk * 128].rearrange("p (t e) -> p t e", e=128), in_=vscr[kb][0:TTk, h, 0:NTk, :]),
                              buf.dsem, reads=[b_vscr[kb], b_kscr[kb]], writes=[buf], skip_own=True)
                    loads.append(load)
            return loads

        def attention(TB, keyblocks):
            S.tag = "attn"
            groups = []
            for h in range(H):
                tiles = []
                for (kb, kc0, TTk, NTk, bias_col, diag) in keyblocks:
                    for kt in range(NTk):
                        tiles.append((kb, kc0, TTk, NTk, bias_col, diag, kt))
                i = 0
                hg = []
                while i < len(tiles):
                    grp = tiles[i:i + 2]
                    if len(grp) == 2 and grp[1][0] != grp[0][0]:
                        grp = grp[:1]
                    hg.append(dict(h=h, tiles=grp, first=(i == 0), last=False))
                    i += len(grp)
                hg[-1]["last"] = True
                groups.extend(hg)
            cur = [None]

            def emit_qk(g):
                h = g["h"]
                grp = g["tiles"]
                if grp[0][6] == 0:
                    cur[0] = KS.next()
                kvt, kvb = cur[0]
                g["kv"] = (kvt, kvb)
                di = rot(5, 2)
                g["di"] = di
                sdb = DB[di]
                TTk = grp[0][2]
                ng = len(grp)
                for gi, (kb, kc0, _, NTk, bias_col, diag, kt) in enumerate(grp):
                    sp = sdb[0:TTk, gi * 512:gi * 512 + TB]
                    S.op(PE, C("matmul", sp, lhsT=kvt[:, kt * TTk:(kt + 1) * TTk], rhs=qnT[:, h, 0:TB], start=True, stop=False),
                         reads=[kvb, b_qn], writes=[b_DB[di][gi]], signal=False)
                for gi, (kb, kc0, _, NTk, bias_col, diag, kt) in enumerate(grp):
                    sp = sdb[0:TTk, gi * 512:gi * 512 + TB]
                    r0 = gi * 64
                    S.op(PE, C("matmul", sp, lhsT=krT_all[r0:r0 + 64, kc0 + kt * TTk:kc0 + (kt + 1) * TTk], rhs=qrT[r0:r0 + 64, h, 0:TB], start=False, stop=True),
                         reads=[b_krT, b_qr], writes=[b_DB[di][gi]])
                pi = rot(4, 2)
                g["pi"] = pi
                bias_col = grp[0][4]
                src = sdb[0:TTk, :].rearrange("p (g c) -> p g c", c=512)[:, 0:ng, 0:TB]
                S.op(ACT, C("activation", out=pT[pi][0:TTk, 0:ng, 0:TB], in_=src, func=AF.Exp, scale=MLA_SCALE, bias=tab[0:TTk, bias_col:bias_col + 1]),
                     reads=[b_DB[di][g_] for g_ in range(ng)] + CONST, writes=[b_pT[pi]])
                for gi, (kb, kc0, _, NTk, bias_col, diag, kt) in enumerate(grp):
                    if diag:
                        if kt > 0:
                            S.op(POOL, C("memset", pT[pi][:, gi, 0:128 * kt], 0.0), writes=[b_pT[pi]])
                        S.op(POOL, C("memset", pT[pi][64:128, gi, 128 * kt:128 * kt + 64], 0.0), writes=[b_pT[pi]])

            def emit_pv(g):
                h = g["h"]
                grp = g["tiles"]
                kvt, kvb = g["kv"]
                pi = g["pi"]
                TTk = grp[0][2]
                set_ = 2 + (h % 2)
                o_ps = bank(set_, 0)[:, 0:TB]
                s_ps = bank(set_, 1)[:, 0:TB]
                obuf, sbuf_ = b_DB[set_][0], b_DB[set_][1]
                ng = len(grp)
                for gi, (kb, kc0, _, NTk, bias_col, diag, kt) in enumerate(grp):
                    first = g["first"] and gi == 0
                    last = g["last"] and gi == ng - 1
                    S.op(PE, C("matmul", o_ps, lhsT=kvt[0:TTk, 512 + kt * 128:512 + (kt + 1) * 128], rhs=pT[pi][0:TTk, gi, 0:TB], start=first, stop=last),
                         reads=[kvb, b_pT[pi]], writes=[obuf], signal=False)
                    S.op(PE, C("matmul", s_ps, lhsT=ones_b[0:TTk, :], rhs=pT[pi][0:TTk, gi, 0:TB], start=first, stop=last),
                         reads=[b_pT[pi]] + CONST, writes=[sbuf_])
                if g["last"]:
                    convert_some(5, 2)
                    j = rot(1, NTMP)
                    S.op(DVE, C("reciprocal", out=tmpf[j][:, 0:TB], in_=s_ps), reads=[sbuf_], writes=[b_tmpf[j]])
                    S.op(DVE, C("tensor_tensor", out=actb[:, 8 + h, 0:TB], in0=o_ps, in1=tmpf[j][:, 0:TB], op=ALU.mult), reads=[obuf, b_tmpf[j]], writes=[b_hT[8 + h]])

            G = len(groups)
            emit_qk(groups[0])
            for gi_ in range(G):
                if gi_ + 1 < G:
                    emit_qk(groups[gi_ + 1])
                emit_pv(groups[gi_])

        def ln_block(TB, seq, nslab, mt_per_slab, kt, src_of, gcol0, mode, out_fn=None):
            S.tag = "ln%d" % mode
            mb, eb = (3, 0), (3, 1)
            pend = []

            def stats_mm(m, i1, i2):
                S.op(PE, C("matmul", bank(*mb)[:, 0:TB], lhsT=ones_b[:], rhs=tmpb[i1][:, 0:TB], start=(m == 0), stop=(m == 15), skip_group_check=True),
                     reads=[b_tmpb[i1]] + CONST, writes=[b_DB[3][0]])
                S.op(PE, C("matmul", bank(*eb)[:, 0:TB], lhsT=ones_b[:], rhs=tmpb[i2][:, 0:TB], start=(m == 0), stop=(m == 15), skip_group_check=True),
                     reads=[b_tmpb[i2]] + CONST, writes=[b_DB[3][1]])
            for sl in range(nslab):
                wt, wb = WS.next()
                for mm in range(mt_per_slab):
                    m = sl * mt_per_slab + mm
                    bi = next_bank()
                    ps = bank(*bi)[:, 0:TB]
                    pbuf = b_DB[bi[0]][bi[1]]
                    rbuf = b_hT if mode == 1 else (b_act + [b_actw])
                    mm_group(ps, pbuf, [(wt[:, (mm * kt + k) * 128:(mm * kt + k + 1) * 128], src_of(k)) for k in range(kt)], rbuf + [wb])
                    S.op(DVE, C("scalar_tensor_tensor", out=xaT[:, m, 0:TB], in0=ps, scalar=modT[:, gcol0 + m, seq:seq + 1], in1=xaT[:, m, 0:TB], op0=ALU.mult, op1=ALU.add),
                         reads=[pbuf, b_mod, b_xaT[m]], writes=[b_xaT[m]])
                    i1, i2 = rot(7, NTB), rot(7, NTB)
                    S.op(ACT, C("activation", out=tmpb[i1][:, 0:TB], in_=xaT[:, m, 0:TB], func=AF.Copy), reads=[b_xaT[m]], writes=[b_tmpb[i1]])
                    S.op(ACT, C("activation", out=tmpb[i2][:, 0:TB], in_=xaT[:, m, 0:TB], func=AF.Square), reads=[b_xaT[m]], writes=[b_tmpb[i2]])
                    pend.append((m, i1, i2))
                    if len(pend) > 1:
                        stats_mm(*pend.pop(0))
            while pend:
                stats_mm(*pend.pop(0))
            stats_finish(bank(*mb)[:, 0:TB], b_DB[3][0], bank(*eb)[:, 0:TB], b_DB[3][1], TB, 1.0 / 2048.0)
            deferred = []

            def normed(m):
                j = rot(1, NTMP)
                S.op(POOL, C("tensor_tensor", out=tmpf[j][:, 0:TB], in0=xaT[:, m, 0:TB], in1=mean_sb[:, 0:TB], op=ALU.subtract), reads=[b_xaT[m], b_mean], writes=[b_tmpf[j]])
                S.op(DVE, C("tensor_tensor", out=tmpf[j][:, 0:TB], in0=tmpf[j][:, 0:TB], in1=rstd_sb[:, 0:TB], op=ALU.mult), reads=[b_tmpf[j], b_rstd], writes=[b_tmpf[j]])
                return j

            for m in range(16):
                j = normed(m)
                if mode == 1:
                    S.op(ACT, C("activation", out=actb[:, m, 0:TB], in_=tmpf[j][:, 0:TB], func=AF.Identity, scale=G2T[:, seq, m:m + 1], bias=B2T[:, seq, m:m + 1]),
                         reads=[b_tmpf[j], b_seqtab], writes=[b_hT[m]])

                    def xa_later(m=m):
                        j2 = normed(m)
                        S.op(ACT, C("activation", out=xaT[:, m, 0:TB], in_=tmpf[j2][:, 0:TB], func=AF.Identity, scale=AG1[:, m:m + 1], bias=AB1[:, m:m + 1]),
                             reads=[b_tmpf[j2], b_seqtab], writes=[b_xaT[m]])
                    deferred.append(xa_later)
                else:
                    S.op(ACT, C("activation", out=xaT[:, m, 0:TB], in_=tmpf[j][:, 0:TB], func=AF.Identity, scale=tab[:, T_LN2G + m:T_LN2G + m + 1], bias=tab[:, T_LN2B + m:T_LN2B + m + 1]),
                         reads=[b_tmpf[j]] + CONST, writes=[b_xaT[m]])
            return deferred

        def ffn_up(TB, deferred=()):
            S.tag = "ffn_up"
            deferred = list(deferred)
            for sl in range(22):
                wt, wb = WS.next()
                for mm in range(2):
                    m = sl * 2 + mm
                    bg = next_bank()
                    gps = bank(*bg)[:, 0:TB]
                    mm_group(gps, b_DB[bg[0]][bg[1]], [(wt[:, (mm * 16 + k) * 128:(mm * 16 + k + 1) * 128], actb[:, k, 0:TB]) for k in range(16)], b_hT + [wb])
                    bu = next_bank()
                    ups = bank(*bu)[:, 0:TB]
                    mm_group(ups, b_DB[bu[0]][bu[1]], [(wt[:, ((2 + mm) * 16 + k) * 128:((2 + mm) * 16 + k + 1) * 128], actb[:, k, 0:TB]) for k in range(16)], b_hT + [wb])
                    j = rot(1, NTMP)
                    S.op(ACT, C("activation", out=tmpf[j][:, 0:TB], in_=gps, func=AF.Silu), reads=[b_DB[bg[0]][bg[1]]], writes=[b_tmpf[j]])
                    S.op(DVE, C("tensor_tensor", out=actT[:, m, 0:TB], in0=ups, in1=tmpf[j][:, 0:TB], op=ALU.mult), reads=[b_DB[bu[0]][bu[1]], b_tmpf[j]], writes=[b_act[m], b_actw])
                    if deferred and m >= 2:
                        deferred.pop(0)()
            while deferred:
                deferred.pop(0)()

        if len(stg) == 2:
            stg.append(ckv_stg[0][:, :])
            b_stg.append(b_ckv_stg[0])

        def output_y(TT, NT, y_dst):
            S.tag = "out_y"
            for t in range(NT):
                for g in range(4):
                    si = rot(8, 3)
                    bi = next_bank()
                    pb = bank(*bi)[0:TT, :].rearrange("p (k c) -> p k c", c=128)
                    for kk in range(4):
                        k = g * 4 + kk
                        S.op(PE, C("transpose", pb[:, kk, :], xaT[:, k, t * TT:(t + 1) * TT], ident[:]),
                             reads=[b_xaT[k]] + CONST, writes=[b_DB[bi[0]][bi[1]]], signal=(kk == 3))
                    if g % 2 == 0:
                        S.op(ACT, C("activation", out=stg[si][0:TT, :], in_=bank(*bi)[0:TT, :], func=AF.Copy), reads=[b_DB[bi[0]][bi[1]]], writes=[b_stg[si]])
                    else:
                        S.op(DVE, C("tensor_copy", out=stg[si][0:TT, :], in_=bank(*bi)[0:TT, :]), reads=[b_DB[bi[0]][bi[1]]], writes=[b_stg[si]])
                    S.dma(POOL, C("dma_start", out=y_dst(t)[:, g * 512:(g + 1) * 512], in_=stg[si][0:TT, :]), b_stg[si].dsem, reads=[b_stg[si]], writes=[b_out])

        def main_block(TT, NT, seq, x_src, cs_rows, key_col0, kblk, C_tabs, keyblocks, ckv_out, kr_out, y_dst, blk_id=None, next_x=None):
            TB = TT * NT
            front(True, TT, NT, seq, x_src, cs_rows, key_col0, kblk, C_tabs, ckv_out=ckv_out, kr_out=kr_out, blk_id=blk_id, next_x=next_x)
            if sub < 1:
                return
            retention(TT, NT, C_tabs)
            if sub < 2:
                return
            qgen(TT, NT)
            if sub < 3:
                return
            attention(TB, keyblocks)
            if sub < 4:
                return
            dfr = ln_block(TB, seq, 4, 4, 16, lambda k: actb[:, k, 0:TB], 32, 1)
            if sub < 5:
                for f_ in dfr:
                    f_()
                return
            ffn_up(TB, dfr)
            if sub < 6:
                return
            ln_block(TB, seq, 16, 1, 44, lambda k: actT[:, k, 0:TB], 80, 2)
            if sub < 7:
                return
            output_y(TT, NT, y_dst)

        C128 = (dt128, qd128, T_KD128, T_CD128)
        C64 = (dt128, qd128, T_KD64, T_CD64)

        for b in range(NBLK):
            WS.extend([L_(k, i) for (k, i) in PREFIX_SLABS])
        for b in range(NBLK):
            WS.extend([L_(k, i) for (k, i) in MAIN_SLABS])
        for s in range(2):
            for b in range(8):
                WS.extend([L_("mla", 0), L_("mla", 1)])
            WS.extend([L_(k, i) for (k, i) in MAIN_SLABS])
        for i in range(NBLK):
            kbs = [(p, p * TBP, 128, 4, T_VF, False) for p in range(8)] + [(8 + q, HALF + q * TBP, 128, 4, T_ZERO, q == i) for q in range(i + 1)]
            KS.extend(kv_loads(kbs))
        for s in range(2):
            kbs = [(p, p * TBP, 128, 4, T_ZERO, False) for p in range(8)] + [(16, HALF, 64, 1, T_ZERO, False)]
            KS.extend(kv_loads(kbs))

        S.op(DVE, C("memset", S_f[:], 0.0), writes=[b_Sf])
        npre = 0 if stage < 1 else (1 if stage == 1 else NBLK)
        def xpre_src(b):
            return lambda t: xpre[b * TBP + t * 128:b * TBP + (t + 1) * 128, :]

        def xown_src(i):
            return lambda t: xown[i * TBP + t * 128:i * TBP + (t + 1) * 128, :]

        S.pfx = "P:"
        for b in range(npre):
            convert_some(2)
            carry_state(4)
            nx = (("pre", b + 1), xpre_src(b + 1), 128, 4) if b + 1 < NBLK else (("own", 0), xown_src(0), 128, 4)
            front(False, 128, 4, 0, xpre_src(b), cs_pre[b * TBP:(b + 1) * TBP, :], b * TBP, b, C128, blk_id=("pre", b), next_x=nx)
            if stage >= 3:
                ada_part(32 + 8 * b, 40 + 8 * b)
        convert_some(100)
        if stage < 4:
            convert_some(100, 2)
        S.op(DVE, C("tensor_scalar", out=S_f[:], in0=S_f[:], scalar1=tab[:, T_VF + 1:T_VF + 2], scalar2=None, op0=ALU.mult), reads=[b_Sf] + CONST, writes=[b_Sf])

        for s in range(3 if stage >= 3 else 0):
            j = rot(1, NTMP)
            S.op(DVE, C("tensor_scalar", out=tmpf[j][:, 0:16], in0=modT[:, 64:80, s], scalar1=1.0, scalar2=None, op0=ALU.add), reads=[b_mod], writes=[b_tmpf[j]])
            S.op(DVE, C("tensor_tensor", out=G2T[:, s, :], in0=tmpf[j][:, 0:16], in1=tab[:, T_LN1G:T_LN1G + 16], op=ALU.mult), reads=[b_tmpf[j]] + CONST, writes=[b_seqtab])
            S.op(DVE, C("tensor_tensor", out=tmpf[j][:, 0:16], in0=tmpf[j][:, 0:16], in1=tab[:, T_LN1B:T_LN1B + 16], op=ALU.mult), reads=[b_tmpf[j]] + CONST, writes=[b_tmpf[j]])
            S.op(DVE, C("tensor_tensor", out=B2T[:, s, :], in0=tmpf[j][:, 0:16], in1=modT[:, 48:64, s], op=ALU.add), reads=[b_tmpf[j], b_mod], writes=[b_seqtab])
        S.op(DVE, C("tensor_scalar", out=AG1[:], in0=tab[:, T_LN1G:T_LN1G + 16], scalar1=ALPHA, scalar2=None, op0=ALU.mult), reads=CONST, writes=[b_seqtab])
        S.op(DVE, C("tensor_scalar", out=AB1[:], in0=tab[:, T_LN1B:T_LN1B + 16], scalar1=ALPHA, scalar2=None, op0=ALU.mult), reads=CONST, writes=[b_seqtab])

        S.pfx = "M:"
        nmain = 0 if stage < 4 else (1 if stage == 4 else (2 if stage == 5 else NBLK))
        for i in range(nmain):
            carry_state(4)
            kbs = [(p, p * TBP, 128, 4, T_VF, False) for p in range(8)] + [(8 + q, HALF + q * TBP, 128, 4, T_ZERO, q == i) for q in range(i + 1)]
            nx = (("own", i + 1), xown_src(i + 1), 128, 4) if i + 1 < NBLK else None
            main_block(128, 4, 0,
                       xown_src(i),
                       cs_own[i * TBP:(i + 1) * TBP, :],
                       HALF + i * TBP, 8 + i, C128, kbs,
                       lambda t, i=i: ckv_own[i * TBP + t * 128:i * TBP + (t + 1) * 128, :],
                       lambda t, i=i: kr_own[i * TBP + t * 128:i * TBP + (t + 1) * 128, :],
                       lambda t, i=i: y_own[i * TBP + t * 128:i * TBP + (t + 1) * 128, :], blk_id=("own", i), next_x=nx)
        S.dma(POOL, C("dma_start", out=st_own, in_=S_f[:]), b_Sf.dsem, reads=[b_Sf], writes=[b_out])

        S.pfx = "S:"
        for s in range(2 if stage >= 7 else 0):
            cache_load(0, ckv_c[s, 0:TBP, :], kr_c[s, 0:TBP, :])
            for p in range(8):
                if p + 1 < 8:
                    cache_load((p + 1) % 2, ckv_c[s, (p + 1) * TBP:(p + 2) * TBP, :], kr_c[s, (p + 1) * TBP:(p + 2) * TBP, :])
                front(False, 128, 4, 0, None, None, p * TBP, p, C128, cache=(None, None, p % 2))
            S.dma(POOL, C("dma_start", out=S_f[:], in_=st_c[s]), b_Sf.dsem, writes=[b_Sf])
            carry_state(1)
            kbs = [(p, p * TBP, 128, 4, T_ZERO, False) for p in range(8)] + [(16, HALF, 64, 1, T_ZERO, False)]
            main_block(64, 1, 1 + s,
                       lambda t, s=s: xsmp[s],
                       cs_smp,
                       HALF, 16, C64, kbs,
                       lambda t, s=s: ckv_so[s],
                       lambda t, s=s: kr_so[s],
                       lambda t, s=s: y_smp[s])
            S.dma(POOL, C("dma_start", out=st_so[s], in_=S_f[:]), b_Sf.dsem, reads=[b_Sf], writes=[b_out])

        S.final_wait(POOL, [b_out])
        S.replay()
        import os
        if os.environ.get("MK_TAGS"):
            import pickle
            pickle.dump(S.pe_tags, open(os.environ["MK_TAGS"], "wb"))
        print("ops", S.n_ops, "waits", S.n_wait, "sems", S.nsem, {k: len(v) for k, v in S.prog.items()})
    return nc


_CACHE = {}


def _prep_shared(inp):
    f = np.float32
    w_in = inp["w_in"][0]
    w_ada = inp["w_ada"][0]
    sh = {}
    wa = w_ada.reshape(16, 128, 96, 128).transpose(2, 1, 0, 3)
    sh["wada"] = np.ascontiguousarray(wa.reshape(24, 4, 128, 16 * 128).transpose(0, 2, 1, 3)).reshape(24, 128, 8192)
    offs = [0, 512, 1024, 1536, 2048, 3072, 3584, 4096, 4160]
    chunks = [slice(0, 512), slice(512, 1024), slice(1024, 1536), slice(1536, 2048), slice(3072, 3584), slice(3584, 4096), slice(4096, 4160)]
    sh["wtm"] = np.stack([_pad(_tm_chunk(w_in, c), 8192) for c in chunks])

    def lhs_tiles(w, col0, ntile):
        kt = w.shape[0] // 128
        sub = w[:, col0:col0 + ntile * 128].reshape(kt, 128, ntile, 128).transpose(2, 1, 0, 3)
        return np.ascontiguousarray(sub)

    rg = lhs_tiles(w_in, 2048, 8)
    sh["wrg"] = np.ascontiguousarray(rg.reshape(2, 4, 128, 2048).transpose(0, 2, 1, 3)).reshape(2, 128, 8192)
    w_uq = inp["w_uq"][0]
    w_uk = inp["w_uk"][0].reshape(512, 1024)
    w_uv = inp["w_uv"][0].reshape(512, 1024)
    uk = lhs_tiles(w_uk, 0, 8)
    uqn = lhs_tiles(np.ascontiguousarray(w_uq[:, :, 0:128]).reshape(512, 1024), 0, 8)
    uqr = np.ascontiguousarray(w_uq[:, :, 128:192]).reshape(512, 512)
    sh["wmla"] = np.stack([
        np.ascontiguousarray(uk.transpose(1, 0, 2, 3)).reshape(128, 4096),
        _tm_chunk(w_uv, slice(0, 1024)),
        np.ascontiguousarray(uqn.transpose(1, 0, 2, 3)).reshape(128, 4096),
        _pad(_tm_chunk(uqr, slice(0, 512)), 4096)])
    wo = lhs_tiles(inp["w_out"][0], 0, 16)
    sh["wout"] = np.ascontiguousarray(wo.reshape(4, 4, 128, 2048).transpose(0, 2, 1, 3)).reshape(4, 128, 8192)
    wg = lhs_tiles(inp["w_gate"][0], 0, 44).reshape(22, 2, 128, 2048)
    wu = lhs_tiles(inp["w_up"][0], 0, 44).reshape(22, 2, 128, 2048)
    gu = np.concatenate([wg, wu], axis=1)
    sh["wgu"] = np.ascontiguousarray(gu.transpose(0, 2, 1, 3)).reshape(22, 128, 8192)
    wd = lhs_tiles(inp["w_down"][0], 0, 16)
    sh["wd"] = wd.reshape(16, 128, 5632)
    dec = _decay_tables()
    sh["ident"] = np.eye(128, dtype=f)
    sh["dt128"], sh["qd128"] = dec[128][0].astype(ml_dtypes.bfloat16), dec[128][1]
    sh["gckv"] = np.ascontiguousarray(np.broadcast_to(inp["g_ckv"][0][None, :], (128, KVL))).astype(f)
    tab = np.zeros((128, NTAB), f)
    tab[:, T_LN1G:T_LN1G + 16] = _fm(inp["ln1_g"][0])
    tab[:, T_LN1B:T_LN1B + 16] = _fm(inp["ln1_b"][0])
    tab[:, T_LN2G:T_LN2G + 16] = _fm(inp["ln2_g"][0])
    tab[:, T_LN2B:T_LN2B + 16] = _fm(inp["ln2_b"][0])
    tab[:, T_GRET:T_GRET + 8] = _fm(inp["g_ret"][0])
    tab[:, T_BRET:T_BRET + 8] = _fm(inp["b_ret"][0])
    tab[:, T_GCQ:T_GCQ + 4] = _fm(inp["g_cq"][0])
    tab[:, T_BADA:T_BADA + 96] = _fm(inp["b_ada"][0])
    tab[:, T_KD128:T_KD128 + 8] = dec[128][2]
    tab[:, T_KD64:T_KD64 + 8] = dec[64][2]
    tab[:, T_CD128:T_CD128 + 4] = dec[128][3]
    tab[:, T_CD64:T_CD64 + 4] = dec[64][3]
    tab[:, T_EPS] = EPS
    tab[:, T_ZERO] = 0.0
    sh["tab"] = tab
    sh["cs_all"] = _rope_table(np.arange(SEQ))
    sh["cs_smp"] = _rope_table(PAST + np.arange(SSEQ))
    return sh


def make_in_maps(inp):
    f = np.float32
    sh = _prep_shared(inp)
    in_maps = []
    pairlay = lambda s: np.ascontiguousarray(s.reshape(4, 2, 64, 128).transpose(1, 2, 0, 3)).reshape(128, 4, 128)
    for c in range(8):
        b, half = c // 2, c % 2
        tab = sh["tab"].copy()
        tab[:, T_VF] = 0.0 if half == 1 else NEG
        tab[:, T_VF + 1] = 1.0 if half == 1 else 0.0
        cs = np.stack([inp["c_prompt"][b], inp["c_sample"][2 * c], inp["c_sample"][2 * c + 1]], axis=1)
        m = {
            "xpre": inp["x_prompt"][b, 0:HALF], "xown": inp["x_prompt"][b, half * HALF:(half + 1) * HALF],
            "xsmp": inp["x_sample"][2 * c:2 * c + 2],
            "cT": np.ascontiguousarray(cs.reshape(16, 128, 3).transpose(1, 0, 2)),
            "tab": tab,
            "cs_pre": sh["cs_all"][0:HALF], "cs_own": sh["cs_all"][half * HALF:(half + 1) * HALF], "cs_smp": sh["cs_smp"],
            "ckv_c": inp["cache_mla_ckv"][0, 2 * c:2 * c + 2], "kr_c": inp["cache_mla_krope"][0, 2 * c:2 * c + 2],
            "st_c": np.stack([pairlay(inp["state_ret"][0, 2 * c + s]) for s in range(2)]),
        }
        for k in ("ident", "dt128", "qd128", "gckv", "wada", "wtm", "wrg", "wmla", "wout", "wgu", "wd"):
            m[k] = sh[k]
        in_maps.append({k: (np.ascontiguousarray(v) if k == "dt128" else np.ascontiguousarray(v, dtype=f)) for k, v in m.items()})
    return in_maps


def assemble(R):
    f = np.float32
    unpair = lambda s: np.ascontiguousarray(s.reshape(2, 64, 4, 128).transpose(2, 0, 1, 3)).reshape(8, 64, 128)
    yp = np.zeros((NB, SEQ, D), f)
    ckvp = np.zeros((1, NB, SEQ, KVL), f)
    krp = np.zeros((1, NB, SEQ, ROPE), f)
    rsp = np.zeros((1, NB, H, RDK, RDV), f)
    ys = np.zeros((NSB, SSEQ, D), f)
    ckvs = np.zeros((1, NSB, SSEQ, KVL), f)
    krs = np.zeros((1, NSB, SSEQ, ROPE), f)
    rss = np.zeros((1, NSB, H, RDK, RDV), f)
    for c in range(8):
        b, half = c // 2, c % 2
        sl = slice(half * HALF, (half + 1) * HALF)
        yp[b, sl] = R[c]["y_own"]
        ckvp[0, b, sl] = R[c]["ckv_own"]
        krp[0, b, sl] = R[c]["kr_own"]
        if half == 1:
            rsp[0, b] = unpair(R[c]["st_own"])
        ys[2 * c:2 * c + 2] = R[c]["y_smp"]
        ckvs[0, 2 * c:2 * c + 2] = R[c]["ckv_so"]
        krs[0, 2 * c:2 * c + 2] = R[c]["kr_so"]
        for s in range(2):
            rss[0, 2 * c + s] = unpair(R[c]["st_so"][s])
    return (yp, ys, ckvp, krp, rsp, ckvs, krs, rss)


def kernel(**inp):
    inp = {k: np.asarray(v) for k, v in inp.items()}
    in_maps = make_in_maps(inp)
    if "nc" not in _CACHE:
        _CACHE["nc"] = build_program()
    nc = _CACHE["nc"]
    res = run_bass_kernel_spmd(nc, in_maps, core_ids=list(range(8)))
    return assemble(res.results)
```

```python
import numpy as np
import ml_dtypes
import concourse.bass as bass
import concourse.mybir as mybir
from concourse.bass_utils import run_bass_kernel_spmd
from contextlib import ExitStack

F32 = mybir.dt.float32
BF16 = mybir.dt.bfloat16
ALU = mybir.AluOpType
AF = mybir.ActivationFunctionType

PE, ACT, DVE, POOL, SP = "pe", "act", "dve", "pool", "sp"


def C(name, *a, **k):
    return (name, a, k)

D = 2048
NB = 4
SEQ = 8192
NSB = 16
SSEQ = 64
PAST = 4096
H = 8
RDK = 64
RDV = 128
NOPE = 128
ROPE = 64
QL = 512
KVL = 512
DFF = 5632
ALPHA = 2.0 ** 0.25
EPS = 1e-5
MLA_SCALE = 192.0 ** -0.5
HALF = 4096
TBP = 512
NBLK = HALF // TBP
NEG = -30000.0


class Buf:
    __slots__ = ("name", "w", "r", "aliases", "dsem", "excl")

    def __init__(self, name, dsem=None):
        self.name = name
        self.w = None
        self.r = {}
        self.aliases = []
        self.dsem = dsem
        self.excl = False


class DmaSem:
    __slots__ = ("sem", "cnt", "key")

    def __init__(self, sem, key):
        self.sem = sem
        self.cnt = 0
        self.key = key


class Sched:
    def __init__(self, nc, stack, same_engine_sync=True):
        self.nc = nc
        self.stack = stack
        self.same_engine_sync = same_engine_sync
        self.sems = {}
        self.prog = {}
        self.cnt = {}
        self.seen = {}
        self.nsem = 0
        for k in (PE, ACT, DVE, POOL, SP):
            self.prog[k] = []
            self.cnt[k] = 0
            self.seen[k] = {}
            if k != SP:
                self.sems[k] = stack.enter_context(nc.semaphore("prog_" + k))
                self.nsem += 1
        self.n_wait = 0
        self.n_ops = 0
        self.dsems = []
        self.tag = ""
        self.pfx = ""
        self.pe_tags = []

    def dma_sem(self, name):
        key = "d_" + name
        self.sems[key] = self.stack.enter_context(self.nc.semaphore(key))
        self.nsem += 1
        ds = DmaSem(self.sems[key], key)
        self.dsems.append(ds)
        return ds

    def buf(self, name, dma=False):
        return Buf(name, self.dma_sem(name) if dma else None)

    def _collect(self, reads, writes, ek=None):
        deps = {}

        def add(ev):
            if ev is None:
                return
            k, v = ev
            if deps.get(k, 0) < v:
                deps[k] = v

        for b in reads:
            add(b.w)
            if b.excl:
                for k, v in b.r.items():
                    if k != ek:
                        add((k, v))
            for a in b.aliases:
                add(a.w)
        for b in writes:
            add(b.w)
            for k, v in b.r.items():
                add((k, v))
            for a in b.aliases:
                add(a.w)
                for k, v in a.r.items():
                    add((k, v))
        return deps

    def _waits(self, ek, deps, self_sync=False):
        seen = self.seen[ek]
        waits = []
        for k, v in deps.items():
            if k == ek and (ek == PE or not self.same_engine_sync) and not self_sync:
                continue
            if seen.get(k, 0) < v:
                seen[k] = v
                waits.append((k, v))
        self.n_wait += len(waits)
        return waits

    def _update(self, ev, reads, writes):
        k, v = ev
        for b in reads:
            if b.r.get(k, 0) < v:
                b.r[k] = v
        for b in writes:
            b.w = ev
            b.r = {}

    def op(self, ek, fn, reads=(), writes=(), signal=True, self_sync=False):
        if ek == PE:
            self.pe_tags.append(self.pfx + self.tag)
        deps = self._collect(reads, writes, ek)
        waits = self._waits(ek, deps, self_sync)
        if signal:
            self.cnt[ek] += 1
            ev = (ek, self.cnt[ek])
            self.prog[ek].append((waits, fn, (ek, 1)))
        else:
            ev = (ek, self.cnt[ek] + 1)
            self.prog[ek].append((waits, fn, None))
        self._update(ev, reads, writes)
        self.n_ops += 1
        return ev

    def dma(self, qk, fn, sem, reads=(), writes=(), skip_own=False):
        deps = self._collect(reads, writes)
        if skip_own:
            deps.pop(sem.key, None)
        waits = self._waits(qk, deps)
        sem.cnt += 16
        ev = (sem.key, sem.cnt)
        self.prog[qk].append((waits, fn, (sem.key, 16)))
        self._update(ev, reads, writes)
        self.n_ops += 1
        return ev

    def final_wait(self, ek, bufs):
        deps = self._collect((), bufs)
        for ds in self.dsems:
            if ds.cnt > 0:
                deps[ds.key] = max(deps.get(ds.key, 0), ds.cnt)
        waits = self._waits(ek, deps)
        self.prog[ek].append((waits, None, None))

    def replay(self):
        nc = self.nc
        sems = self.sems
        prog = self.prog

        def run(eng, items):
            for waits, fn, inc in items:
                for k, v in waits:
                    eng.wait_ge(sems[k], v)
                if fn is None:
                    continue
                ins = getattr(eng, fn[0])(*fn[1], **fn[2])
                if inc is not None:
                    ins.then_inc(sems[inc[0]], inc[1])

        with nc.Block() as block:
            @block.tensor
            def _(e):
                run(e, prog[PE])

            @block.scalar
            def _(e):
                run(e, prog[ACT])

            @block.vector
            def _(e):
                run(e, prog[DVE])

            @block.gpsimd
            def _(e):
                run(e, prog[POOL])

            @block.sync
            def _(e):
                run(e, prog[SP])


class Stream:
    def __init__(self, S, tiles, name, slack=0):
        self.S = S
        self.slack = slack
        self.tiles = tiles
        self.bufs = [S.buf(f"{name}{i}", dma=True) for i in range(len(tiles))]
        self.sw_sems = [S.dma_sem(f"{name}{i}_sw") for i in range(len(tiles))]
        self.plan = []
        self.issued = 0
        self.taken = 0

    def extend(self, loads):
        self.plan.extend(loads)

    def next(self):
        R = len(self.tiles)
        while self.issued < len(self.plan) and self.issued < max(self.taken + 1, self.taken + R - self.slack):
            i = self.issued
            s = i % R
            self.plan[i](self.tiles[s], self.bufs[s], self.sw_sems[s])
            self.issued += 1
        s = self.taken % R
        assert self.taken < self.issued
        self.taken += 1
        return self.tiles[s], self.bufs[s]


def _tm_chunk(w, cols):
    sub = w[:, cols]
    kt = sub.shape[0] // 128
    return np.ascontiguousarray(sub.reshape(kt, 128, sub.shape[1]).transpose(1, 0, 2)).reshape(128, -1)


def _pad(a, L):
    if a.shape[1] == L:
        return a
    out = np.zeros((a.shape[0], L), a.dtype)
    out[:, :a.shape[1]] = a
    return out


def _fm(v):
    return np.ascontiguousarray(v.reshape(-1, 128).T)


def _decay_tables():
    h = np.arange(H, dtype=np.float64)
    logg = np.log1p(-np.exp2(-5.0 - h))
    out = {}
    for C in (128, 64):
        i = np.arange(C, dtype=np.float64)
        diff = i[None, :] - i[:, None]
        dt = np.where(diff[:, None, :] >= 0, np.exp(np.maximum(diff, 0)[:, None, :] * logg[None, :, None]), 0.0)
        DT = np.zeros((128, H, C), np.float32)
        DT[:C] = dt
        qd = np.zeros((128, 4, C), np.float32)
        for m in range(4):
            for hh in range(2):
                qd[hh * 64:(hh + 1) * 64, m, :] = (RDK ** -0.5) * np.exp((i + 1.0) * logg[2 * m + hh])[None, :]
        kd = np.zeros((128, H), np.float32)
        kd[:C] = np.exp((C - 1.0 - i)[:, None] * logg[None, :])
        cd = np.zeros((128, 4), np.float32)
        for m in range(4):
            for hh in range(2):
                cd[hh * 64:(hh + 1) * 64, m] = np.exp(C * logg[2 * m + hh])
        out[C] = (DT, qd, kd, cd)
    return out


def _rope_table(pos):
    half = 32
    inv = (np.float32(10000.0) ** (-np.arange(half, dtype=np.float32) / np.float32(half))).astype(np.float32)
    ang = (pos.astype(np.float32)[:, None] * inv[None, :]).astype(np.float32)
    return np.concatenate([np.cos(ang.astype(np.float64)), np.sin(ang.astype(np.float64))], axis=1).astype(np.float32)


T_LN1G, T_LN1B, T_LN2G, T_LN2B = 0, 16, 32, 48
T_GRET, T_BRET, T_GCQ = 64, 72, 80
T_BADA = 84
T_VF = 180
T_KD128, T_KD64 = 182, 190
T_CD128, T_CD64 = 198, 202
T_EPS, T_ZERO = 206, 207
NTAB = 208


def build_program(stage=99, sub=99):
    nc = bass.Bass("TRN2", target_bir_lowering=False)

    def din(name, shape, dt=F32):
        return nc.dram_tensor(name, list(shape), dt, kind="ExternalInput").ap()

    def dout(name, shape, dt=F32):
        return nc.dram_tensor(name, list(shape), dt, kind="ExternalOutput").ap()

    def dscr(name, shape, dt=BF16):
        return nc.dram_tensor(name, list(shape), dt, kind="Internal").ap()

    xpre = din("xpre", [HALF, D])
    xown = din("xown", [HALF, D])
    xsmp = din("xsmp", [2, SSEQ, D])
    cT_d = din("cT", [128, 16, 3])
    tab_d = din("tab", [128, NTAB])
    cs_pre = din("cs_pre", [HALF, 64])
    cs_own = din("cs_own", [HALF, 64])
    cs_smp = din("cs_smp", [SSEQ, 64])
    ckv_c = din("ckv_c", [2, PAST, KVL])
    kr_c = din("kr_c", [2, PAST, ROPE])
    st_c = din("st_c", [2, 128, 4, 128])
    ident_d = din("ident", [128, 128])
    dt128_d = din("dt128", [128, H, 128], BF16)
    qd128_d = din("qd128", [128, 4, 128])
    gckv_d = din("gckv", [128, KVL])
    wada_d = din("wada", [24, 128, 8192])
    wtm_d = din("wtm", [7, 128, 8192])
    wrg_d = din("wrg", [2, 128, 8192])
    wmla_d = din("wmla", [4, 128, 4096])
    wout_d = din("wout", [4, 128, 8192])
    wgu_d = din("wgu", [22, 128, 8192])
    wd_d = din("wd", [16, 128, 5632])

    y_own = dout("y_own", [HALF, D])
    y_smp = dout("y_smp", [2, SSEQ, D])
    ckv_own = dout("ckv_own", [HALF, KVL])
    kr_own = dout("kr_own", [HALF, ROPE])
    st_own = dout("st_own", [128, 4, 128])
    ckv_so = dout("ckv_so", [2, SSEQ, KVL])
    kr_so = dout("kr_so", [2, SSEQ, ROPE])
    st_so = dout("st_so", [2, 128, 4, 128])

    wtm_s = dscr("wtm_s", [7, 128, 8192])
    wrg_s = dscr("wrg_s", [2, 128, 8192])
    wmla_s = dscr("wmla_s", [4, 128, 4096])
    wout_s = dscr("wout_s", [4, 128, 8192])
    wgu_s = dscr("wgu_s", [22, 128, 8192])
    wd_s = dscr("wd_s", [16, 128, 5632])
    NKB = 17
    kscr = dscr("kscr", [NKB, 128, H, TBP])
    vscr = dscr("vscr", [NKB, 128, H, 4, 128])

    with ExitStack() as st:
        S = Sched(nc, st)
        nalloc = [0]

        def T(name, shape, dt):
            return st.enter_context(nc.sbuf_tensor("s_" + name, list(shape), dt))

        x_stage = [T(f"x_stage{i}", [128, D], F32) for i in range(2)]
        b_xs = [S.buf(f"x_stage{i}", dma=True) for i in range(2)]
        xs_hw = [S.dma_sem(f"x_stage{i}_hw") for i in range(2)]

        xaT = T("xaT", [128, 16, TBP], F32); b_xaT = [S.buf(f"xaT{i}") for i in range(16)]
        actb = T("actb", [128, 16, TBP], BF16); b_hT = [S.buf(f"hT{i}") for i in range(16)]
        U = T("U", [128, 22528], BF16)
        uo = [0]

        def carve(n_elems, shape):
            a = U[:, uo[0]:uo[0] + n_elems]
            uo[0] += n_elems
            return a

        kdec_tok = U[:, 0:2048].rearrange("p (t c) -> p t c", t=4)
        v_tok = U[:, 2048:6144].rearrange("p (t c) -> p t c", t=4)
        qT_ret = U[:, 6144:8192].rearrange("p (m c) -> p m c", m=4)
        kT_ret = U[:, 8192:10240].rearrange("p (m c) -> p m c", m=4)
        qdT_ret = U[:, 10240:12288].rearrange("p (m c) -> p m c", m=4)
        rgT = U[:, 12288:16384].rearrange("p (m c) -> p m c", m=8)
        cqnT = U[:, 16384:18432].rearrange("p (m c) -> p m c", m=4)
        ckvT = U[:, 18432:20480].rearrange("p (m c) -> p m c", m=4)
        knT_blk = U[:, 0:4096].rearrange("p (h c) -> p h c", h=8)
        v_blk = U[:, 4096:8192].rearrange("p (h t e) -> p h t e", h=8, t=4)
        qnT = U[:, 0:4096].rearrange("p (h c) -> p h c", h=8)
        qrT = U[:, 4096:8192].rearrange("p (h c) -> p h c", h=8)
        actT = U[:, 0:44 * 512].rearrange("p (m c) -> p m c", m=44)
        b_U = S.buf("U")
        b_kdec = S.buf("kdec_tok"); b_vtok = S.buf("v_tok"); b_qT = S.buf("qT_ret"); b_kT = S.buf("kT_ret")
        b_qdT = S.buf("qdT_ret"); b_rg = S.buf("rgT"); b_cqn = S.buf("cqnT"); b_ckvT = S.buf("ckvT")
        b_knb = S.buf("knT_blk", dma=True); b_vb = S.buf("v_blk", dma=True)
        b_qn = S.buf("qnT"); b_qr = S.buf("qrT"); b_act = [S.buf(f"actT{i}") for i in range(44)]; b_actw = S.buf("actT_all")
        retb = [b_kdec, b_vtok, b_qT, b_kT, b_qdT]
        ag = [[b_knb, b_vb], [b_kdec, b_vtok, b_qT], [b_qn, b_qr]]
        for gi_, g_ in enumerate(ag):
            for x_ in g_:
                for gj_, h_ in enumerate(ag):
                    if gi_ != gj_:
                        x_.aliases.extend(h_)
        allA = retb + [b_rg, b_cqn, b_ckvT, b_knb, b_vb, b_qn, b_qr]
        b_actw.aliases = list(allA)
        for a_ in allA:
            a_.aliases.append(b_actw)

        krT_all = T("krT_all", [128, 8192 + 128], BF16); b_krT = S.buf("krT_all")
        WR = 2
        w_ring = [T(f"w_ring{i}", [128, 8192], BF16) for i in range(WR)]
        WS = Stream(S, w_ring, "w_ring")
        KR = 3
        kv_ring = [T(f"kv_ring{i}", [128, 1024], BF16) for i in range(KR)]
        KS = Stream(S, kv_ring, "kv_ring", slack=1)
        xa_bf = xaT[:, :, :].rearrange("p k t -> p (k t)").bitcast(BF16)
        ada_ring = [xa_bf[:, 0:8192], xa_bf[:, 8192:16384], U[:, 8192:16384]]
        AS = Stream(S, ada_ring, "ada_ring")
        for q_ in b_xaT:
            q_.aliases.extend(AS.bufs[0:2])
        for q_ in (b_kT, b_qdT, b_rg):
            q_.aliases.append(AS.bufs[2])
        pT = [T(f"pT{i}", [128, 2, 512], BF16) for i in range(2)]
        b_pT = [S.buf(f"pT{i}") for i in range(2)]
        NTMP = 3
        tmpf = [T(f"tmpf{i}", [128, 512], F32) for i in range(NTMP)]
        b_tmpf = [S.buf(f"tmpf{i}") for i in range(NTMP)]
        NTB = 4
        tmpb = [T(f"tmpb{i}", [128, 512], BF16) for i in range(NTB)]
        b_tmpb = [S.buf(f"tmpb{i}") for i in range(NTB)]
        mean_sb = T("mean_sb", [128, 512], F32); b_mean = S.buf("mean_sb")
        rstd_sb = T("rstd_sb", [128, 512], F32); b_rstd = S.buf("rstd_sb")
        ckv_stg = [T(f"ckv_stg{i}", [128, 512], F32) for i in range(1)]
        b_ckv_stg = [S.buf(f"ckv_stg{i}", dma=True) for i in range(1)]
        kr_stg = [T(f"kr_stg{i}", [128, 64], F32) for i in range(1)]
        b_kr_stg = [S.buf(f"kr_stg{i}", dma=True) for i in range(1)]
        tok_b = [T(f"tok_b{i}", [128, 512], BF16) for i in range(2)]
        b_tok_b = [S.buf(f"tok_b{i}") for i in range(2)]
        sstat = T("sstat", [128, 8], F32); b_sstat = S.buf("sstat")
        cs_blk = [T(f"cs_blk{i}", [128, 4, 64], F32) for i in range(1)]
        b_csb = [S.buf(f"cs_blk{i}", dma=True) for i in range(1)]
        cur_cs = [0]
        cstage_raw = T("cstage_raw", [128, 2304], BF16)
        cache_stage = [cstage_raw[:, :].rearrange("p (t c) -> p t c", t=4),
                       x_stage[1][:, :].bitcast(BF16)[:, 0:2304].rearrange("p (t c) -> p t c", t=4)]
        b_cstage = [S.buf("cache_stage0", dma=True), b_xs[1]]
        ystg_all = cstage_raw[:, 0:2048].bitcast(F32)
        stg = [ystg_all[:, 0:512], ystg_all[:, 512:1024]]
        b_stg = [S.buf(f"ystg{i}", dma=True) for i in range(2)]
        for q_ in b_stg:
            q_.aliases.append(b_cstage[0])
            b_cstage[0].aliases.append(q_)
        S_f = T("S_f", [128, 4, 128], F32); b_Sf = S.buf("S_f", dma=True)
        S_bf = T("S_bf", [128, 5, 4, 128], BF16); b_Sbf = [S.buf(f"S_bf{i}") for i in range(5)]
        tab = T("tab", [128, NTAB], F32); b_const = S.buf("const", dma=True)
        ident = T("ident", [128, 128], F32)
        identb = T("identb", [128, 128], BF16)
        ones_b = T("ones_b", [128, 128], BF16)
        dt128 = T("dt128", [128, H, 128], BF16)
        qd128 = T("qd128", [128, 4, 128], F32)
        gckv = T("gckv", [128, KVL], F32)
        cT = mean_sb[:, 0:48].rearrange("p (k s) -> p k s", s=3)
        cTb = T("cTb", [128, 16, 3], BF16)
        modT = T("modT", [128, 96, 3], F32); b_mod = S.buf("modT")
        SC1P = T("SC1P", [128, 3, 16], F32)
        G2T = T("G2T", [128, 3, 16], F32)
        B2T = T("B2T", [128, 3, 16], F32)
        AG1 = T("AG1", [128, 16], F32)
        AB1 = T("AB1", [128, 16], F32)
        b_seqtab = S.buf("seqtab")

        def P(name):
            return st.enter_context(nc.psum_tensor(name, [128, 1024], F32))

        DB = [P(f"DB{i}") for i in range(4)]
        b_DB = [[S.buf(f"DB{i}_{j}") for j in range(2)] for i in range(4)]
        for r_ in b_DB:
            for q_ in r_:
                q_.excl = True

        def bank(i, j):
            return DB[i][:, j * 512:(j + 1) * 512]

        mmrot = [0]

        ROT = [(0, 0), (0, 1), (1, 0), (1, 1), (3, 0), (3, 1)]
        rotN = [4]

        def next_bank():
            r = mmrot[0] % rotN[0]
            mmrot[0] = (r + 1) % rotN[0]
            return ROT[r]

        tmpi = {}

        def rot(idx, n):
            v = tmpi.get((idx, n), 0)
            tmpi[(idx, n)] = (v + 1) % n
            return v

        b_wscr = {}
        b_kscr = [S.buf(f"kscr{i}") for i in range(NKB)]
        b_vscr = [S.buf(f"vscr{i}") for i in range(NKB)]
        b_out = S.buf("outputs")
        d2d_sems = [S.dma_sem("d2dA"), S.dma_sem("d2dB"), S.dma_sem("d2dC")]

        def cload(dst, src):
            S.dma(SP, C("dma_start", out=dst, in_=src), b_const.dsem, writes=[b_const])

        cload(tab[:], tab_d)
        S.dma(SP, C("dma_start", out=cT, in_=cT_d), b_const.dsem, writes=[b_const, b_mean])
        cload(ident[:], ident_d)
        cload(dt128[:], dt128_d)
        cload(qd128[:], qd128_d)
        cload(gckv[:], gckv_d)
        b_c2 = S.buf("const2")
        S.op(DVE, C("tensor_copy", out=identb[:], in_=ident[:]), reads=[b_const], writes=[b_c2])
        S.op(DVE, C("memset", ones_b[:], 1.0), writes=[b_c2])
        S.op(ACT, C("activation", out=cTb[:], in_=cT, func=AF.Silu), reads=[b_const, b_mean], writes=[b_c2])
        CONST = [b_const, b_c2]

        def slab_from_scratch(scr, key, i, L):
            def load(tile, buf, swsem):
                S.dma(SP, C("dma_start", out=tile[:, 0:L], in_=scr[i]), buf.dsem,
                      reads=[b_wscr[(key, i)]], writes=[buf])
            return load

        def slab_ada(i):
            def load(tile, buf, swsem):
                S.dma(POOL, C("dma_start", out=tile[:, :], in_=wada_d[i]), swsem, writes=[buf])
            return load

        groups = {"tm": (wtm_d, wtm_s, 7, 8192), "rg": (wrg_d, wrg_s, 2, 8192), "mla": (wmla_d, wmla_s, 4, 4096),
                  "out": (wout_d, wout_s, 4, 8192), "gu": (wgu_d, wgu_s, 22, 8192), "d": (wd_d, wd_s, 16, 5632)}

        conv_groups = [[], [], []]

        def convert(key, idxs, grp):
            src, dst, n, L = groups[key]
            for i in idxs:
                b_wscr[(key, i)] = S.buf(f"wscr_{key}{i}")
                S.dma(POOL, C("dma_start", out=dst[i], in_=src[i]), d2d_sems[grp], writes=[b_wscr[(key, i)]])
                conv_groups[grp].append(b_wscr[(key, i)])

        def seal(grp):
            for b_ in conv_groups[grp]:
                b_.w = (d2d_sems[grp].key, d2d_sems[grp].cnt)

        def L_(key, i):
            return slab_from_scratch(groups[key][1], key, i, groups[key][3])

        PREFIX_SLABS = [("tm", 5), ("tm", 6), ("mla", 0), ("mla", 1), ("tm", 1), ("tm", 2), ("tm", 3)]
        MAIN_SLABS = ([("tm", 4), ("tm", 5), ("tm", 6), ("mla", 0), ("mla", 1), ("tm", 0), ("tm", 1), ("tm", 2), ("tm", 3),
                       ("rg", 0), ("rg", 1), ("mla", 2), ("mla", 3)]
                      + [("out", i) for i in range(4)] + [("gu", i) for i in range(22)] + [("d", i) for i in range(16)])
        convert("tm", [5, 6], 0)
        convert("mla", [0, 1], 0)
        convert("tm", [1, 2, 3], 0)
        seal(0)
        AS.extend([slab_ada(i) for i in range(24)])

        def ada_part(f0, f1):
            S.tag = "ada"
            bk = (3, 1)
            for sl in range(f0 // 4, f1 // 4):
                wt, wb = AS.next()
                for mm in range(4):
                    f = sl * 4 + mm
                    for k in range(16):
                        S.op(PE, C("matmul",
                            bank(*bk)[:, f * 3:(f + 1) * 3], lhsT=wt[:, (mm * 16 + k) * 128:(mm * 16 + k + 1) * 128],
                            rhs=cTb[:, k, :], start=(k == 0), stop=(k == 15)),
                            reads=[wb] + CONST, writes=[b_DB[3][1]], signal=(k == 15))
            S.op(DVE, C("tensor_tensor",
                out=modT[:, f0:f1, :], in0=bank(*bk)[:, f0 * 3:f1 * 3].rearrange("p (f s) -> p f s", s=3),
                in1=tab[:, T_BADA + f0:T_BADA + f1].unsqueeze(2).to_broadcast([128, f1 - f0, 3]), op=ALU.add),
                reads=[b_DB[3][1]] + CONST, writes=[b_mod])

        ada_part(0, 32)
        S.op(DVE, C("tensor_scalar", out=SC1P[:].rearrange("p s f -> p f s"), in0=modT[:, 16:32, :], scalar1=1.0, scalar2=None, op0=ALU.add),
             reads=[b_mod], writes=[b_seqtab])
        conv_pending = {1: [("tm", 4), ("tm", 0), ("rg", 0), ("rg", 1), ("mla", 2), ("mla", 3)] + [("out", i) for i in range(4)],
                        2: [("gu", i) for i in range(22)] + [("d", i) for i in range(16)]}

        def convert_some(n, grp=1):
            lst = conv_pending[grp]
            if not lst:
                return
            for _ in range(n):
                if lst:
                    k_, i_ = lst.pop(0)
                    convert(k_, [i_], grp)
            if not lst:
                seal(grp)

        def mm_group(out_ap, obuf, pairs, reads):
            n = len(pairs)
            for i, (l, r) in enumerate(pairs):
                S.op(PE, C("matmul", out_ap, lhsT=l, rhs=r, start=(i == 0), stop=(i == n - 1)),
                     reads=reads, writes=[obuf], signal=(i == n - 1))

        def rope(src, G, TT, cs, b_cs_, reads, outs):
            s3 = src.rearrange("p (g c) -> p g c", c=64)
            x1 = s3[:, :, 0:32]
            x2 = s3[:, :, 32:64]
            cosb = cs[0:TT, 0:32].unsqueeze(1).to_broadcast([TT, G, 32])
            sinb = cs[0:TT, 32:64].unsqueeze(1).to_broadcast([TT, G, 32])
            ia, ib = rot(0, NTMP), rot(0, NTMP)
            ta = tmpf[ia][0:TT, 0:G * 32].rearrange("p (g c) -> p g c", c=32)
            tb = tmpf[ia][0:TT, 256:256 + G * 32].rearrange("p (g c) -> p g c", c=32)
            tc = tmpf[ib][0:TT, 0:G * 32].rearrange("p (g c) -> p g c", c=32)
            td = tmpf[ib][0:TT, 256:256 + G * 32].rearrange("p (g c) -> p g c", c=32)
            rd = reads + [b_cs_]
            S.op(DVE, C("tensor_tensor", out=ta, in0=x1, in1=cosb, op=ALU.mult), reads=rd, writes=[b_tmpf[ia]])
            S.op(DVE, C("tensor_tensor", out=tb, in0=x2, in1=sinb, op=ALU.mult), reads=rd, writes=[b_tmpf[ia]])
            S.op(DVE, C("tensor_tensor", out=tc, in0=x2, in1=cosb, op=ALU.mult), reads=rd, writes=[b_tmpf[ib]])
            S.op(DVE, C("tensor_tensor", out=td, in0=x1, in1=sinb, op=ALU.mult), reads=rd, writes=[b_tmpf[ib]])
            for half, (pa, pb_, opc, bufx) in enumerate(((ta, tb, ALU.subtract, b_tmpf[ia]), (tc, td, ALU.add, b_tmpf[ib]))):
                for o_ in outs:
                    if len(o_) == 2:
                        dst, dbuf = o_
                        d3 = dst.rearrange("p (g c) -> p g c", c=64)[:, :, half * 32:(half + 1) * 32]
                        S.op(DVE, C("tensor_tensor", out=d3, in0=pa, in1=pb_, op=opc), reads=[bufx], writes=[dbuf])
                    else:
                        dfn, g0, g1, dbuf = o_
                        S.op(DVE, C("tensor_tensor", out=dfn(half), in0=pa[:, g0:g1, :], in1=pb_[:, g0:g1, :], op=opc), reads=[bufx], writes=[dbuf])

        def transposes_bf(src_tile, src_buf, TT, nblk, dst_fn, dst_bufs, eng_fn):
            bi = next_bank()
            pb = bank(*bi).bitcast(BF16)[:, 0:nblk * TT].rearrange("p (n t) -> p n t", t=TT)
            for n in range(nblk):
                S.op(PE, C("transpose", pb[:, n, :], src_tile[0:TT, n * 128:(n + 1) * 128], identb[0:TT, 0:TT]),
                     reads=[src_buf] + CONST, writes=[b_DB[bi[0]][bi[1]]], signal=(n == nblk - 1))
            eng_fn(pb, b_DB[bi[0]][bi[1]])

        def rmsnorm_rstd(ps, pbuf, TT):
            j = rot(1, NTMP)
            c = rot(2, 4)
            S.op(ACT, C("activation", out=tmpf[j][0:TT, :], in_=ps, func=AF.Square),
                 reads=[pbuf], writes=[b_tmpf[j]])
            S.op(DVE, C("reduce_sum", out=sstat[0:TT, 2 * c:2 * c + 1], in_=tmpf[j][0:TT, :], axis=mybir.AxisListType.X),
                 reads=[b_tmpf[j]], writes=[b_sstat])
            S.op(ACT, C("activation", out=sstat[0:TT, 2 * c + 1:2 * c + 2], in_=sstat[0:TT, 2 * c:2 * c + 1], func=AF.Sqrt,
                                             scale=1.0 / 512.0, bias=tab[0:TT, T_EPS:T_EPS + 1]),
                 reads=[b_sstat] + CONST, writes=[b_sstat])
            S.op(DVE, C("reciprocal", out=sstat[0:TT, 2 * c + 1:2 * c + 2], in_=sstat[0:TT, 2 * c + 1:2 * c + 2]),
                 reads=[b_sstat], writes=[b_sstat])
            return sstat[0:TT, 2 * c + 1:2 * c + 2]

        x_pref = [None]

        def cache_load(ci, ckv_src, kr_src):
            S.dma(POOL, C("dma_start", out=cache_stage[ci][:, :, 0:512], in_=ckv_src.rearrange("(t p) c -> p t c", p=128)), b_cstage[ci].dsem,
                  writes=[b_cstage[ci]])
            S.dma(POOL, C("dma_start", out=cache_stage[ci][:, :, 512:576], in_=kr_src.rearrange("(t p) c -> p t c", p=128)), b_cstage[ci].dsem,
                  writes=[b_cstage[ci]], skip_own=True)

        def front(full, TT, NT, seq, x_src, cs_rows, key_col0, kblk, C_tabs, ckv_out=None, kr_out=None,
                  cache=None, blk_id=None, next_x=None):
            TB = TT * NT
            DT, QD, KDc, CDc = C_tabs

            def kvgen():
                S.tag = "kvgen"
                kvgen_impl(TT, NT, kblk)
                S.tag = "front_win"
            S.tag = "front_x"
            rotN[0] = 6
            if cache is None:
                cur_cs[0] = 0
                cb = cur_cs[0]
                S.dma(POOL, C("dma_start", out=cs_blk[cb][0:TT, 0:NT, :], in_=cs_rows.rearrange("(t p) c -> p t c", p=TT)), b_csb[cb].dsem,
                      writes=[b_csb[cb]])
                for t in range(NT):
                    sx = t % 2
                    if not (x_pref[0] is not None and x_pref[0] == blk_id and t < 2):
                        S.dma(SP, C("dma_start", out=x_stage[sx][0:TT, :], in_=x_src(t)), xs_hw[sx],
                              writes=[b_xs[sx]])
                    for g in range(4):
                        bi = next_bank()
                        pb = bank(*bi)[:, 0:4 * TT].rearrange("p (k t) -> p k t", t=TT)
                        for kk in range(4):
                            k = g * 4 + kk
                            S.op(PE, C("transpose", pb[:, kk, :], x_stage[sx][0:TT, k * 128:(k + 1) * 128], ident[0:TT, 0:TT]),
                                 reads=[b_xs[sx]] + CONST, writes=[b_DB[bi[0]][bi[1]]], signal=(kk == 3))
                        pbuf = b_DB[bi[0]][bi[1]]
                        if full:
                            S.op(ACT, C("activation", out=xaT[:, g * 4:g * 4 + 4, t * TT:(t + 1) * TT], in_=pb, func=AF.Identity, scale=ALPHA),
                                 reads=[pbuf], writes=b_xaT[g * 4:g * 4 + 4])
                        j = rot(1, NTMP)
                        tv = tmpf[j][:, 0:4 * TT].rearrange("p (k t) -> p k t", t=TT)
                        S.op(DVE, C("tensor_tensor", out=tv, in0=pb, in1=SC1P[:, seq, g * 4:g * 4 + 4].unsqueeze(2).to_broadcast([128, 4, TT]), op=ALU.mult),
                             reads=[pbuf, b_seqtab], writes=[b_tmpf[j]])
                        S.op(DVE, C("tensor_tensor", out=actb[:, g * 4:g * 4 + 4, t * TT:(t + 1) * TT], in0=tv,
                                                                              in1=modT[:, g * 4:g * 4 + 4, seq].unsqueeze(2).to_broadcast([128, 4, TT]), op=ALU.add),
                             reads=[b_tmpf[j], b_mod], writes=b_hT[g * 4:g * 4 + 4])
                x_pref[0] = None
                if next_x is not None:
                    nid, nsrc, nTT, nNT = next_x
                    for t in range(min(2, nNT)):
                        S.dma(SP, C("dma_start", out=x_stage[t][0:nTT, :], in_=nsrc(t)), xs_hw[t], writes=[b_xs[t]])
                    x_pref[0] = nid
                S.tag = "front_win"
                chunks = [4, 5, 6, "kv", 0, 1, 2, 3] if full else [5, 6, "kv", 1, 2, 3]
                if full and sub < 0:
                    chunks = chunks[:-sub - 1]
                for c in chunks:
                    if c == "kv":
                        kvgen()
                        continue
                    wt, wb = WS.next()
                    ncol = 64 if c == 6 else 512
                    w3 = wt[:, 0:16 * ncol].rearrange("p (k n) -> p k n", n=ncol)
                    pend = []
                    for t in range(NT):
                        bi = next_bank()
                        ps = bank(*bi)[0:TT, 0:ncol]
                        pbuf = b_DB[bi[0]][bi[1]]
                        mm_group(ps, pbuf, [(actb[:, k, t * TT:(t + 1) * TT], w3[:, k, :]) for k in range(16)], b_hT + [wb])
                        def post(t=t, ps=ps, pbuf=pbuf, c=c):
                            cst = cs_blk[cb][:, t, :]
                            if c == 0:
                                j = rot(3, 2)
                                rope(ps, 8, TT, cst, b_csb[cb], [pbuf], [(tok_b[j][0:TT, :], b_tok_b[j])])

                                def ev(pb, pbb, t=t):
                                    S.op(ACT, C("activation", out=qT_ret[:, :, t * TT:(t + 1) * TT], in_=pb, func=AF.Identity, scale=RDK ** -0.5),
                                         reads=[pbb], writes=[b_qT])
                                    S.op(DVE, C("tensor_tensor", out=qdT_ret[:, :, t * TT:(t + 1) * TT], in0=pb, in1=QD[:, :, 0:TT], op=ALU.mult),
                                         reads=[pbb] + CONST, writes=[b_qdT])
                                transposes_bf(tok_b[j], b_tok_b[j], TT, 4, None, None, ev)
                            elif c == 1:
                                j = rot(3, 2)
                                rope(ps, 8, TT, cst, b_csb[cb], [pbuf], [(tok_b[j][0:TT, :], b_tok_b[j])])
                                S.op(DVE, C("tensor_tensor",
                                    out=kdec_tok[0:TT, t, :].rearrange("p (h c) -> p h c", c=64), in0=tok_b[j][0:TT, :].rearrange("p (h c) -> p h c", c=64),
                                    in1=tab[0:TT, KDc:KDc + 8].unsqueeze(2).to_broadcast([TT, 8, 64]), op=ALU.mult),
                                    reads=[b_tok_b[j]] + CONST, writes=[b_kdec])
                                if full:
                                    def ev(pb, pbb, t=t):
                                        S.op(ACT, C("activation", out=kT_ret[:, :, t * TT:(t + 1) * TT], in_=pb, func=AF.Copy),
                                             reads=[pbb], writes=[b_kT])
                                    transposes_bf(tok_b[j], b_tok_b[j], TT, 4, None, None, ev)
                            elif c in (2, 3):
                                S.op(ACT, C("activation", out=v_tok[0:TT, t, (c - 2) * 512:(c - 1) * 512], in_=ps, func=AF.Copy),
                                     reads=[pbuf], writes=[b_vtok])
                                if c == 3:
                                    state_step(t, TT, CDc)
                            elif c == 4:
                                rs = rmsnorm_rstd(ps, pbuf, TT)
                                j = rot(3, 2)
                                S.op(ACT, C("activation", out=tok_b[j][0:TT, :], in_=ps, func=AF.Identity, scale=rs),
                                     reads=[pbuf, b_sstat], writes=[b_tok_b[j]])

                                def ev(pb, pbb, t=t):
                                    for kk in range(4):
                                        S.op(ACT, C("activation", out=cqnT[:, kk, t * TT:(t + 1) * TT], in_=pb[:, kk, :], func=AF.Identity,
                                                                                scale=tab[:, T_GCQ + kk:T_GCQ + kk + 1]),
                                             reads=[pbb] + CONST, writes=[b_cqn])
                                transposes_bf(tok_b[j], b_tok_b[j], TT, 4, None, None, ev)
                            elif c == 5:
                                rs = rmsnorm_rstd(ps, pbuf, TT)
                                si = 0
                                S.op(DVE, C("scalar_tensor_tensor", out=ckv_stg[si][0:TT, :], in0=ps, scalar=rs, in1=gckv[0:TT, :], op0=ALU.mult, op1=ALU.mult),
                                     reads=[pbuf, b_sstat] + CONST, writes=[b_ckv_stg[si]])
                                j = rot(3, 2)
                                S.op(ACT, C("activation", out=tok_b[j][0:TT, :], in_=ckv_stg[si][0:TT, :], func=AF.Copy),
                                     reads=[b_ckv_stg[si]], writes=[b_tok_b[j]])
                                if ckv_out is not None:
                                    S.dma(POOL, C("dma_start", out=ckv_out(t), in_=ckv_stg[si][0:TT, :]), b_ckv_stg[si].dsem,
                                          reads=[b_ckv_stg[si]], writes=[b_out])

                                def ev(pb, pbb, t=t):
                                    S.op(ACT, C("activation", out=ckvT[:, :, t * TT:(t + 1) * TT], in_=pb, func=AF.Copy), reads=[pbb], writes=[b_ckvT])
                                transposes_bf(tok_b[j], b_tok_b[j], TT, 4, None, None, ev)
                            else:
                                si = 0
                                j = rot(3, 2)
                                rope(ps, 1, TT, cst, b_csb[cb], [pbuf],
                                     [(kr_stg[si][0:TT, :], b_kr_stg[si]), (tok_b[j][0:TT, 0:64], b_tok_b[j]), (tok_b[j][0:TT, 64:128], b_tok_b[j])])
                                if kr_out is not None:
                                    S.dma(POOL, C("dma_start", out=kr_out(t), in_=kr_stg[si][0:TT, :]), b_kr_stg[si].dsem,
                                          reads=[b_kr_stg[si]], writes=[b_out])

                                def ev(pb, pbb, t=t):
                                    S.op(ACT, C("activation", out=krT_all[:, key_col0 + t * TT:key_col0 + (t + 1) * TT], in_=pb[:, 0, :], func=AF.Copy),
                                         reads=[pbb], writes=[b_krT])
                                transposes_bf(tok_b[j], b_tok_b[j], TT, 1, None, None, ev)
                        pend.append(post)
                        if len(pend) > 1:
                            pend.pop(0)()
                    while pend:
                        pend.pop(0)()
                S.tag = "front_rg"
                if full and sub >= 0:
                    for sl in range(2):
                        wt, wb = WS.next()
                        for mm in range(4):
                            m = sl * 4 + mm
                            bi = next_bank()
                            ps = bank(*bi)[:, 0:TB]
                            pbuf = b_DB[bi[0]][bi[1]]
                            mm_group(ps, pbuf, [(wt[:, (mm * 16 + k) * 128:(mm * 16 + k + 1) * 128], actb[:, k, 0:TB]) for k in range(16)], b_hT + [wb])
                            S.op(ACT, C("activation", out=rgT[:, m, 0:TB], in_=ps, func=AF.Silu), reads=[pbuf], writes=[b_rg])
            else:
                ci = cache[2]
                for t in range(4):
                    def ev(pb, pbb, t=t):
                        S.op(ACT, C("activation", out=ckvT[:, :, t * 128:(t + 1) * 128], in_=pb, func=AF.Copy), reads=[pbb], writes=[b_ckvT])
                    transposes_bf(cache_stage[ci][:, t, 0:512], b_cstage[ci], 128, 4, None, None, ev)
                    j = rot(3, 2)
                    S.op(DVE, C("tensor_copy", out=tok_b[j][:, 0:64], in_=cache_stage[ci][:, t, 512:576]), reads=[b_cstage[ci]], writes=[b_tok_b[j]])
                    S.op(DVE, C("tensor_copy", out=tok_b[j][:, 64:128], in_=cache_stage[ci][:, t, 512:576]), reads=[b_cstage[ci]], writes=[b_tok_b[j]])

                    def ev2(pb, pbb, t=t):
                        S.op(ACT, C("activation", out=krT_all[:, key_col0 + t * 128:key_col0 + (t + 1) * 128], in_=pb[:, 0, :], func=AF.Copy),
                             reads=[pbb], writes=[b_krT])
                    transposes_bf(tok_b[j], b_tok_b[j], 128, 1, None, None, ev2)
            if cache is not None:
                kvgen()
            rotN[0] = 4
            mmrot[0] = 0

        def kvgen_impl(TT, NT, kblk):
            TB = TT * NT
            wt, wb = WS.next()
            for h in range(H):
                bi = next_bank()
                ps = bank(*bi)[:, 0:TB]
                pbuf = b_DB[bi[0]][bi[1]]
                mm_group(ps, pbuf, [(wt[:, (h * 4 + kk) * 128:(h * 4 + kk + 1) * 128], ckvT[:, kk, 0:TB]) for kk in range(4)], [b_ckvT, wb])
                eng = ACT if h % 2 == 0 else DVE
                if eng == ACT:
                    S.op(ACT, C("activation", out=knT_blk[:, h, 0:TB], in_=ps, func=AF.Copy), reads=[pbuf], writes=[b_knb])
                else:
                    S.op(DVE, C("tensor_copy", out=knT_blk[:, h, 0:TB], in_=ps), reads=[pbuf], writes=[b_knb])
            wt, wb = WS.next()
            w3 = wt[:, 0:4096].rearrange("p (k n) -> p k n", n=1024)
            for t in range(NT):
                for c in range(2):
                    bi = next_bank()
                    ps = bank(*bi)[0:TT, :]
                    pbuf = b_DB[bi[0]][bi[1]]
                    mm_group(ps, pbuf, [(ckvT[:, kk, t * TT:(t + 1) * TT], w3[:, kk, c * 512:(c + 1) * 512]) for kk in range(4)], [b_ckvT, wb])
                    dst = v_blk[0:TT, c * 4:c * 4 + 4, t, :]
                    src = ps.rearrange("p (h e) -> p h e", e=128)
                    if c == 0:
                        S.op(ACT, C("activation", out=dst, in_=src, func=AF.Copy), reads=[pbuf], writes=[b_vb])
                    else:
                        S.op(DVE, C("tensor_copy", out=dst, in_=src), reads=[pbuf], writes=[b_vb])
            S.dma(POOL, C("dma_start", out=kscr[kblk][:, :, 0:TB], in_=knT_blk[:, :, 0:TB]), b_knb.dsem, reads=[b_knb], writes=[b_kscr[kblk]])
            S.dma(POOL, C("dma_start", out=vscr[kblk][0:TT, :, 0:NT, :], in_=v_blk[0:TT, :, 0:NT, :]), b_vb.dsem, reads=[b_vb], writes=[b_vscr[kblk]])

        def state_step(t, TT, CDc):
            kvb = (2, 0)
            for m in range(4):
                hb = b_DB[2][m // 2]
                out = DB[2][:, m * 256:(m + 1) * 256]
                S.op(PE, C("matmul", out, lhsT=kdec_tok[0:TT, t, m * 128:(m + 1) * 128], rhs=v_tok[0:TT, t, m * 256:(m + 1) * 256], start=True, stop=True),
                     reads=[b_kdec, b_vtok], writes=[hb])
            kv3 = DB[2][:, :].rearrange("p (m c) -> p m c", c=256)
            S.op(DVE, C("tensor_tensor", out=S_f[:], in0=S_f[:], in1=tab[:, CDc:CDc + 4].unsqueeze(2).to_broadcast([128, 4, 128]), op=ALU.mult),
                 reads=[b_Sf] + CONST, writes=[b_Sf])
            S.op(DVE, C("tensor_tensor", out=S_f[0:64], in0=S_f[0:64], in1=kv3[0:64, :, 0:128], op=ALU.add),
                 reads=[b_Sf, b_DB[2][0], b_DB[2][1]], writes=[b_Sf])
            S.op(DVE, C("tensor_tensor", out=S_f[64:128], in0=S_f[64:128], in1=kv3[64:128, :, 128:256], op=ALU.add),
                 reads=[b_Sf, b_DB[2][0], b_DB[2][1]], writes=[b_Sf])
            S.op(POOL, C("tensor_copy", out=S_bf[:, t + 1], in_=S_f[:]), reads=[b_Sf], writes=[b_Sbf[t + 1]])

        def carry_state(NT):
            S.op(POOL, C("tensor_copy", out=S_bf[:, 0], in_=S_f[:]), reads=[b_Sf], writes=[b_Sbf[0]])

        def retention(TT, NT, C_tabs):
            S.tag = "retention"
            TB = TT * NT
            DT, QD, KDc, CDc = C_tabs
            ob = [(3, 0), (3, 1)]

            def emit_st(m):
                if mmrot[0] % 2:
                    next_bank()
                sb0 = next_bank()
                next_bank()
                dbi = sb0[0]
                pr = m % 2
                for t in range(NT):
                    for hh in range(2):
                        S.op(PE, C("matmul", DB[dbi][0:TT, hh * 512 + t * TT:hh * 512 + (t + 1) * TT],
                                   lhsT=kT_ret[hh * 64:(hh + 1) * 64, m, t * TT:(t + 1) * TT],
                                   rhs=qT_ret[hh * 64:(hh + 1) * 64, m, t * TT:(t + 1) * TT], start=True, stop=True),
                             reads=[b_kT, b_qT], writes=[b_DB[dbi][hh]])
                sps = DB[dbi][0:TT, :].rearrange("p (h c) -> p h c", c=512)[:, :, 0:TB].rearrange("p h (t i) -> p h t i", i=TT)
                dtb = DT[0:TT, 2 * m:2 * m + 2, 0:TT].unsqueeze(2).to_broadcast([TT, 2, NT, TT])
                outp = pT[pr][0:TT, :, 0:TB].rearrange("p h (t i) -> p h t i", i=TT)
                S.op(DVE, C("tensor_tensor", out=outp, in0=sps, in1=dtb, op=ALU.mult),
                     reads=[b_DB[dbi][0], b_DB[dbi][1]] + CONST, writes=[b_pT[pr]])

            def emit_o(m):
                pr = m % 2
                for t in range(NT):
                    for hh in range(2):
                        h = 2 * m + hh
                        o_ap = bank(*ob[hh])[:, t * TT:(t + 1) * TT]
                        obuf = b_DB[ob[hh][0]][ob[hh][1]]
                        S.op(PE, C("matmul", o_ap, lhsT=v_tok[0:TT, t, h * 128:(h + 1) * 128], rhs=pT[pr][0:TT, hh, t * TT:(t + 1) * TT], start=True, stop=False),
                             reads=[b_vtok, b_pT[pr]], writes=[obuf], signal=(TT < 128))
                        S.op(PE, C("matmul", o_ap, lhsT=S_bf[hh * 64:(hh + 1) * 64, t, m, :], rhs=qdT_ret[hh * 64:(hh + 1) * 64, m, t * TT:(t + 1) * TT], start=False, stop=True),
                             reads=[b_Sbf[t], b_qdT], writes=[obuf], self_sync=(TT < 128))
                for hh in range(2):
                    h = 2 * m + hh
                    groupnorm_gate(bank(*ob[hh])[:, 0:TB], b_DB[ob[hh][0]][ob[hh][1]], h, TB)
                S.tag = "retention"

            emit_st(0)
            for m in range(4):
                if m + 1 < 4:
                    emit_st(m + 1)
                emit_o(m)

        def stats_finish(mean_ps, mean_buf, ex2_ps, ex2_buf, TB, inv):
            S.op(ACT, C("activation", out=mean_sb[:, 0:TB], in_=mean_ps, func=AF.Identity, scale=inv), reads=[mean_buf], writes=[b_mean])
            j = rot(1, NTMP)
            S.op(DVE, C("tensor_tensor", out=tmpf[j][:, 0:TB], in0=mean_sb[:, 0:TB], in1=mean_sb[:, 0:TB], op=ALU.mult), reads=[b_mean], writes=[b_tmpf[j]])
            S.op(DVE, C("scalar_tensor_tensor", out=tmpf[j][:, 0:TB], in0=ex2_ps, scalar=inv, in1=tmpf[j][:, 0:TB], op0=ALU.mult, op1=ALU.subtract), reads=[ex2_buf, b_tmpf[j]], writes=[b_tmpf[j]])
            S.op(ACT, C("activation", out=rstd_sb[:, 0:TB], in_=tmpf[j][:, 0:TB], func=AF.Sqrt, bias=tab[:, T_EPS:T_EPS + 1]), reads=[b_tmpf[j]] + CONST, writes=[b_rstd])
            S.op(DVE, C("reciprocal", out=rstd_sb[:, 0:TB], in_=rstd_sb[:, 0:TB]), reads=[b_rstd], writes=[b_rstd])

        def groupnorm_gate(o_ps, obuf, h, TB):
            i1, i2 = rot(7, NTB), rot(7, NTB)
            j = rot(1, NTMP)
            S.op(ACT, C("activation", out=tmpf[j][:, 0:TB], in_=o_ps, func=AF.Copy), reads=[obuf], writes=[b_tmpf[j]])
            S.op(ACT, C("activation", out=tmpb[i1][:, 0:TB], in_=tmpf[j][:, 0:TB], func=AF.Copy), reads=[b_tmpf[j]], writes=[b_tmpb[i1]])
            S.op(ACT, C("activation", out=tmpb[i2][:, 0:TB], in_=tmpf[j][:, 0:TB], func=AF.Square), reads=[b_tmpf[j]], writes=[b_tmpb[i2]])
            mb, eb = (2, 0), (2, 1)
            S.op(PE, C("matmul", bank(*mb)[:, 0:TB], lhsT=ones_b[:], rhs=tmpb[i1][:, 0:TB], start=True, stop=True), reads=[b_tmpb[i1]] + CONST, writes=[b_DB[2][0]])
            S.op(PE, C("matmul", bank(*eb)[:, 0:TB], lhsT=ones_b[:], rhs=tmpb[i2][:, 0:TB], start=True, stop=True), reads=[b_tmpb[i2]] + CONST, writes=[b_DB[2][1]])
            stats_finish(bank(*mb)[:, 0:TB], b_DB[2][0], bank(*eb)[:, 0:TB], b_DB[2][1], TB, 1.0 / 128.0)
            S.op(DVE, C("tensor_tensor", out=tmpf[j][:, 0:TB], in0=tmpf[j][:, 0:TB], in1=mean_sb[:, 0:TB], op=ALU.subtract), reads=[b_tmpf[j], b_mean], writes=[b_tmpf[j]])
            S.op(DVE, C("tensor_tensor", out=tmpf[j][:, 0:TB], in0=tmpf[j][:, 0:TB], in1=rstd_sb[:, 0:TB], op=ALU.mult), reads=[b_tmpf[j], b_rstd], writes=[b_tmpf[j]])
            S.op(ACT, C("activation", out=tmpf[j][:, 0:TB], in_=tmpf[j][:, 0:TB], func=AF.Identity, scale=tab[:, T_GRET + h:T_GRET + h + 1], bias=tab[:, T_BRET + h:T_BRET + h + 1]),
                 reads=[b_tmpf[j]] + CONST, writes=[b_tmpf[j]])
            S.op(DVE, C("tensor_tensor", out=actb[:, h, 0:TB], in0=tmpf[j][:, 0:TB], in1=rgT[:, h, 0:TB], op=ALU.mult), reads=[b_tmpf[j], b_rg], writes=[b_hT[h]])

        def qgen(TT, NT):
            S.tag = "qgen"
            TB = TT * NT
            wt, wb = WS.next()
            for h in range(H):
                bi = next_bank()
                ps = bank(*bi)[:, 0:TB]
                pbuf = b_DB[bi[0]][bi[1]]
                mm_group(ps, pbuf, [(wt[:, (h * 4 + kk) * 128:(h * 4 + kk + 1) * 128], cqnT[:, kk, 0:TB]) for kk in range(4)], [b_cqn, wb])
                if h % 2 == 0:
                    S.op(ACT, C("activation", out=qnT[:, h, 0:TB], in_=ps, func=AF.Copy), reads=[pbuf], writes=[b_qn])
                else:
                    S.op(DVE, C("tensor_copy", out=qnT[:, h, 0:TB], in_=ps), reads=[pbuf], writes=[b_qn])
            wt, wb = WS.next()
            w3 = wt[:, 0:2048].rearrange("p (k n) -> p k n", n=512)
            qpend = []
            for t in range(NT):
                cb = cur_cs[0]
                cst = cs_blk[cb][:, t, :]
                bi = next_bank()
                ps = bank(*bi)[0:TT, :]
                pbuf = b_DB[bi[0]][bi[1]]
                mm_group(ps, pbuf, [(cqnT[:, kk, t * TT:(t + 1) * TT], w3[:, kk, :]) for kk in range(4)], [b_cqn, wb])
                outs = []
                for jj in range(2):
                    v4 = tok_b[jj][0:TT, :].rearrange("p (h c d) -> p h c d", c=2, d=64)
                    for cpy in range(2):
                        outs.append((lambda half, v4=v4, cpy=cpy: v4[:, :, cpy, half * 32:(half + 1) * 32], jj * 4, jj * 4 + 4, b_tok_b[jj]))
                def post(t=t, ps=ps, pbuf=pbuf, cst=cst, outs=outs):
                    rope(ps, 8, TT, cst, b_csb[cb], [pbuf], outs)
                    for jj in range(2):
                        def ev(pb, pbb, t=t, jj=jj):
                            S.op(ACT, C("activation", out=qrT[:, jj * 4:jj * 4 + 4, t * TT:(t + 1) * TT], in_=pb, func=AF.Copy), reads=[pbb], writes=[b_qr])
                        transposes_bf(tok_b[jj], b_tok_b[jj], TT, 4, None, None, ev)
                qpend.append(post)
                if len(qpend) > 1:
                    qpend.pop(0)()
            while qpend:
                qpend.pop(0)()

        def kv_loads(keyblocks):
            loads = []
            for h in range(H):
                for (kb, kc0, TTk, NTk, bias_col, diag) in keyblocks:
                    def load(tile, buf, swsem, kb=kb, h=h, TTk=TTk, NTk=NTk):
                        S.dma(SP, C("dma_start", out=tile[:, 0:TTk * NTk], in_=kscr[kb][:, h, 0:TTk * NTk]), buf.dsem, reads=[b_kscr[kb]], writes=[buf])
                        S.dma(SP, C("dma_start", out=tile[0:TTk, 512:512 + NTk * 128].rearrange("p (t e) -> p t e", e=128), in_=vscr[kb][0:TTk, h, 0:NTk, :]),
                              buf.dsem, reads=[b_vscr[kb], b_kscr[kb]], writes=[buf], skip_own=True)
                    loads.append(load)
            return loads

        def attention(TB, keyblocks):
            S.tag = "attn"
            groups = []
            for h in range(H):
                tiles = []
                for (kb, kc0, TTk, NTk, bias_col, diag) in keyblocks:
                    for kt in range(NTk):
                        tiles.append((kb, kc0, TTk, NTk, bias_col, diag, kt))
                i = 0
                hg = []
                while i < len(tiles):
                    grp = tiles[i:i + 2]
                    if len(grp) == 2 and grp[1][0] != grp[0][0]:
                        grp = grp[:1]
                    hg.append(dict(h=h, tiles=grp, first=(i == 0), last=False))
                    i += len(grp)
                hg[-1]["last"] = True
                groups.extend(hg)
            cur = [None]

            def emit_qk(g):
                h = g["h"]
                grp = g["tiles"]
                if grp[0][6] == 0:
                    cur[0] = KS.next()
                kvt, kvb = cur[0]
                g["kv"] = (kvt, kvb)
                di = rot(5, 2)
                g["di"] = di
                sdb = DB[di]
                TTk = grp[0][2]
                ng = len(grp)
                for gi, (kb, kc0, _, NTk, bias_col, diag, kt) in enumerate(grp):
                    sp = sdb[0:TTk, gi * 512:gi * 512 + TB]
                    S.op(PE, C("matmul", sp, lhsT=kvt[:, kt * TTk:(kt + 1) * TTk], rhs=qnT[:, h, 0:TB], start=True, stop=False),
                         reads=[kvb, b_qn], writes=[b_DB[di][gi]], signal=False)
                for gi, (kb, kc0, _, NTk, bias_col, diag, kt) in enumerate(grp):
                    sp = sdb[0:TTk, gi * 512:gi * 512 + TB]
                    r0 = gi * 64
                    S.op(PE, C("matmul", sp, lhsT=krT_all[r0:r0 + 64, kc0 + kt * TTk:kc0 + (kt + 1) * TTk], rhs=qrT[r0:r0 + 64, h, 0:TB], start=False, stop=True),
                         reads=[b_krT, b_qr], writes=[b_DB[di][gi]])
                pi = rot(4, 2)
                g["pi"] = pi
                bias_col = grp[0][4]
                src = sdb[0:TTk, :].rearrange("p (g c) -> p g c", c=512)[:, 0:ng, 0:TB]
                S.op(ACT, C("activation", out=pT[pi][0:TTk, 0:ng, 0:TB], in_=src, func=AF.Exp, scale=MLA_SCALE, bias=tab[0:TTk, bias_col:bias_col + 1]),
                     reads=[b_DB[di][g_] for g_ in range(ng)] + CONST, writes=[b_pT[pi]])
                for gi, (kb, kc0, _, NTk, bias_col, diag, kt) in enumerate(grp):
                    if diag:
                        if kt > 0:
                            S.op(POOL, C("memset", pT[pi][:, gi, 0:128 * kt], 0.0), writes=[b_pT[pi]])
                        S.op(POOL, C("memset", pT[pi][64:128, gi, 128 * kt:128 * kt + 64], 0.0), writes=[b_pT[pi]])

            def emit_pv(g):
                h = g["h"]
                grp = g["tiles"]
                kvt, kvb = g["kv"]
                pi = g["pi"]
                TTk = grp[0][2]
                set_ = 2 + (h % 2)
                o_ps = bank(set_, 0)[:, 0:TB]
                s_ps = bank(set_, 1)[:, 0:TB]
                obuf, sbuf_ = b_DB[set_][0], b_DB[set_][1]
                ng = len(grp)
                for gi, (kb, kc0, _, NTk, bias_col, diag, kt) in enumerate(grp):
                    first = g["first"] and gi == 0
                    last = g["last"] and gi == ng - 1
                    S.op(PE, C("matmul", o_ps, lhsT=kvt[0:TTk, 512 + kt * 128:512 + (kt + 1) * 128], rhs=pT[pi][0:TTk, gi, 0:TB], start=first, stop=last),
                         reads=[kvb, b_pT[pi]], writes=[obuf], signal=False)
                    S.op(PE, C("matmul", s_ps, lhsT=ones_b[0:TTk, :], rhs=pT[pi][0:TTk, gi, 0:TB], start=first, stop=last),
                         reads=[b_pT[pi]] + CONST, writes=[sbuf_])
                if g["last"]:
                    convert_some(5, 2)
                    j = rot(1, NTMP)
                    S.op(DVE, C("reciprocal", out=tmpf[j][:, 0:TB], in_=s_ps), reads=[sbuf_], writes=[b_tmpf[j]])
                    S.op(DVE, C("tensor_tensor", out=actb[:, 8 + h, 0:TB], in0=o_ps, in1=tmpf[j][:, 0:TB], op=ALU.mult), reads=[obuf, b_tmpf[j]], writes=[b_hT[8 + h]])

            G = len(groups)
            emit_qk(groups[0])
            for gi_ in range(G):
                if gi_ + 1 < G:
                    emit_qk(groups[gi_ + 1])
                emit_pv(groups[gi_])

        def ln_block(TB, seq, nslab, mt_per_slab, kt, src_of, gcol0, mode, out_fn=None):
            S.tag = "ln%d" % mode
            mb, eb = (3, 0), (3, 1)
            pend = []

            def stats_mm(m, i1, i2):
                S.op(PE, C("matmul", bank(*mb)[:, 0:TB], lhsT=ones_b[:], rhs=tmpb[i1][:, 0:TB], start=(m == 0), stop=(m == 15), skip_group_check=True),
                     reads=[b_tmpb[i1]] + CONST, writes=[b_DB[3][0]])
                S.op(PE, C("matmul", bank(*eb)[:, 0:TB], lhsT=ones_b[:], rhs=tmpb[i2][:, 0:TB], start=(m == 0), stop=(m == 15), skip_group_check=True),
                     reads=[b_tmpb[i2]] + CONST, writes=[b_DB[3][1]])
            for sl in range(nslab):
                wt, wb = WS.next()
                for mm in range(mt_per_slab):
                    m = sl * mt_per_slab + mm
                    bi = next_bank()
                    ps = bank(*bi)[:, 0:TB]
                    pbuf = b_DB[bi[0]][bi[1]]
                    rbuf = b_hT if mode == 1 else (b_act + [b_actw])
                    mm_group(ps, pbuf, [(wt[:, (mm * kt + k) * 128:(mm * kt + k + 1) * 128], src_of(k)) for k in range(kt)], rbuf + [wb])
                    S.op(DVE, C("scalar_tensor_tensor", out=xaT[:, m, 0:TB], in0=ps, scalar=modT[:, gcol0 + m, seq:seq + 1], in1=xaT[:, m, 0:TB], op0=ALU.mult, op1=ALU.add),
                         reads=[pbuf, b_mod, b_xaT[m]], writes=[b_xaT[m]])
                    i1, i2 = rot(7, NTB), rot(7, NTB)
                    S.op(ACT, C("activation", out=tmpb[i1][:, 0:TB], in_=xaT[:, m, 0:TB], func=AF.Copy), reads=[b_xaT[m]], writes=[b_tmpb[i1]])
                    S.op(ACT, C("activation", out=tmpb[i2][:, 0:TB], in_=xaT[:, m, 0:TB], func=AF.Square), reads=[b_xaT[m]], writes=[b_tmpb[i2]])
                    pend.append((m, i1, i2))
                    if len(pend) > 1:
                        stats_mm(*pend.pop(0))
            while pend:
                stats_mm(*pend.pop(0))
            stats_finish(bank(*mb)[:, 0:TB], b_DB[3][0], bank(*eb)[:, 0:TB], b_DB[3][1], TB, 1.0 / 2048.0)
            for m in range(16):
                j = rot(1, NTMP)
                S.op(POOL, C("tensor_tensor", out=tmpf[j][:, 0:TB], in0=xaT[:, m, 0:TB], in1=mean_sb[:, 0:TB], op=ALU.subtract), reads=[b_xaT[m], b_mean], writes=[b_tmpf[j]])
                S.op(DVE, C("tensor_tensor", out=tmpf[j][:, 0:TB], in0=tmpf[j][:, 0:TB], in1=rstd_sb[:, 0:TB], op=ALU.mult), reads=[b_tmpf[j], b_rstd], writes=[b_tmpf[j]])
                if mode == 1:
                    S.op(ACT, C("activation", out=actb[:, m, 0:TB], in_=tmpf[j][:, 0:TB], func=AF.Identity, scale=G2T[:, seq, m:m + 1], bias=B2T[:, seq, m:m + 1]),
                         reads=[b_tmpf[j], b_seqtab], writes=[b_hT[m]])
                    S.op(ACT, C("activation", out=xaT[:, m, 0:TB], in_=tmpf[j][:, 0:TB], func=AF.Identity, scale=AG1[:, m:m + 1], bias=AB1[:, m:m + 1]),
                         reads=[b_tmpf[j], b_seqtab], writes=[b_xaT[m]])
                else:
                    S.op(ACT, C("activation", out=xaT[:, m, 0:TB], in_=tmpf[j][:, 0:TB], func=AF.Identity, scale=tab[:, T_LN2G + m:T_LN2G + m + 1], bias=tab[:, T_LN2B + m:T_LN2B + m + 1]),
                         reads=[b_tmpf[j]] + CONST, writes=[b_xaT[m]])

        def ffn_up(TB):
            S.tag = "ffn_up"
            for sl in range(22):
                wt, wb = WS.next()
                for mm in range(2):
                    m = sl * 2 + mm
                    bg = next_bank()
                    gps = bank(*bg)[:, 0:TB]
                    mm_group(gps, b_DB[bg[0]][bg[1]], [(wt[:, (mm * 16 + k) * 128:(mm * 16 + k + 1) * 128], actb[:, k, 0:TB]) for k in range(16)], b_hT + [wb])
                    bu = next_bank()
                    ups = bank(*bu)[:, 0:TB]
                    mm_group(ups, b_DB[bu[0]][bu[1]], [(wt[:, ((2 + mm) * 16 + k) * 128:((2 + mm) * 16 + k + 1) * 128], actb[:, k, 0:TB]) for k in range(16)], b_hT + [wb])
                    j = rot(1, NTMP)
                    S.op(ACT, C("activation", out=tmpf[j][:, 0:TB], in_=gps, func=AF.Silu), reads=[b_DB[bg[0]][bg[1]]], writes=[b_tmpf[j]])
                    S.op(DVE, C("tensor_tensor", out=actT[:, m, 0:TB], in0=ups, in1=tmpf[j][:, 0:TB], op=ALU.mult), reads=[b_DB[bu[0]][bu[1]], b_tmpf[j]], writes=[b_act[m], b_actw])

        if len(stg) == 2:
            stg.append(ckv_stg[0][:, :])
            b_stg.append(b_ckv_stg[0])

        def output_y(TT, NT, y_dst):
            S.tag = "out_y"
            for t in range(NT):
                for g in range(4):
                    si = rot(8, 3)
                    bi = next_bank()
                    pb = bank(*bi)[0:TT, :].rearrange("p (k c) -> p k c", c=128)
                    for kk in range(4):
                        k = g * 4 + kk
                        S.op(PE, C("transpose", pb[:, kk, :], xaT[:, k, t * TT:(t + 1) * TT], ident[:]),
                             reads=[b_xaT[k]] + CONST, writes=[b_DB[bi[0]][bi[1]]], signal=(kk == 3))
                    if g % 2 == 0:
                        S.op(ACT, C("activation", out=stg[si][0:TT, :], in_=bank(*bi)[0:TT, :], func=AF.Copy), reads=[b_DB[bi[0]][bi[1]]], writes=[b_stg[si]])
                    else:
                        S.op(DVE, C("tensor_copy", out=stg[si][0:TT, :], in_=bank(*bi)[0:TT, :]), reads=[b_DB[bi[0]][bi[1]]], writes=[b_stg[si]])
                    S.dma(POOL, C("dma_start", out=y_dst(t)[:, g * 512:(g + 1) * 512], in_=stg[si][0:TT, :]), b_stg[si].dsem, reads=[b_stg[si]], writes=[b_out])

        def main_block(TT, NT, seq, x_src, cs_rows, key_col0, kblk, C_tabs, keyblocks, ckv_out, kr_out, y_dst, blk_id=None, next_x=None):
            TB = TT * NT
            front(True, TT, NT, seq, x_src, cs_rows, key_col0, kblk, C_tabs, ckv_out=ckv_out, kr_out=kr_out, blk_id=blk_id, next_x=next_x)
            if sub < 1:
                return
            retention(TT, NT, C_tabs)
            if sub < 2:
                return
            qgen(TT, NT)
            if sub < 3:
                return
            attention(TB, keyblocks)
            if sub < 4:
                return
            ln_block(TB, seq, 4, 4, 16, lambda k: actb[:, k, 0:TB], 32, 1)
            if sub < 5:
                return
            ffn_up(TB)
            if sub < 6:
                return
            ln_block(TB, seq, 16, 1, 44, lambda k: actT[:, k, 0:TB], 80, 2)
            if sub < 7:
                return
            output_y(TT, NT, y_dst)

        C128 = (dt128, qd128, T_KD128, T_CD128)
        C64 = (dt128, qd128, T_KD64, T_CD64)

        for b in range(NBLK):
            WS.extend([L_(k, i) for (k, i) in PREFIX_SLABS])
        for b in range(NBLK):
            WS.extend([L_(k, i) for (k, i) in MAIN_SLABS])
        for s in range(2):
            for b in range(8):
                WS.extend([L_("mla", 0), L_("mla", 1)])
            WS.extend([L_(k, i) for (k, i) in MAIN_SLABS])
        for i in range(NBLK):
            kbs = [(p, p * TBP, 128, 4, T_VF, False) for p in range(8)] + [(8 + q, HALF + q * TBP, 128, 4, T_ZERO, q == i) for q in range(i + 1)]
            KS.extend(kv_loads(kbs))
        for s in range(2):
            kbs = [(p, p * TBP, 128, 4, T_ZERO, False) for p in range(8)] + [(16, HALF, 64, 1, T_ZERO, False)]
            KS.extend(kv_loads(kbs))

        S.op(DVE, C("memset", S_f[:], 0.0), writes=[b_Sf])
        npre = 0 if stage < 1 else (1 if stage == 1 else NBLK)
        def xpre_src(b):
            return lambda t: xpre[b * TBP + t * 128:b * TBP + (t + 1) * 128, :]

        def xown_src(i):
            return lambda t: xown[i * TBP + t * 128:i * TBP + (t + 1) * 128, :]

        S.pfx = "P:"
        for b in range(npre):
            convert_some(2)
            carry_state(4)
            nx = (("pre", b + 1), xpre_src(b + 1), 128, 4) if b + 1 < NBLK else (("own", 0), xown_src(0), 128, 4)
            front(False, 128, 4, 0, xpre_src(b), cs_pre[b * TBP:(b + 1) * TBP, :], b * TBP, b, C128, blk_id=("pre", b), next_x=nx)
            if stage >= 3:
                ada_part(32 + 8 * b, 40 + 8 * b)
        convert_some(100)
        if stage < 4:
            convert_some(100, 2)
        S.op(DVE, C("tensor_scalar", out=S_f[:], in0=S_f[:], scalar1=tab[:, T_VF + 1:T_VF + 2], scalar2=None, op0=ALU.mult), reads=[b_Sf] + CONST, writes=[b_Sf])

        for s in range(3 if stage >= 3 else 0):
            j = rot(1, NTMP)
            S.op(DVE, C("tensor_scalar", out=tmpf[j][:, 0:16], in0=modT[:, 64:80, s], scalar1=1.0, scalar2=None, op0=ALU.add), reads=[b_mod], writes=[b_tmpf[j]])
            S.op(DVE, C("tensor_tensor", out=G2T[:, s, :], in0=tmpf[j][:, 0:16], in1=tab[:, T_LN1G:T_LN1G + 16], op=ALU.mult), reads=[b_tmpf[j]] + CONST, writes=[b_seqtab])
            S.op(DVE, C("tensor_tensor", out=tmpf[j][:, 0:16], in0=tmpf[j][:, 0:16], in1=tab[:, T_LN1B:T_LN1B + 16], op=ALU.mult), reads=[b_tmpf[j]] + CONST, writes=[b_tmpf[j]])
            S.op(DVE, C("tensor_tensor", out=B2T[:, s, :], in0=tmpf[j][:, 0:16], in1=modT[:, 48:64, s], op=ALU.add), reads=[b_tmpf[j], b_mod], writes=[b_seqtab])
        S.op(DVE, C("tensor_scalar", out=AG1[:], in0=tab[:, T_LN1G:T_LN1G + 16], scalar1=ALPHA, scalar2=None, op0=ALU.mult), reads=CONST, writes=[b_seqtab])
        S.op(DVE, C("tensor_scalar", out=AB1[:], in0=tab[:, T_LN1B:T_LN1B + 16], scalar1=ALPHA, scalar2=None, op0=ALU.mult), reads=CONST, writes=[b_seqtab])

        S.pfx = "M:"
        nmain = 0 if stage < 4 else (1 if stage == 4 else (2 if stage == 5 else NBLK))
        for i in range(nmain):
            carry_state(4)
            kbs = [(p, p * TBP, 128, 4, T_VF, False) for p in range(8)] + [(8 + q, HALF + q * TBP, 128, 4, T_ZERO, q == i) for q in range(i + 1)]
            nx = (("own", i + 1), xown_src(i + 1), 128, 4) if i + 1 < NBLK else None
            main_block(128, 4, 0,
                       xown_src(i),
                       cs_own[i * TBP:(i + 1) * TBP, :],
                       HALF + i * TBP, 8 + i, C128, kbs,
                       lambda t, i=i: ckv_own[i * TBP + t * 128:i * TBP + (t + 1) * 128, :],
                       lambda t, i=i: kr_own[i * TBP + t * 128:i * TBP + (t + 1) * 128, :],
                       lambda t, i=i: y_own[i * TBP + t * 128:i * TBP + (t + 1) * 128, :], blk_id=("own", i), next_x=nx)
        S.dma(POOL, C("dma_start", out=st_own, in_=S_f[:]), b_Sf.dsem, reads=[b_Sf], writes=[b_out])

        S.pfx = "S:"
        for s in range(2 if stage >= 7 else 0):
            cache_load(0, ckv_c[s, 0:TBP, :], kr_c[s, 0:TBP, :])
            for p in range(8):
                if p + 1 < 8:
                    cache_load((p + 1) % 2, ckv_c[s, (p + 1) * TBP:(p + 2) * TBP, :], kr_c[s, (p + 1) * TBP:(p + 2) * TBP, :])
                front(False, 128, 4, 0, None, None, p * TBP, p, C128, cache=(None, None, p % 2))
            S.dma(POOL, C("dma_start", out=S_f[:], in_=st_c[s]), b_Sf.dsem, writes=[b_Sf])
            carry_state(1)
            kbs = [(p, p * TBP, 128, 4, T_ZERO, False) for p in range(8)] + [(16, HALF, 64, 1, T_ZERO, False)]
            main_block(64, 1, 1 + s,
                       lambda t, s=s: xsmp[s],
                       cs_smp,
                       HALF, 16, C64, kbs,
                       lambda t, s=s: ckv_so[s],
                       lambda t, s=s: kr_so[s],
                       lambda t, s=s: y_smp[s])
            S.dma(POOL, C("dma_start", out=st_so[s], in_=S_f[:]), b_Sf.dsem, reads=[b_Sf], writes=[b_out])

        S.final_wait(POOL, [b_out])
        S.replay()
        import os
        if os.environ.get("MK_TAGS"):
            import pickle
            pickle.dump(S.pe_tags, open(os.environ["MK_TAGS"], "wb"))
        print("ops", S.n_ops, "waits", S.n_wait, "sems", S.nsem, {k: len(v) for k, v in S.prog.items()})
    return nc


_CACHE = {}


def _prep_shared(inp):
    f = np.float32
    w_in = inp["w_in"][0]
    w_ada = inp["w_ada"][0]
    sh = {}
    wa = w_ada.reshape(16, 128, 96, 128).transpose(2, 1, 0, 3)
    sh["wada"] = np.ascontiguousarray(wa.reshape(24, 4, 128, 16 * 128).transpose(0, 2, 1, 3)).reshape(24, 128, 8192)
    offs = [0, 512, 1024, 1536, 2048, 3072, 3584, 4096, 4160]
    chunks = [slice(0, 512), slice(512, 1024), slice(1024, 1536), slice(1536, 2048), slice(3072, 3584), slice(3584, 4096), slice(4096, 4160)]
    sh["wtm"] = np.stack([_pad(_tm_chunk(w_in, c), 8192) for c in chunks])

    def lhs_tiles(w, col0, ntile):
        kt = w.shape[0] // 128
        sub = w[:, col0:col0 + ntile * 128].reshape(kt, 128, ntile, 128).transpose(2, 1, 0, 3)
        return np.ascontiguousarray(sub)

    rg = lhs_tiles(w_in, 2048, 8)
    sh["wrg"] = np.ascontiguousarray(rg.reshape(2, 4, 128, 2048).transpose(0, 2, 1, 3)).reshape(2, 128, 8192)
    w_uq = inp["w_uq"][0]
    w_uk = inp["w_uk"][0].reshape(512, 1024)
    w_uv = inp["w_uv"][0].reshape(512, 1024)
    uk = lhs_tiles(w_uk, 0, 8)
    uqn = lhs_tiles(np.ascontiguousarray(w_uq[:, :, 0:128]).reshape(512, 1024), 0, 8)
    uqr = np.ascontiguousarray(w_uq[:, :, 128:192]).reshape(512, 512)
    sh["wmla"] = np.stack([
        np.ascontiguousarray(uk.transpose(1, 0, 2, 3)).reshape(128, 4096),
        _tm_chunk(w_uv, slice(0, 1024)),
        np.ascontiguousarray(uqn.transpose(1, 0, 2, 3)).reshape(128, 4096),
        _pad(_tm_chunk(uqr, slice(0, 512)), 4096)])
    wo = lhs_tiles(inp["w_out"][0], 0, 16)
    sh["wout"] = np.ascontiguousarray(wo.reshape(4, 4, 128, 2048).transpose(0, 2, 1, 3)).reshape(4, 128, 8192)
    wg = lhs_tiles(inp["w_gate"][0], 0, 44).reshape(22, 2, 128, 2048)
    wu = lhs_tiles(inp["w_up"][0], 0, 44).reshape(22, 2, 128, 2048)
    gu = np.concatenate([wg, wu], axis=1)
    sh["wgu"] = np.ascontiguousarray(gu.transpose(0, 2, 1, 3)).reshape(22, 128, 8192)
    wd = lhs_tiles(inp["w_down"][0], 0, 16)
    sh["wd"] = wd.reshape(16, 128, 5632)
    dec = _decay_tables()
    sh["ident"] = np.eye(128, dtype=f)
    sh["dt128"], sh["qd128"] = dec[128][0].astype(ml_dtypes.bfloat16), dec[128][1]
    sh["gckv"] = np.ascontiguousarray(np.broadcast_to(inp["g_ckv"][0][None, :], (128, KVL))).astype(f)
    tab = np.zeros((128, NTAB), f)
    tab[:, T_LN1G:T_LN1G + 16] = _fm(inp["ln1_g"][0])
    tab[:, T_LN1B:T_LN1B + 16] = _fm(inp["ln1_b"][0])
    tab[:, T_LN2G:T_LN2G + 16] = _fm(inp["ln2_g"][0])
    tab[:, T_LN2B:T_LN2B + 16] = _fm(inp["ln2_b"][0])
    tab[:, T_GRET:T_GRET + 8] = _fm(inp["g_ret"][0])
    tab[:, T_BRET:T_BRET + 8] = _fm(inp["b_ret"][0])
    tab[:, T_GCQ:T_GCQ + 4] = _fm(inp["g_cq"][0])
    tab[:, T_BADA:T_BADA + 96] = _fm(inp["b_ada"][0])
    tab[:, T_KD128:T_KD128 + 8] = dec[128][2]
    tab[:, T_KD64:T_KD64 + 8] = dec[64][2]
    tab[:, T_CD128:T_CD128 + 4] = dec[128][3]
    tab[:, T_CD64:T_CD64 + 4] = dec[64][3]
    tab[:, T_EPS] = EPS
    tab[:, T_ZERO] = 0.0
    sh["tab"] = tab
    sh["cs_all"] = _rope_table(np.arange(SEQ))
    sh["cs_smp"] = _rope_table(PAST + np.arange(SSEQ))
    return sh


def make_in_maps(inp):
    f = np.float32
    sh = _prep_shared(inp)
    in_maps = []
    pairlay = lambda s: np.ascontiguousarray(s.reshape(4, 2, 64, 128).transpose(1, 2, 0, 3)).reshape(128, 4, 128)
    for c in range(8):
        b, half = c // 2, c % 2
        tab = sh["tab"].copy()
        tab[:, T_VF] = 0.0 if half == 1 else NEG
        tab[:, T_VF + 1] = 1.0 if half == 1 else 0.0
        cs = np.stack([inp["c_prompt"][b], inp["c_sample"][2 * c], inp["c_sample"][2 * c + 1]], axis=1)
        m = {
            "xpre": inp["x_prompt"][b, 0:HALF], "xown": inp["x_prompt"][b, half * HALF:(half + 1) * HALF],
            "xsmp": inp["x_sample"][2 * c:2 * c + 2],
            "cT": np.ascontiguousarray(cs.reshape(16, 128, 3).transpose(1, 0, 2)),
            "tab": tab,
            "cs_pre": sh["cs_all"][0:HALF], "cs_own": sh["cs_all"][half * HALF:(half + 1) * HALF], "cs_smp": sh["cs_smp"],
            "ckv_c": inp["cache_mla_ckv"][0, 2 * c:2 * c + 2], "kr_c": inp["cache_mla_krope"][0, 2 * c:2 * c + 2],
            "st_c": np.stack([pairlay(inp["state_ret"][0, 2 * c + s]) for s in range(2)]),
        }
        for k in ("ident", "dt128", "qd128", "gckv", "wada", "wtm", "wrg", "wmla", "wout", "wgu", "wd"):
            m[k] = sh[k]
        in_maps.append({k: (np.ascontiguousarray(v) if k == "dt128" else np.ascontiguousarray(v, dtype=f)) for k, v in m.items()})
    return in_maps


def assemble(R):
    f = np.float32
    unpair = lambda s: np.ascontiguousarray(s.reshape(2, 64, 4, 128).transpose(2, 0, 1, 3)).reshape(8, 64, 128)
    yp = np.zeros((NB, SEQ, D), f)
    ckvp = np.zeros((1, NB, SEQ, KVL), f)
    krp = np.zeros((1, NB, SEQ, ROPE), f)
    rsp = np.zeros((1, NB, H, RDK, RDV), f)
    ys = np.zeros((NSB, SSEQ, D), f)
    ckvs = np.zeros((1, NSB, SSEQ, KVL), f)
    krs = np.zeros((1, NSB, SSEQ, ROPE), f)
    rss = np.zeros((1, NSB, H, RDK, RDV), f)
    for c in range(8):
        b, half = c // 2, c % 2
        sl = slice(half * HALF, (half + 1) * HALF)
        yp[b, sl] = R[c]["y_own"]
        ckvp[0, b, sl] = R[c]["ckv_own"]
        krp[0, b, sl] = R[c]["kr_own"]
        if half == 1:
            rsp[0, b] = unpair(R[c]["st_own"])
        ys[2 * c:2 * c + 2] = R[c]["y_smp"]
        ckvs[0, 2 * c:2 * c + 2] = R[c]["ckv_so"]
        krs[0, 2 * c:2 * c + 2] = R[c]["kr_so"]
        for s in range(2):
            rss[0, 2 * c + s] = unpair(R[c]["st_so"][s])
    return (yp, ys, ckvp, krp, rsp, ckvs, krs, rss)


def kernel(**inp):
    inp = {k: np.asarray(v) for k, v in inp.items()}
    in_maps = make_in_maps(inp)
    if "nc" not in _CACHE:
        _CACHE["nc"] = build_program()
    nc = _CACHE["nc"]
    res = run_bass_kernel_spmd(nc, in_maps, core_ids=list(range(8)))
    return assemble(res.results)
```
